# Optimizing a Trainium2 kernel written in Bass

```python
import math
import jax, jax.numpy as jnp
from jax import lax
import numpy as np

D_MODEL = 1024
BATCH = 8
SEQ = 4096
DEPTH = 2

EPS = 1e-6
SSD_HEAD_DIM = 64
SSD_INNER = D_MODEL
SSD_HEADS = SSD_INNER // SSD_HEAD_DIM
SSD_GROUPS = 2
SSD_STATE = 128
SSD_CONV = 4
SSD_CHUNK = 128
SSD_XBC = SSD_INNER + 2 * SSD_GROUPS * SSD_STATE
MLA_HEADS = D_MODEL // 128
MLA_NOPE = 128
MLA_ROPE = 64
MLA_V = 128
MLA_Q_LORA = D_MODEL // 2
MLA_KV_LORA = D_MODEL // 4
MLA_QBLOCK = 128
ROPE_THETA = 10000.0
GDN_HEAD_K = 128
GDN_HEAD_V = 128
GDN_V_HEADS = D_MODEL // GDN_HEAD_V
GDN_QK_HEADS = GDN_V_HEADS // 2
GDN_CONV = 4
GDN_CHUNK = 64
GDN_QK = GDN_QK_HEADS * GDN_HEAD_K
GDN_VW = GDN_V_HEADS * GDN_HEAD_V
GDN_QKV = 2 * GDN_QK + GDN_VW
N_BRANCH = 3
D_FF = 4 * D_MODEL
IN_SIZES = (SSD_INNER, SSD_XBC, SSD_HEADS, MLA_Q_LORA, MLA_KV_LORA, MLA_ROPE,
            GDN_QKV, GDN_VW, GDN_V_HEADS, GDN_V_HEADS, N_BRANCH * D_MODEL)
N_IN = (SSD_INNER + SSD_XBC + SSD_HEADS + MLA_Q_LORA + MLA_KV_LORA + MLA_ROPE
        + GDN_QKV + GDN_VW + 2 * GDN_V_HEADS + N_BRANCH * D_MODEL)

kernel_name = 'hybrid_ssd_mla_gdn_parallel_block'


def rmsnorm(x, g):
    xf = x.astype(jnp.float32)
    y = xf * lax.rsqrt(jnp.mean(xf * xf, axis=-1, keepdims=True) + EPS)
    return (y * g.astype(jnp.float32)).astype(x.dtype)


def l2norm(x):
    xf = x.astype(jnp.float32)
    return xf * lax.rsqrt(jnp.sum(xf * xf, axis=-1, keepdims=True) + EPS)


def causal_dwconv(x, w):
    K, C = w.shape
    return lax.conv_general_dilated(
        x, w[:, None, :].astype(x.dtype), window_strides=(1,), padding=[(K - 1, 0)],
        dimension_numbers=('NWC', 'WIO', 'NWC'), feature_group_count=C)


def rope_tables(positions):
    inv = ROPE_THETA ** (-jnp.arange(0, MLA_ROPE, 2, dtype=jnp.float32) / MLA_ROPE)
    ang = positions.astype(jnp.float32)[..., None] * inv
    return jnp.cos(ang), jnp.sin(ang)


def apply_rope(t, cos, sin):
    half = t.shape[-1] // 2
    tf = t.astype(jnp.float32)
    t1, t2 = tf[..., :half], tf[..., half:]
    return jnp.concatenate([t1 * cos - t2 * sin, t2 * cos + t1 * sin], axis=-1).astype(t.dtype)


def ssd_chunked(X, dA, Bm, Cm):
    b, s, g, e, p = X.shape
    n = Bm.shape[-1]
    Q = SSD_CHUNK
    c = s // Q
    X = X.reshape(b, c, Q, g, e, p)
    Bc = Bm.reshape(b, c, Q, g, n)
    Cc = Cm.reshape(b, c, Q, g, n)
    A = jnp.transpose(dA.reshape(b, c, Q, g, e), (0, 3, 4, 1, 2))
    A_cum = jnp.cumsum(A, axis=-1)
    tril = jnp.tril(jnp.ones((Q, Q), dtype=bool))
    L = jnp.exp(jnp.where(tril, A_cum[..., :, None] - A_cum[..., None, :], -jnp.inf))
    CB = jnp.einsum('bclgn,bcsgn->bcgls', Cc, Bc)
    y_diag = jnp.einsum('bcgls,bgecls,bcsgep->bclgep', CB, L, X)
    decay_states = jnp.exp(A_cum[..., -1:] - A_cum)
    states = jnp.einsum('bclgn,bgecl,bclgep->bcgepn', Bc, decay_states, X)
    chunk_decay = jnp.exp(A_cum[..., -1])

    def step(h, inp):
        st, dec = inp
        return h * dec[..., None, None] + st, h

    _, h_prev = lax.scan(step, jnp.zeros_like(states[:, 0]),
                         (jnp.moveaxis(states, 1, 0), jnp.moveaxis(chunk_decay, -1, 0)))
    h_prev = jnp.moveaxis(h_prev, 0, 1)
    y_off = jnp.einsum('bclgn,bcgepn,bgecl->bclgep', Cc, h_prev, jnp.exp(A_cum))
    return (y_diag + y_off).reshape(b, s, g, e, p)


def ssd_mixer(z, xbc, dt_raw, conv_w, conv_b, dt_bias, a_log, d_skip, norm_g):
    Bsz, S, _ = z.shape
    G, E, P, N = SSD_GROUPS, SSD_HEADS // SSD_GROUPS, SSD_HEAD_DIM, SSD_STATE
    xbc = jax.nn.silu(causal_dwconv(xbc, conv_w) + conv_b.astype(xbc.dtype))
    xs, bm, cm = jnp.split(xbc, [SSD_INNER, SSD_INNER + G * N], axis=-1)
    xs = xs.astype(jnp.float32).reshape(Bsz, S, G, E, P)
    bm = bm.astype(jnp.float32).reshape(Bsz, S, G, N)
    cm = cm.astype(jnp.float32).reshape(Bsz, S, G, N)
    dt = jax.nn.softplus(dt_raw.astype(jnp.float32) + dt_bias.astype(jnp.float32)).reshape(Bsz, S, G, E)
    a = -jnp.exp(a_log.astype(jnp.float32)).reshape(G, E)
    y = ssd_chunked(xs * dt[..., None], dt * a, bm, cm)
    y = y + d_skip.astype(jnp.float32).reshape(G, E)[:, :, None] * xs
    y = y.reshape(Bsz, S, SSD_INNER) * jax.nn.silu(z.astype(jnp.float32))
    y = rmsnorm(y.reshape(Bsz, S, G, SSD_INNER // G), norm_g.reshape(G, SSD_INNER // G))
    return y.reshape(Bsz, S, SSD_INNER).astype(z.dtype)


def mla_mixer(cq, ckv, k_rope, cos, sin, q_norm_g, w_uq, kv_norm_g, w_ukv):
    Bsz, S, _ = cq.shape
    H = MLA_HEADS
    q = (rmsnorm(cq, q_norm_g) @ w_uq).reshape(Bsz, S, H, MLA_NOPE + MLA_ROPE)
    q_nope, q_pe = jnp.split(q, [MLA_NOPE], axis=-1)
    q_pe = apply_rope(q_pe, cos[:, :, None, :], sin[:, :, None, :])
    kv = (rmsnorm(ckv, kv_norm_g) @ w_ukv).reshape(Bsz, S, H, MLA_NOPE + MLA_V)
    k_nope, v = jnp.split(kv, [MLA_NOPE], axis=-1)
    k_pe = apply_rope(k_rope, cos, sin)
    nblk = S // MLA_QBLOCK
    scale = (MLA_NOPE + MLA_ROPE) ** -0.5
    kpos = jnp.arange(S)

    def to_blocks(t):
        return jnp.moveaxis(t.reshape((Bsz, nblk, MLA_QBLOCK) + t.shape[2:]), 1, 0)

    def attend(args):
        qn, qp, blk = args
        sc = (jnp.einsum('bqhd,bkhd->bhqk', qn, k_nope, preferred_element_type=jnp.float32)
              + jnp.einsum('bqhr,bkr->bhqk', qp, k_pe, preferred_element_type=jnp.float32))
        qpos = blk * MLA_QBLOCK + jnp.arange(MLA_QBLOCK)
        sc = jnp.where(kpos[None, :] <= qpos[:, None], sc * scale, -jnp.inf)
        pr = jax.nn.softmax(sc, axis=-1)
        return jnp.einsum('bhqk,bkhd->bqhd', pr.astype(v.dtype), v)

    o = lax.map(attend, (to_blocks(q_nope), to_blocks(q_pe), jnp.arange(nblk)))
    return jnp.moveaxis(o, 0, 1).reshape(Bsz, S, H * MLA_V)


def gated_delta_chunked(q, k, v, g, beta):
    Bsz, S, H, DK = k.shape
    DV = v.shape[-1]
    L = GDN_CHUNK
    C = S // L

    def chunks(t):
        return jnp.moveaxis(t.reshape((Bsz, C, L, H) + t.shape[3:]), 3, 1)

    q = chunks(q * (DK ** -0.5))
    k = chunks(k)
    v = chunks(v)
    g = jnp.cumsum(chunks(g), axis=-1)
    beta = chunks(beta)
    incl = jnp.tril(jnp.ones((L, L), dtype=bool))
    strict = jnp.tril(jnp.ones((L, L), dtype=bool), -1)
    decay = jnp.exp(jnp.where(incl, g[..., :, None] - g[..., None, :], -jnp.inf))
    kb = k * beta[..., None]
    a_mat = jnp.where(strict, jnp.einsum('bhcld,bhcmd->bhclm', kb, k) * decay, 0.0)
    rhs = jnp.concatenate([v * beta[..., None], kb * jnp.exp(g)[..., None]], axis=-1)
    sol = lax.linalg.triangular_solve(a_mat + jnp.eye(L, dtype=jnp.float32), rhs,
                                      left_side=True, lower=True)
    u, w = jnp.split(sol, [DV], axis=-1)
    qk = jnp.where(incl, jnp.einsum('bhcld,bhcmd->bhclm', q, k) * decay, 0.0)

    def step(state, inp):
        qc, kc, uc, wc, gc, ac = inp
        v_new = uc - jnp.einsum('bhld,bhdv->bhlv', wc, state)
        o = (jnp.einsum('bhld,bhdv->bhlv', qc * jnp.exp(gc)[..., None], state)
             + jnp.einsum('bhlm,bhmv->bhlv', ac, v_new))
        g_last = gc[..., -1:]
        state = (state * jnp.exp(g_last)[..., None]
                 + jnp.einsum('bhld,bhlv->bhdv', kc * jnp.exp(g_last - gc)[..., None], v_new))
        return state, o

    xs = tuple(jnp.moveaxis(t, 2, 0) for t in (q, k, u, w, g, qk))
    _, o = lax.scan(step, jnp.zeros((Bsz, H, DK, DV), jnp.float32), xs)
    return jnp.transpose(o, (1, 0, 3, 2, 4)).reshape(Bsz, S, H, DV)


def gdn_mixer(qkv, z, b_raw, a_raw, conv_w, dt_bias, a_log, norm_g):
    Bsz, S, _ = qkv.shape
    qkv = jax.nn.silu(causal_dwconv(qkv, conv_w))
    q, k, v = jnp.split(qkv, [GDN_QK, 2 * GDN_QK], axis=-1)
    rep = GDN_V_HEADS // GDN_QK_HEADS
    q = jnp.repeat(l2norm(q.reshape(Bsz, S, GDN_QK_HEADS, GDN_HEAD_K)), rep, axis=2)
    k = jnp.repeat(l2norm(k.reshape(Bsz, S, GDN_QK_HEADS, GDN_HEAD_K)), rep, axis=2)
    v = v.reshape(Bsz, S, GDN_V_HEADS, GDN_HEAD_V).astype(jnp.float32)
    beta = jax.nn.sigmoid(b_raw.astype(jnp.float32))
    g = -jnp.exp(a_log.astype(jnp.float32)) * jax.nn.softplus(a_raw.astype(jnp.float32) + dt_bias.astype(jnp.float32))
    o = gated_delta_chunked(q, k, v, g, beta)
    o = rmsnorm(o, norm_g) * jax.nn.silu(z.astype(jnp.float32).reshape(Bsz, S, GDN_V_HEADS, GDN_HEAD_V))
    return o.reshape(Bsz, S, GDN_VW).astype(z.dtype)


def hybrid_layer(x, cos, sin, norm1_g, w_in, ssd_conv_w, ssd_conv_b, ssd_dt_bias, ssd_a_log, ssd_d,
                 ssd_norm_g, mla_q_norm_g, mla_w_uq, mla_kv_norm_g, mla_w_ukv, gdn_conv_w, gdn_dt_bias,
                 gdn_a_log, gdn_norm_g, w_ssd_out, w_mla_out, w_gdn_out, w_out, norm2_g, w_up, w_down):
    xn = rmsnorm(x, norm1_g)
    proj = xn @ w_in
    split_at = [int(i) for i in np.cumsum(IN_SIZES)[:-1]]
    (ssd_z, ssd_xbc, ssd_dt, mla_cq, mla_ckv, mla_kr, gdn_qkv, gdn_z, gdn_b, gdn_a,
     gate_logits) = jnp.split(proj, split_at, axis=-1)
    y_ssd = ssd_mixer(ssd_z, ssd_xbc, ssd_dt, ssd_conv_w, ssd_conv_b, ssd_dt_bias, ssd_a_log, ssd_d, ssd_norm_g)
    y_mla = mla_mixer(mla_cq, mla_ckv, mla_kr, cos, sin, mla_q_norm_g, mla_w_uq, mla_kv_norm_g, mla_w_ukv)
    y_gdn = gdn_mixer(gdn_qkv, gdn_z, gdn_b, gdn_a, gdn_conv_w, gdn_dt_bias, gdn_a_log, gdn_norm_g)
    gates = jax.nn.sigmoid(gate_logits.astype(jnp.float32)).astype(x.dtype)
    g_ssd, g_mla, g_gdn = jnp.split(gates, N_BRANCH, axis=-1)
    mixed = g_ssd * (y_ssd @ w_ssd_out) + g_mla * (y_mla @ w_mla_out) + g_gdn * (y_gdn @ w_gdn_out)
    h = x + mixed @ w_out
    f = jnp.square(jax.nn.relu(rmsnorm(h, norm2_g) @ w_up)) @ w_down
    return h + f


def setup_inputs(seed: int = 0) -> dict:
    key = jax.random.key(seed)
    ks = iter(jax.random.split(key, 40))

    def nrm(shape, scale):
        return jax.random.normal(next(ks), shape, jnp.float32) * scale

    def gain(shape):
        return 1.0 + nrm(shape, 0.02)

    def dt_bias_init(shape):
        dt = jnp.exp(jax.random.uniform(next(ks), shape, jnp.float32, math.log(1e-3), math.log(1e-1)))
        return dt + jnp.log(-jnp.expm1(-dt))

    def a_log_init(shape):
        return jnp.log(jax.random.uniform(next(ks), shape, jnp.float32, 1.0, 16.0))

    x = jax.random.normal(next(ks), (BATCH, SEQ, D_MODEL), jnp.float32)
    offset = jax.random.randint(next(ks), (BATCH, 1), 0, 1024, dtype=jnp.int32)
    positions = offset + jnp.arange(SEQ, dtype=jnp.int32)[None, :]
    return {
        'x': x,
        'positions': positions,
        'norm1_g': gain((DEPTH, D_MODEL)),
        'w_in': nrm((DEPTH, D_MODEL, N_IN), D_MODEL ** -0.5),
        'ssd_conv_w': nrm((DEPTH, SSD_CONV, SSD_XBC), SSD_CONV ** -0.5),
        'ssd_conv_b': nrm((DEPTH, SSD_XBC), 0.01),
        'ssd_dt_bias': dt_bias_init((DEPTH, SSD_HEADS)),
        'ssd_a_log': a_log_init((DEPTH, SSD_HEADS)),
        'ssd_d': gain((DEPTH, SSD_HEADS)),
        'ssd_norm_g': gain((DEPTH, SSD_INNER)),
        'mla_q_norm_g': gain((DEPTH, MLA_Q_LORA)),
        'mla_w_uq': nrm((DEPTH, MLA_Q_LORA, MLA_HEADS * (MLA_NOPE + MLA_ROPE)), MLA_Q_LORA ** -0.5),
        'mla_kv_norm_g': gain((DEPTH, MLA_KV_LORA)),
        'mla_w_ukv': nrm((DEPTH, MLA_KV_LORA, MLA_HEADS * (MLA_NOPE + MLA_V)), MLA_KV_LORA ** -0.5),
        'gdn_conv_w': nrm((DEPTH, GDN_CONV, GDN_QKV), GDN_CONV ** -0.5),
        'gdn_dt_bias': dt_bias_init((DEPTH, GDN_V_HEADS)),
        'gdn_a_log': a_log_init((DEPTH, GDN_V_HEADS)),
        'gdn_norm_g': gain((DEPTH, GDN_HEAD_V)),
        'w_ssd_out': nrm((DEPTH, SSD_INNER, D_MODEL), SSD_INNER ** -0.5),
        'w_mla_out': nrm((DEPTH, MLA_HEADS * MLA_V, D_MODEL), (MLA_HEADS * MLA_V) ** -0.5),
        'w_gdn_out': nrm((DEPTH, GDN_VW, D_MODEL), GDN_VW ** -0.5),
        'w_out': nrm((DEPTH, D_MODEL, D_MODEL), D_MODEL ** -0.5),
        'norm2_g': gain((DEPTH, D_MODEL)),
        'w_up': nrm((DEPTH, D_MODEL, D_FF), D_MODEL ** -0.5),
        'w_down': nrm((DEPTH, D_FF, D_MODEL), D_FF ** -0.5),
        'final_norm_g': gain((D_MODEL,)),
    }


def reference(x, positions, norm1_g, w_in, ssd_conv_w, ssd_conv_b, ssd_dt_bias, ssd_a_log, ssd_d,
              ssd_norm_g, mla_q_norm_g, mla_w_uq, mla_kv_norm_g, mla_w_ukv, gdn_conv_w, gdn_dt_bias,
              gdn_a_log, gdn_norm_g, w_ssd_out, w_mla_out, w_gdn_out, w_out, norm2_g, w_up, w_down,
              final_norm_g):
    cos, sin = rope_tables(positions)
    for l in range(DEPTH):
        x = hybrid_layer(x, cos, sin, norm1_g[l], w_in[l], ssd_conv_w[l], ssd_conv_b[l], ssd_dt_bias[l],
                         ssd_a_log[l], ssd_d[l], ssd_norm_g[l], mla_q_norm_g[l], mla_w_uq[l],
                         mla_kv_norm_g[l], mla_w_ukv[l], gdn_conv_w[l], gdn_dt_bias[l], gdn_a_log[l],
                         gdn_norm_g[l], w_ssd_out[l], w_mla_out[l], w_gdn_out[l], w_out[l], norm2_g[l],
                         w_up[l], w_down[l])
    return rmsnorm(x, final_norm_g)
```

```python
import math
from contextlib import ExitStack
import numpy as np
import concourse.bass as bass
import concourse.mybir as mybir
from concourse.bass_utils import run_bass_kernel_spmd

F32 = mybir.dt.float32
BF16 = mybir.dt.bfloat16
I32 = mybir.dt.int32
AF = mybir.ActivationFunctionType
ALU = mybir.AluOpType
AX = mybir.AxisListType

COMPUTE = ('pe', 'act', 'dve', 'pool')
ALLENG = ('pe', 'act', 'dve', 'pool', 'sp')
NSEM_ENG = 4
NSEM_DMA = 40
NSEM_SWDMA = 16

D = 1024
DEPTH = 2
N_IN = 9568
EPS = 1e-6
NCOL = 130


class Buf:
    def __init__(self, name, t):
        self.name = name
        self.t = t
        self.last_w = None
        self.readers = []
        self.excl = False

    def __getitem__(self, k):
        return V(self, self.t[k])


class DR:
    def __init__(self, t):
        self.t = t

    def __getitem__(self, k):
        return V(None, self.t[k])

    def re(self, s, **kw):
        return V(None, self.t[:].rearrange(s, **kw) if not hasattr(self.t, 'rearrange') else self.t.rearrange(s, **kw))


class V:
    def __init__(self, buf, ap):
        self.buf = buf
        self.ap = ap

    def __getitem__(self, k):
        return V(self.buf, self.ap[k])

    def bc(self, shape):
        return V(self.buf, self.ap.to_broadcast(list(shape)))

    def re(self, s, **kw):
        return V(self.buf, self.ap.rearrange(s, **kw))

    def unsq(self, axis):
        return V(self.buf, self.ap.unsqueeze(axis))

    def pbc(self, n):
        return V(self.buf, self.ap.partition_broadcast(n))

    def bitcast(self, dt):
        return V(self.buf, self.ap.bitcast(dt))

    @property
    def shape(self):
        return self.ap.shape


class Op:
    __slots__ = ('eng', 'fn', 'deps', 'idx', 'dma', 'signal', 'sem', 'val', 'waits', 'gid')

    def __init__(self, eng, fn, dma):
        self.eng = eng
        self.fn = fn
        self.dma = dma
        self.deps = []
        self.idx = -1
        self.signal = False
        self.sem = None
        self.val = 0
        self.waits = []
        self.gid = -1


class Arena:
    def __init__(self, ap, nwords):
        self.ap = ap
        self.n = nwords
        self.off = 0

    def reset(self):
        self.off = 0

    def alloc(self, name, shape, dtype=F32):
        p = shape[0]
        nfree = 1
        for s in shape[1:]:
            nfree *= s
        esz = 2 if dtype == BF16 else 4
        words = (nfree * esz + 3) // 4
        words = (words + 7) // 8 * 8
        assert self.off + words <= self.n, f"arena overflow allocating {name}: {self.off}+{words}>{self.n}"
        a = self.ap[0:p, self.off:self.off + words]
        self.off += words
        if dtype != F32:
            a = a.bitcast(dtype)
        a = a[:, 0:nfree]
        if len(shape) == 3:
            a = a.rearrange("p (a b) -> p a b", a=shape[1])
        elif len(shape) == 4:
            a = a.rearrange("p (a b c) -> p a b c", a=shape[1], b=shape[2])
        return Buf(name, a)


class Prog:
    def __init__(self, nc):
        self.nc = nc
        self.ops = []
        self.stack = None
        self.last_op = {}
        self.dmas_since = []
        self.pending = {}
        self.bar_t = None
        self.rr = 0

    def sbt(self, name, shape, dtype=F32):
        t = self.stack.enter_context(self.nc.sbuf_tensor(name, list(shape), dtype))
        return Buf(name, t)

    def pst(self, name, shape, dtype=F32):
        t = self.stack.enter_context(self.nc.psum_tensor(name, list(shape), dtype))
        return Buf(name, t)

    def dram(self, name, shape, dtype=F32, kind="Internal"):
        return DR(self.nc.dram_tensor(name, list(shape), dtype, kind=kind))

    def add(self, eng, fn, reads, writes, dma=False, extra_deps=()):
        op = Op(eng, fn, dma)
        op.gid = len(self.ops)
        deps = {}
        rb, wb = [], []
        for v in reads:
            b = v.buf if isinstance(v, V) else v
            if b is not None and b not in rb:
                rb.append(b)
        for v in writes:
            b = v.buf if isinstance(v, V) else v
            if b is not None and b not in wb:
                wb.append(b)
        for b in rb:
            if b.last_w is not None:
                deps[b.last_w.gid] = b.last_w
            if b.excl:
                for r in b.readers:
                    if r.eng != eng:
                        deps[r.gid] = r
        for b in wb:
            if b.last_w is not None:
                deps[b.last_w.gid] = b.last_w
            for r in b.readers:
                deps[r.gid] = r
        for d in extra_deps:
            deps[d.gid] = d
        pb = self.pending.pop(eng, None)
        if pb is not None:
            deps[pb.gid] = pb
        for b in rb:
            if b not in wb:
                b.readers.append(op)
        for b in wb:
            b.last_w = op
            b.readers = []
        op.deps = list(deps.values())
        self.ops.append(op)
        if dma:
            self.dmas_since.append(op)
        else:
            self.last_op[eng] = op
        return op

    def barrier(self):
        deps = [o for o in self.last_op.values()] + list(self.dmas_since)
        a = self._a
        bt = self.bar_t
        op = self.add('dve', lambda e: e.memset(a(bt[:]), 0.0), [], [bt[:]], extra_deps=deps)
        self.dmas_since = []
        self.pending = {e: op for e in ALLENG if e != 'dve'}
        return op

    @staticmethod
    def _a(x):
        return x.ap if isinstance(x, V) else x

    def matmul(self, out, lhsT, rhs, start=True, stop=True):
        a = self._a
        return self.add('pe', lambda e: e.matmul(a(out), a(lhsT), a(rhs), start=start, stop=stop),
                        [lhsT, rhs], [out])

    def transpose(self, out, in_, ident):
        a = self._a
        return self.add('pe', lambda e: e.transpose(a(out), a(in_), a(ident)), [in_, ident], [out])

    def act(self, out, in_, func, bias=None, scale=None, accum_out=None):
        a = self._a
        kw = {}
        reads = [in_]
        writes = [out]
        if bias is not None:
            kw['bias'] = a(bias)
            if isinstance(bias, V):
                reads.append(bias)
        if scale is not None:
            kw['scale'] = a(scale)
            if isinstance(scale, V):
                reads.append(scale)
        if accum_out is not None:
            kw['accum_out'] = a(accum_out)
            writes.append(accum_out)
        return self.add('act', lambda e: e.activation(a(out), a(in_), func, **kw), reads, writes)

    def tt(self, out, in0, in1, op, eng='dve'):
        a = self._a
        return self.add(eng, lambda e: e.tensor_tensor(a(out), a(in0), a(in1), op), [in0, in1], [out])

    def ts(self, out, in0, s1, s2, op0, op1=None, eng='dve'):
        a = self._a
        reads = [in0] + [s for s in (s1, s2) if isinstance(s, V)]
        kw = {}
        if op1 is not None:
            kw['op1'] = op1
        return self.add(eng, lambda e: e.tensor_scalar(a(out), a(in0), a(s1), a(s2) if s2 is not None else None, op0, **kw),
                        reads, [out])

    def stt(self, out, in0, scalar, in1, op0, op1, eng='dve'):
        a = self._a
        reads = [in0, in1] + ([scalar] if isinstance(scalar, V) else [])
        return self.add(eng, lambda e: e.scalar_tensor_tensor(a(out), a(in0), a(scalar), a(in1), op0, op1),
                        reads, [out])

    def copy(self, out, in_, eng='dve'):
        a = self._a
        if eng == 'act':
            return self.add('act', lambda e: e.copy(a(out), a(in_)), [in_], [out])
        return self.add(eng, lambda e: e.tensor_copy(a(out), a(in_)), [in_], [out])

    def acopy(self, out, in_):
        self.rr += 1
        return self.copy(out, in_, eng='act' if self.rr % 2 else 'dve')

    def memset(self, out, val, eng='dve'):
        a = self._a
        return self.add(eng, lambda e: e.memset(a(out), val), [], [out])

    def recip(self, out, in_):
        a = self._a
        return self.add('dve', lambda e: e.reciprocal(a(out), a(in_)), [in_], [out])

    def affine_select(self, out, in_, pattern, cmp, fill, base=0, cm=0):
        a = self._a
        return self.add('pool', lambda e: e.affine_select(a(out), a(in_), pattern, cmp, fill, base=base,
                                                          channel_multiplier=cm), [in_], [out])

    def dma(self, out, in_, q='sp'):
        a = self._a
        return self.add(q, lambda e: e.dma_start(a(out), a(in_)), [in_], [out], dma=True)

    def emit(self):
        nc = self.nc
        ops = self.ops
        eng_ops = {e: [] for e in ALLENG}
        for op in ops:
            op.idx = len(eng_ops[op.eng])
            eng_ops[op.eng].append(op)
        known = {e: {x: -1 for x in ALLENG} for e in ALLENG}
        known_dma = {e: set() for e in ALLENG}
        for op in ops:
            E = op.eng
            for d in op.deps:
                if d.dma:
                    if d.gid in known_dma[E]:
                        continue
                    known_dma[E].add(d.gid)
                    op.waits.append(d)
                else:
                    if d.eng == 'pe' and E == 'pe' and not op.dma:
                        continue
                    if known[E][d.eng] >= d.idx:
                        continue
                    known[E][d.eng] = d.idx
                    d.signal = True
                    op.waits.append(d)
        st = self.stack
        sems = {e: [st.enter_context(nc.semaphore(f"s_{e}{i}")) for i in range(NSEM_ENG)] for e in COMPUTE}
        dsems = [st.enter_context(nc.semaphore(f"s_dma{i}")) for i in range(NSEM_DMA)]
        swsems = [st.enter_context(nc.semaphore(f"s_swdma{i}")) for i in range(NSEM_SWDMA)]
        cnt = {e: 0 for e in COMPUTE}
        dcnt = 0
        swcnt = 0
        last_dma_val = {}
        for op in ops:
            if op.dma and op.eng == 'pool':
                op.signal = True
                k = swcnt
                swcnt += 1
                op.sem = swsems[k % NSEM_SWDMA]
                op.val = 16 * (k // NSEM_SWDMA + 1)
                last_dma_val[('sw', k % NSEM_SWDMA)] = (op.sem, op.val)
            elif op.dma:
                op.signal = True
                k = dcnt
                dcnt += 1
                op.sem = dsems[k % NSEM_DMA]
                op.val = 16 * (k // NSEM_DMA + 1)
                last_dma_val[('hw', k % NSEM_DMA)] = (op.sem, op.val)
            elif op.signal:
                k = cnt[op.eng]
                cnt[op.eng] += 1
                op.sem = sems[op.eng][k % NSEM_ENG]
                op.val = k // NSEM_ENG + 1
        block = st.enter_context(nc.Block())

        def body(ename):
            def f(e):
                for op in eng_ops[ename]:
                    for d in op.waits:
                        e.wait_ge(d.sem, d.val)
                    ins = op.fn(e)
                    if op.signal:
                        ins.then_inc(op.sem, 16 if op.dma else 1)
                if ename == 'sp':
                    for (sm, v) in last_dma_val.values():
                        e.wait_ge(sm, v)
            return f

        block.tensor(body('pe'))
        block.scalar(body('act'))
        block.vector(body('dve'))
        block.gpsimd(body('pool'))
        block.sync(body('sp'))
        stats = {e: len(eng_ops[e]) for e in ALLENG}
        stats['signals'] = dict(cnt)
        stats['dmas'] = dcnt
        return stats


class K:
    def __init__(self, S, depth=DEPTH, debug=()):
        self.S = S
        self.NT = S // 128
        self.NQ = S // 512
        self.NCH = S // 64
        self.depth = depth
        self.debug = set(debug)
        self.nc = bass.Bass("TRN2", target_bir_lowering=False)
        self.P = Prog(self.nc)

    def scratch(self, name, shape, dtype=F32):
        kind = "ExternalOutput" if name in self.debug else "Internal"
        return self.P.dram(name, shape, dtype, kind=kind)

    def build(self):
        P, nc, S = self.P, self.nc, self.S
        L = self.depth
        inp = lambda n, sh, dt=F32: P.dram(n, sh, dt, kind="ExternalInput")
        self.x_in = inp("x", [S, D])
        self.pos_in = inp("positions", [S], I32)
        self.colpack = inp("colpack", [DEPTH, 128, NCOL])
        self.invf = inp("invf", [64, 1])
        W = {}
        for n, sh in [("norm1_g", [DEPTH, D]), ("w_in", [DEPTH, D, N_IN]), ("ssd_dt_bias", [DEPTH, 16]),
                      ("ssd_a_log", [DEPTH, 16]), ("ssd_d", [DEPTH, 16]), ("ssd_norm_g", [DEPTH, D]),
                      ("mla_w_uq", [DEPTH, 512, 1536]), ("mla_w_ukv", [DEPTH, 256, 2048]),
                      ("gdn_dt_bias", [DEPTH, 8]), ("gdn_a_log", [DEPTH, 8]), ("gdn_norm_g", [DEPTH, 128]),
                      ("w_ssd_out", [DEPTH, D, D]), ("w_mla_out", [DEPTH, D, D]), ("w_gdn_out", [DEPTH, D, D]),
                      ("w_out", [DEPTH, D, D]), ("norm2_g", [DEPTH, D]), ("w_up", [DEPTH, D, 4 * D]),
                      ("w_down", [DEPTH, 4 * D, D]), ("final_norm_g", [D])]:
            W[n] = inp(n, sh)
        self.W = W
        self.out = P.dram("out", [S, D], F32, kind="ExternalOutput")
        sc = self.scratch
        self.xres = sc("xres", [S, D])
        self.hres = sc("hres", [S, D])
        self.zs2 = sc("zs2", [S, D])
        self.zg2 = sc("zg2", [S, D])
        self.xbcT = sc("xbcT", [1536, S])
        self.qkvT = sc("qkvT", [2048, S])
        self.dtk = sc("dtk", [S, 16])
        self.gbk = sc("gbk", [S, 16])
        self.cqT = sc("cqT", [512, S])
        self.ckvT = sc("ckvT", [256, S])
        self.krT = sc("krT", [128, S])
        self.gatesT = sc("gatesT", [3072, S])
        self.yT = sc("yT", [3, D, S], BF16)
        self.uT = sc("uT", [4 * D, S], BF16)
        self.hnT = sc("hnT", [D, S], BF16)
        self.cosT = sc("cosT", [64, S])
        self.sinT = sc("sinT", [64, S])

        with ExitStack() as st:
            P.stack = st
            C = {}
            C['maskLE'] = P.sbt("maskLE", [128, 128], F32)
            C['maskGT'] = P.sbt("maskGT", [128, 128], F32)
            C['maskLT'] = P.sbt("maskLT", [128, 128], F32)
            C['ones'] = P.sbt("onesf", [128, 128], F32)
            C['identf'] = P.sbt("identf", [128, 128], F32)
            C['ident'] = P.sbt("identb", [128, 128], BF16)
            C['onesb'] = P.sbt("onesb", [128, 128], BF16)
            C['maskLEb'] = P.sbt("maskLEb", [128, 128], BF16)
            C['nmaskLT'] = P.sbt("nmaskLT", [128, 128], F32)
            C['eps'] = P.sbt("epsc", [128, 1], F32)
            C['one'] = P.sbt("onec", [128, 1], F32)
            P.bar_t = P.sbt("bar_t", [128, 1], F32)
            self.C = C
            ARW = 46000
            arena_t = st.enter_context(nc.sbuf_tensor("arena", [128, ARW], F32))
            self.A = Arena(arena_t, ARW)
            self.banks = [P.pst(f"bank{i}", [128, 512], F32) for i in range(8)]
            for b_ in self.banks:
                b_.excl = True

            P.memset(C['ones'][:], 1.0)
            P.memset(C['onesb'][:], 1.0)
            P.memset(C['eps'][:], EPS)
            P.memset(C['one'][:], 1.0)
            P.memset(C['identf'][:], 0.0)
            P.affine_select(C['identf'][:], C['identf'][:], [[-1, 128]], ALU.not_equal, 1.0, base=0, cm=1)
            P.copy(C['ident'][:], C['identf'][:])
            P.affine_select(C['maskLE'][:], C['ones'][:], [[1, 128]], ALU.is_ge, 0.0, base=0, cm=-1)
            P.affine_select(C['maskLT'][:], C['ones'][:], [[1, 128]], ALU.is_ge, 0.0, base=-1, cm=-1)
            P.affine_select(C['maskGT'][:], C['ones'][:], [[-1, 128]], ALU.is_ge, 0.0, base=-1, cm=1)
            P.copy(C['maskLEb'][:], C['maskLE'][:])
            P.ts(C['nmaskLT'][:], C['maskLT'][:], -1.0, None, ALU.mult)

            self.rope_tables()
            P.barrier()
            for l in range(L):
                xsrc = self.x_in if l == 0 else self.xres
                self.phase1(l, xsrc)
                P.barrier()
                if 'stop1' in self.debug:
                    break
                if 'skipssd' not in self.debug:
                    self.phase_ssd(l)
                    P.barrier()
                if 'stop2' in self.debug:
                    break
                if 'skipmla' not in self.debug:
                    self.phase_mla(l)
                    P.barrier()
                if 'stop3' in self.debug:
                    break
                self.phase_gdn(l)
                P.barrier()
                if 'stop4' in self.debug:
                    break
                self.phase_merge(l, xsrc)
                P.barrier()
                self.phase_ffn_up(l)
                P.barrier()
                self.phase_ffn_down(l, last=(l == L - 1))
                P.barrier()
            self.stats = P.emit()
        return nc

    def rmsnorm_rstd(self, rstd, ssq, n):
        P = self.P
        p = ssq.shape[0]
        P.act(rstd, ssq, AF.Sqrt, bias=self.C['eps'][0:p, 0:1], scale=1.0 / n)
        P.recip(rstd, rstd)

    def load_w(self, dst, src):
        self.P.dma(dst, src, q='pool')

    def rope_tables(self):
        P, A, S = self.P, self.A, self.S
        A.reset()
        posi = A.alloc("posi", [64, S], I32)
        posf = A.alloc("posf", [64, S], F32)
        ang = A.alloc("ang", [64, S], F32)
        kk = A.alloc("kk", [64, S], F32)
        res = A.alloc("res", [64, S], F32)
        invc = A.alloc("invc", [64, 1], F32)
        P.dma(posi[:], self.pos_in[:].pbc(64))
        P.dma(invc[:], self.invf[:])
        P.copy(posf[:], posi[:])
        P.ts(ang[:], posf[:], invc[:, 0:1], None, ALU.mult)
        MAG = 12582912.0
        for name, shift, dst in (("sin", 0.0, self.sinT), ("cos", math.pi / 2, self.cosT)):
            a2 = ang
            if shift != 0.0:
                P.ts(posf[:], ang[:], shift, None, ALU.add)
                a2 = posf
            P.ts(kk[:], a2[:], 1.0 / (2 * math.pi), MAG, ALU.mult, ALU.add)
            P.ts(kk[:], kk[:], -MAG, None, ALU.add)
            P.stt(res[:], kk[:], -2 * math.pi, a2[:], ALU.mult, ALU.add)
            P.ts(res[:], res[:], math.pi, -math.pi, ALU.min, ALU.max)
            P.act(res[:], res[:], AF.Sin)
            P.dma(dst[:], res[:])

    def phase1(self, l, xsrc):
        P, A, S, C = self.P, self.A, self.S, self.C
        NT, NQ = self.NT, self.NQ
        A.reset()
        xnT = A.alloc("xnT", [128, 8, S], BF16)
        gB = A.alloc("gB", [128, D], F32)
        xt = [A.alloc(f"xt{i}", [128, D], F32) for i in range(2)]
        xn = [A.alloc(f"xn{i}", [128, D], BF16) for i in range(2)]
        junk = A.alloc("junk", [128, D], BF16)
        ssq = [A.alloc(f"ssq{i}", [128, 1], F32) for i in range(2)]
        rstd = [A.alloc(f"rstd{i}", [128, 1], F32) for i in range(2)]
        P.dma(gB[:], self.W['norm1_g'][l].pbc(128))
        bk = self.banks

        def load(t):
            P.dma(xt[t % 2][:], xsrc[t * 128:(t + 1) * 128, :])
        load(0)
        for t in range(NT):
            if t + 1 < NT:
                load(t + 1)
            b = t % 2
            P.act(junk[:], xt[b][:], AF.Square, accum_out=ssq[b][:])
            self.rmsnorm_rstd(rstd[b][:], ssq[b][:], D)
            P.stt(xn[b][:], xt[b][:], rstd[b][:, 0:1], gB[:], ALU.mult, ALU.mult)
            pb = bk[t % 2][:].bitcast(BF16)
            for j in range(8):
                P.transpose(pb[:, j * 128:(j + 1) * 128], xn[b][:, j * 128:(j + 1) * 128], C['ident'][:])
            P.acopy(xnT[:, :, t * 128:(t + 1) * 128], pb.re("p (j t) -> p j t", j=8))

        wsl = [A.alloc(f"wsl{i}", [128, 8, 512], BF16) for i in range(2)]
        stg = [A.alloc(f"stg{i}", [128, S], F32) for i in range(2)]
        stk = [A.alloc(f"stk{i}", [128, 512], F32) for i in range(3)]
        tmp = [A.alloc(f"tmp{i}", [128, 512], F32) for i in range(2)]
        win = self.W['w_in']
        wv = win.t[l].rearrange("(j p) c -> p j c", p=128)
        segs = [
            (0, 1024, 'silu2', 'tok', self.zs2, 0),
            (1024, 1536, 'copy', 'feat', self.xbcT, 0),
            (2560, 16, 'copy', 'tok', self.dtk, 0),
            (2576, 512, 'copy', 'feat', self.cqT, 0),
            (3088, 256, 'copy', 'feat', self.ckvT, 0),
            (3344, 64, 'rope', 'feat', self.krT, 0),
            (3408, 2048, 'copy', 'feat', self.qkvT, 0),
            (5456, 1024, 'silu2', 'tok', self.zg2, 0),
            (6480, 16, 'copy', 'tok', self.gbk, 0),
            (6496, 3072, 'sigmoid', 'feat', self.gatesT, 0),
        ]
        cnt = {'slab': 0, 'bank': 0, 'stg': 0, 'stk': 0, 'tmp': 0}

        def evac(dst, ps, kind):
            if kind == 'copy' or kind == 'rope':
                P.acopy(dst, ps)
            elif kind == 'silu2':
                tm = tmp[cnt['tmp'] % 2]
                cnt['tmp'] += 1
                w = ps.shape[1]
                P.act(tm[:, 0:w], ps, AF.Tanh, scale=0.5)
                P.stt(dst, tm[:, 0:w], 1.0, ps, ALU.add, ALU.mult)
            elif kind == 'sigmoid':
                P.act(dst, ps, AF.Tanh, scale=0.5)
                P.ts(dst, dst, 0.5, 0.5, ALU.mult, ALU.add, eng='pool')

        for (c0, ncols, kind, layout, dst, _) in segs:
            for s0 in range(0, ncols, 512):
                w = min(512, ncols - s0)
                sl = wsl[cnt['slab'] % 2]
                cnt['slab'] += 1
                self.load_w(sl[:, :, 0:w], V(None, wv[:, :, c0 + s0:c0 + s0 + w]))
                wuse = w
                if kind == 'rope':
                    P.ts(sl[:, :, 64:96], sl[:, :, 32:64], -1.0, None, ALU.mult)
                    P.copy(sl[:, :, 96:128], sl[:, :, 0:32])
                    wuse = 128
                if layout == 'feat':
                    for b0 in range(0, wuse, 128):
                        nb = min(128, wuse - b0)
                        sg = stg[cnt['stg'] % 2]
                        cnt['stg'] += 1
                        for q in range(NQ):
                            ps = bk[cnt['bank'] % 4]
                            cnt['bank'] += 1
                            for k in range(8):
                                P.matmul(ps[0:nb, :], sl[:, k, b0:b0 + nb], xnT[:, k, q * 512:(q + 1) * 512],
                                         start=(k == 0), stop=(k == 7))
                            evac(sg[0:nb, q * 512:(q + 1) * 512], ps[0:nb, :], kind)
                        r0 = c0 - segs_base(c0, segs) + s0 + b0
                        P.dma(dst[r0:r0 + nb, :], sg[0:nb, :], q='act')
                else:
                    for t in range(NT):
                        ps = bk[cnt['bank'] % 4]
                        cnt['bank'] += 1
                        for k in range(8):
                            P.matmul(ps[:, 0:w], xnT[:, k, t * 128:(t + 1) * 128], sl[:, k, 0:w],
                                     start=(k == 0), stop=(k == 7))
                        sk = stk[cnt['stk'] % 3]
                        cnt['stk'] += 1
                        evac(sk[:, 0:w], ps[:, 0:w], kind)
                        P.dma(dst[t * 128:(t + 1) * 128, s0:s0 + w], sk[:, 0:w], q='act')

    def conv_block(self, srcT, row0, wcols, bcol, dst_fn, bufs, S):
        P = self.P
        CW = min(S, 1024)
        raw, acc, th = bufs
        nseg = S // CW
        for sgi in range(nseg):
            t0 = sgi * CW
            r = raw[sgi % 2]
            if t0 == 0:
                P.memset(r[:, 0:3], 0.0, eng='pool')
                P.dma(r[:, 3:3 + CW], srcT[row0:row0 + 128, t0:t0 + CW])
            else:
                P.dma(r[:, 0:3 + CW], srcT[row0:row0 + 128, t0 - 3:t0 + CW])
            a = acc[sgi % 2]
            if bcol is not None:
                P.ts(a[:], r[:, 3:3 + CW], wcols[:, 3:4], bcol, ALU.mult, ALU.add)
            else:
                P.ts(a[:], r[:, 3:3 + CW], wcols[:, 3:4], None, ALU.mult)
            P.stt(a[:], r[:, 2:2 + CW], wcols[:, 2:3], a[:], ALU.mult, ALU.add)
            P.stt(a[:], r[:, 1:1 + CW], wcols[:, 1:2], a[:], ALU.mult, ALU.add)
            P.stt(a[:], r[:, 0:CW], wcols[:, 0:1], a[:], ALU.mult, ALU.add)
            t = th[sgi % 2]
            P.act(t[:], a[:], AF.Tanh, scale=0.5)
            P.stt(a[:], t[:], 1.0, a[:], ALU.add, ALU.mult)
            P.add('act', (lambda o, i: (lambda e: e.mul(o, i, 0.5)))(dst_fn(t0, CW).ap, a[:].ap), [a[:]], [dst_fn(t0, CW)])

    def softplus(self, out, x, tmp1, tmp2):
        P = self.P
        p = x.shape[0]
        P.act(tmp1, x, AF.Abs)
        P.act(tmp1, tmp1, AF.Exp, scale=-1.0)
        P.act(tmp1, tmp1, AF.Ln, bias=self.C['one'][0:p, 0:1], scale=1.0)
        P.stt(out, x, 0.0, tmp1, ALU.max, ALU.add)

    def phase_ssd(self, l):
        P, A, S, C = self.P, self.A, self.S, self.C
        NT = self.NT
        A.reset()
        bk = self.banks
        xcT = A.alloc("xcT", [128, 12, S], BF16)
        cp = A.alloc("cp", [128, NCOL], F32)
        P.dma(cp[:], self.colpack[l])
        CW = min(S, 1024)
        mark = A.off
        raw = [A.alloc(f"raw{i}", [128, CW + 3], F32) for i in range(2)]
        acc = [A.alloc(f"acc{i}", [128, CW], F32) for i in range(2)]
        th = [A.alloc(f"th{i}", [128, CW], F32) for i in range(2)]
        for j in range(12):
            self.conv_block(self.xbcT, j * 128, cp[:, j * 4:(j + 1) * 4], cp[:, 48 + j:49 + j],
                            (lambda jj: (lambda t0, cw: xcT[:, jj, t0:t0 + cw]))(j), (raw, acc, th), S)
        P.barrier()
        A.off = mark
        dtb = A.alloc("dtb", [128, 16], F32)
        alog = A.alloc("alog", [128, 16], F32)
        aB = A.alloc("aB", [128, 16], F32)
        dB = A.alloc("dB", [128, 16], F32)
        ngB = A.alloc("ngB", [128, D], F32)
        P.dma(dtb[:], self.W['ssd_dt_bias'][l].pbc(128))
        P.dma(alog[:], self.W['ssd_a_log'][l].pbc(128))
        P.dma(dB[:], self.W['ssd_d'][l].pbc(128))
        P.dma(ngB[:], self.W['ssd_norm_g'][l].pbc(128))
        P.act(aB[:], alog[:], AF.Exp)
        P.ts(aB[:], aB[:], -1.0, None, ALU.mult)
        dt_all = A.alloc("dt_all", [128, NT, 16], F32)
        dA_all = A.alloc("dA_all", [128, NT, 16], F32)
        t1 = A.alloc("t1", [128, NT, 16], F32)
        P.dma(dt_all[:], self.dtk.t[:].rearrange("(n p) h -> p n h", p=128) if False else V(None, self.dtk.t.ap().rearrange("(n p) h -> p n h", p=128)))
        P.tt(dt_all[:], dt_all[:], dtb[:].unsq(1).bc([128, NT, 16]), ALU.add)
        self.softplus(dt_all[:], dt_all[:], t1[:], None)
        P.tt(dA_all[:], dt_all[:], aB[:].unsq(1).bc([128, NT, 16]), ALU.mult)

        hs = A.alloc("hs", [128, D], F32)
        hsb = A.alloc("hsb", [128, D], BF16)
        P.memset(hs[:], 0.0)
        P.memset(hsb[:], 0.0, eng='pool')
        Xsb = A.alloc("Xsb", [128, D], BF16)
        Btok = A.alloc("Btok", [128, 256], BF16)
        ex = A.alloc("ex", [128, 48], F32)
        CBm = A.alloc("CBm", [128, 2, 128], BF16)
        rhsD = A.alloc("rhsD", [128, 16, 128], F32)
        LT = A.alloc("LT", [128, 8, 128], BF16)
        MT = A.alloc("MT", [128, 16, 128], BF16)
        Xdt = A.alloc("Xdt", [128, D], BF16)
        Xds = A.alloc("Xds", [128, D], BF16)
        y1 = A.alloc("y1", [128, D], F32)
        t2 = A.alloc("t2", [128, D], F32)
        yz = A.alloc("yz", [128, D], F32)
        jk = A.alloc("jk", [128, 512], BF16)
        ssq = A.alloc("ssq", [128, 2], F32)
        rstd = A.alloc("rstd", [128, 2], F32)
        yn = A.alloc("yn", [128, D], BF16)
        zt = [A.alloc(f"zt{i}", [128, D], F32) for i in range(2)]
        ystg = [A.alloc(f"ystg{i}", [128, 8, 128], BF16) for i in range(2)]
        mLE, mGT, ones = C['maskLE'], C['maskGT'], C['ones']
        yTd = self.yT.t[0].rearrange("(j p) t -> p j t", p=128)

        def loadz(c):
            P.dma(zt[c % 2][:], self.zs2[c * 128:(c + 1) * 128, :])
        loadz(0)
        for c in range(NT):
            if c + 1 < NT:
                loadz(c + 1)
            cs = slice(c * 128, (c + 1) * 128)
            dA = dA_all[:, c, :]
            dt = dt_all[:, c, :]
            pb0 = bk[0][:].bitcast(BF16)
            for j in range(8):
                P.transpose(pb0[:, j * 128:(j + 1) * 128], xcT[:, j, cs], C['ident'][:])
            P.copy(Xsb[:], pb0, eng='act')
            pb1 = bk[1][:].bitcast(BF16)
            for j in range(2):
                P.transpose(pb1[:, j * 128:(j + 1) * 128], xcT[:, 8 + j, cs], C['ident'][:])
            P.copy(Btok[:], pb1[:, 0:256])
            P.matmul(bk[2][:, 0:16], mLE[:], dA)
            P.matmul(bk[2][:, 16:32], mGT[:], dA)
            P.matmul(bk[2][:, 32:48], ones[:], dA)
            P.act(ex[:], bk[2][:, 0:48], AF.Exp)
            eA, ds, cd = ex[:, 0:16], ex[:, 16:32], ex[:, 32:48]
            for g in range(2):
                P.matmul(bk[2][:, 64 + g * 128:64 + (g + 1) * 128], xcT[:, 8 + g, cs], xcT[:, 10 + g, cs])
            P.tt(CBm[:], bk[2][:, 64:320].re("p (g l) -> p g l", g=2), mLE[:].unsq(1).bc([128, 2, 128]), ALU.mult)
            P.tt(rhsD[:], mLE[:].unsq(1).bc([128, 16, 128]), dA.unsq(2).bc([128, 16, 128]), ALU.mult, eng='pool')
            for g in range(2):
                for i in range(2):
                    P.matmul(bk[3 + i][:], mGT[:], rhsD[:, g * 8 + i * 4:g * 8 + i * 4 + 4, :].re("p h l -> p (h l)"))
                    P.act(LT[:, i * 4:(i + 1) * 4, :].re("p h l -> p (h l)"), bk[3 + i][:], AF.Exp)
                P.tt(MT[:, g * 8:(g + 1) * 8, :], LT[:], CBm[:, g:g + 1, :].bc([128, 8, 128]), ALU.mult)
            P.tt(Xdt[:].re("p (h q) -> p h q", h=16), Xsb[:].re("p (h q) -> p h q", h=16),
                 dt.unsq(2).bc([128, 16, 64]), ALU.mult)
            P.tt(Xds[:].re("p (h q) -> p h q", h=16), Xdt[:].re("p (h q) -> p h q", h=16),
                 ds.unsq(2).bc([128, 16, 64]), ALU.mult, eng='pool')
            for h in range(16):
                P.matmul(bk[5 + h // 8][:, (h % 8) * 64:(h % 8 + 1) * 64], MT[:, h, :], Xdt[:, h * 64:(h + 1) * 64])
            for g in range(2):
                P.matmul(bk[3 + g][:], xcT[:, 10 + g, cs], hsb[:, g * 512:(g + 1) * 512])
            sbank = (bk[7], bk[1])
            for g in range(2):
                P.matmul(sbank[g][:], Btok[:, g * 128:(g + 1) * 128], Xds[:, g * 512:(g + 1) * 512])
            for g in range(2):
                gs = slice(g * 512, (g + 1) * 512)
                P.tt(y1[:, gs].re("p (h q) -> p h q", h=8), bk[3 + g][:].re("p (h q) -> p h q", h=8),
                     eA[:, g * 8:(g + 1) * 8].unsq(2).bc([128, 8, 64]), ALU.mult)
                P.tt(y1[:, gs], y1[:, gs], bk[5 + g][:], ALU.add)
            P.tt(t2[:].re("p (h q) -> p h q", h=16), Xsb[:].re("p (h q) -> p h q", h=16),
                 dB[:].unsq(2).bc([128, 16, 64]), ALU.mult, eng='pool')
            P.tt(y1[:], y1[:], t2[:], ALU.add)
            P.stt(yz[:], y1[:], 0.5, zt[c % 2][:], ALU.mult, ALU.mult)
            for g in range(2):
                gs = slice(g * 512, (g + 1) * 512)
                P.act(jk[:], yz[:, gs], AF.Square, accum_out=ssq[:, g:g + 1])
            self.rmsnorm_rstd(rstd[:], ssq[:], 512)
            for g in range(2):
                gs = slice(g * 512, (g + 1) * 512)
                P.stt(yn[:, gs], yz[:, gs], rstd[:, g:g + 1], ngB[:, gs], ALU.mult, ALU.mult)
            pb0 = bk[0][:].bitcast(BF16)
            for j in range(8):
                P.transpose(pb0[:, j * 128:(j + 1) * 128], yn[:, j * 128:(j + 1) * 128], C['ident'][:])
            ys = ystg[c % 2]
            P.copy(ys[:], pb0.re("p (j t) -> p j t", j=8), eng='act')
            P.dma(V(None, yTd[:, :, cs]), ys[:], q='act')
            for g in range(2):
                gs = slice(g * 512, (g + 1) * 512)
                P.tt(hs[:, gs].re("p (h q) -> p h q", h=8), hs[:, gs].re("p (h q) -> p h q", h=8),
                     cd[:, g * 8:(g + 1) * 8].unsq(2).bc([128, 8, 64]), ALU.mult)
                P.tt(hs[:, gs], hs[:, gs], sbank[g][:], ALU.add)
            P.copy(hsb[:], hs[:], eng='pool')

    def phase_mla(self, l):
        P, A, S, C = self.P, self.A, self.S, self.C
        NT, NQ = self.NT, self.NQ
        A.reset()
        bk = self.banks
        cp = A.alloc("cp", [128, NCOL], F32)
        P.dma(cp[:], self.colpack[l])
        cqn = A.alloc("cqn", [128, 4, S], BF16)
        ckvn = A.alloc("ckvn", [128, 2, S], BF16)
        kpe = A.alloc("kpe", [64, S], BF16)
        wuq = A.alloc("wuq", [128, 4, 1536], BF16)
        wukv = A.alloc("wukv", [128, 2, 2048], BF16)
        wrot = A.alloc("wrot", [128, 4, 512], BF16)
        self.load_w(wuq[:], V(None, self.W['mla_w_uq'].t[l].rearrange("(j p) c -> p j c", p=128)))
        self.load_w(wukv[:], V(None, self.W['mla_w_ukv'].t[l].rearrange("(j p) c -> p j c", p=128)))
        for h in range(8):
            P.ts(wrot[:, :, h * 64:h * 64 + 32], wuq[:, :, h * 192 + 160:h * 192 + 192], -1.0, None, ALU.mult)
            P.copy(wrot[:, :, h * 64 + 32:h * 64 + 64], wuq[:, :, h * 192 + 128:h * 192 + 160], eng='pool')
        mark = A.off
        cin = [A.alloc(f"cin{i}", [128, 4, 512], F32) for i in range(2)]
        kin = [A.alloc(f"kin{i}", [128, 2, 512], F32) for i in range(2)]
        rin = [A.alloc(f"rin{i}", [64, 2, 512], F32) for i in range(2)]
        cs_ = [A.alloc(f"cs{i}", [64, 2, 512], F32) for i in range(2)]
        sq = A.alloc("sq", [128, 4, 512], BF16)
        rs = A.alloc("rs", [128, 2, 512], F32)
        tr = A.alloc("tr", [64, 2, 512], F32)
        cqv = self.cqT.t.ap().rearrange("(j p) t -> p j t", p=128)
        ckv = self.ckvT.t.ap().rearrange("(j p) t -> p j t", p=128)
        krv = self.krT.t.ap().rearrange("(j p) t -> p j t", p=64)

        def load3a(q):
            qs = slice(q * 512, (q + 1) * 512)
            P.dma(cin[q % 2][:], V(None, cqv[:, :, qs]))
            P.dma(kin[q % 2][:], V(None, ckv[:, :, qs]))
            P.dma(rin[q % 2][:], V(None, krv[:, :, qs]))
            P.dma(cs_[q % 2][:, 0, :], self.cosT[:, qs])
            P.dma(cs_[q % 2][:, 1, :], self.sinT[:, qs])
        load3a(0)
        for q in range(NQ):
            if q + 1 < NQ:
                load3a(q + 1)
            qs = slice(q * 512, (q + 1) * 512)
            ci, ki, ri, cs2 = cin[q % 2], kin[q % 2], rin[q % 2], cs_[q % 2]
            P.act(sq[:], ci[:], AF.Square)
            for j in range(4):
                P.matmul(bk[0][:], C['onesb'][:], sq[:, j, :], start=(j == 0), stop=(j == 3))
            P.act(sq[:, 0:2, :], ki[:], AF.Square)
            for j in range(2):
                P.matmul(bk[1][:], C['onesb'][:], sq[:, j, :], start=(j == 0), stop=(j == 1))
            P.act(rs[:, 0, :], bk[0][:], AF.Sqrt, bias=C['eps'][:, 0:1], scale=1.0 / 512)
            P.act(rs[:, 1, :], bk[1][:], AF.Sqrt, bias=C['eps'][:, 0:1], scale=1.0 / 256)
            P.recip(rs[:], rs[:])
            for j in range(4):
                P.stt(cqn[:, j, qs], ci[:, j, :], cp[:, 124 + j:125 + j], rs[:, 0, :], ALU.mult, ALU.mult)
            for j in range(2):
                P.stt(ckvn[:, j, qs], ki[:, j, :], cp[:, 128 + j:129 + j], rs[:, 1, :], ALU.mult, ALU.mult)
            P.tt(tr[:], ri[:], cs2[:], ALU.mult)
            P.tt(kpe[:, qs], tr[:, 0, :], tr[:, 1, :], ALU.add)
        P.barrier()
        A.off = mark
        KT = [A.alloc(f"KT{i}", [128, S], BF16) for i in range(2)]
        Vh = [A.alloc(f"Vh{i}", [128, NT, 128], BF16) for i in range(2)]
        QT = [A.alloc(f"QT{i}", [128, S], BF16) for i in range(2)]
        qpe = [A.alloc(f"qpe{i}", [64, S], BF16) for i in range(2)]
        csq = [A.alloc(f"csq{i}", [64, 2, 512], F32) for i in range(2)]
        tq = A.alloc("tq", [64, 2, 512], F32)
        pT = [A.alloc(f"pT{i}", [128, 512], BF16) for i in range(3)]
        rden = A.alloc("rden", [128, 512], F32)
        yst = [A.alloc(f"yst{i}", [128, 512], BF16) for i in range(2)]
        scale = 192.0 ** -0.5
        cnt = {'p': 0, 'y': 0, 'cs': 0}
        for h in range(8):
            hb = h % 2
            for q in range(NQ):
                qs = slice(q * 512, (q + 1) * 512)
                ps = bk[6 + q % 2]
                for j in range(2):
                    P.matmul(ps[:], wukv[:, j, h * 256:h * 256 + 128], ckvn[:, j, qs], start=(j == 0), stop=(j == 1))
                P.acopy(KT[hb][:, qs], ps[:])
                ps = bk[6 + (q + 1) % 2]
                for j in range(4):
                    P.matmul(ps[:], wuq[:, j, h * 192:h * 192 + 128], cqn[:, j, qs], start=(j == 0), stop=(j == 3))
                P.acopy(QT[hb][:, qs], ps[:])
                cq2 = csq[cnt['cs'] % 2]
                cnt['cs'] += 1
                P.dma(cq2[:, 0, :], self.cosT[:, qs])
                P.dma(cq2[:, 1, :], self.sinT[:, qs])
                ps = bk[6 + q % 2]
                for j in range(4):
                    P.matmul(ps[0:64, 0:512], wuq[:, j, h * 192 + 128:h * 192 + 192], cqn[:, j, qs],
                             start=(j == 0), stop=(j == 3))
                P.tt(tq[:, 0, :], ps[0:64, :], cq2[:, 0, :], ALU.mult)
                ps = bk[6 + (q + 1) % 2]
                for j in range(4):
                    P.matmul(ps[0:64, 0:512], wrot[:, j, h * 64:(h + 1) * 64], cqn[:, j, qs],
                             start=(j == 0), stop=(j == 3))
                P.tt(tq[:, 1, :], ps[0:64, :], cq2[:, 1, :], ALU.mult)
                P.tt(qpe[hb][:, qs], tq[:, 0, :], tq[:, 1, :], ALU.add)
            for t4 in range(0, NT, 4):
                ps = bk[6 + (t4 // 4) % 2]
                for tt_ in range(4):
                    t = t4 + tt_
                    for j in range(2):
                        P.matmul(ps[:, tt_ * 128:(tt_ + 1) * 128], ckvn[:, j, t * 128:(t + 1) * 128],
                                 wukv[:, j, h * 256 + 128:h * 256 + 256], start=(j == 0), stop=(j == 1))
                P.acopy(Vh[hb][:, t4:t4 + 4, :], ps[:].re("p (a v) -> p a v", a=4))
            for c in range(NQ):
                qs = slice(c * 512, (c + 1) * 512)
                ob = bk[2 + c % 2]
                db = bk[4 + c % 2]
                nkb = 4 * c + 4
                for kb in range(nkb):
                    j = kb - 4 * c
                    lo = 0 if j <= 0 else 128 * j
                    ks = slice(kb * 128, (kb + 1) * 128)
                    sb_ = bk[kb % 2]
                    P.matmul(sb_[:, lo:512], KT[hb][:, ks], QT[hb][:, c * 512 + lo:(c + 1) * 512], start=True, stop=False)
                    P.matmul(sb_[:, lo:512], kpe[:, ks], qpe[hb][:, c * 512 + lo:(c + 1) * 512], start=False, stop=True)
                    pt = pT[cnt['p'] % 3]
                    cnt['p'] += 1
                    P.act(pt[:, lo:512], sb_[:, lo:512], AF.Exp, scale=scale)
                    if j >= 0:
                        P.tt(pt[:, lo:lo + 128], pt[:, lo:lo + 128], C['maskLEb'][:], ALU.mult, eng='pool')
                    P.matmul(ob[:, lo:512], Vh[hb][:, kb, :], pt[:, lo:512], start=(kb == 0), stop=(kb == nkb - 1))
                    P.matmul(db[:, lo:512], C['onesb'][:], pt[:, lo:512], start=(kb == 0), stop=(kb == nkb - 1))
                P.recip(rden[:], db[:])
                ys = yst[cnt['y'] % 2]
                cnt['y'] += 1
                P.tt(ys[:], ob[:], rden[:], ALU.mult)
                P.dma(self.yT[1, h * 128:(h + 1) * 128, qs], ys[:], q='act')

    def phase_gdn(self, l):
        P, A, S, C = self.P, self.A, self.S, self.C
        NCH = self.NCH
        A.reset()
        bk = self.banks
        cp = A.alloc("cp", [128, NCOL], F32)
        P.dma(cp[:], self.colpack[l])
        gb = A.alloc("gb", [64, NCH, 16], F32)
        P.dma(gb[:], V(None, self.gbk.t.ap().rearrange("(n p) c -> p n c", p=64)))
        beta_all = A.alloc("beta_all", [64, NCH, 8], F32)
        g_all = A.alloc("g_all", [64, NCH, 8], F32)
        tg = A.alloc("tg", [64, NCH, 8], F32)
        dtb = A.alloc("gdtb", [64, 8], F32)
        alog = A.alloc("galog", [64, 8], F32)
        aB = A.alloc("gaB", [64, 8], F32)
        ngB = A.alloc("gngB", [64, 128], F32)
        P.dma(dtb[:], self.W['gdn_dt_bias'][l].pbc(64))
        P.dma(alog[:], self.W['gdn_a_log'][l].pbc(64))
        P.dma(ngB[:], self.W['gdn_norm_g'][l].pbc(64))
        P.act(aB[:], alog[:], AF.Exp)
        P.ts(aB[:], aB[:], -1.0, None, ALU.mult)
        P.act(beta_all[:], gb[:, :, 0:8], AF.Tanh, scale=0.5)
        P.ts(beta_all[:], beta_all[:], 0.5, 0.5, ALU.mult, ALU.add)
        P.tt(g_all[:], gb[:, :, 8:16], dtb[:].unsq(1).bc([64, NCH, 8]), ALU.add)
        self.softplus(g_all[:], g_all[:], tg[:], None)
        P.tt(g_all[:], g_all[:], aB[:].unsq(1).bc([64, NCH, 8]), ALU.mult)
        g16 = A.alloc("g16", [64, NCH, 16], F32)
        P.copy(g16[:, :, 0:8], g_all[:])
        P.copy(g16[:, :, 8:16], g_all[:])
        mark0 = A.off
        mLE, mGT, nmLT, ones, identf = C['maskLE'], C['maskGT'], C['nmaskLT'], C['ones'], C['identf']
        CW = min(S, 1024)
        dbg = self.debug
        if 'g0' in dbg:
            return
        for i in range(4):
            A.off = mark0
            qn = A.alloc("qn", [128, S], BF16)
            kn = A.alloc("kn", [128, S], BF16)
            vT = A.alloc("vT", [128, 2, S], BF16)
            ostg = A.alloc("ostg", [128, 2, S], BF16)
            mark1 = A.off
            raw = [A.alloc(f"raw{k}", [128, CW + 3], F32) for k in range(2)]
            acc = [A.alloc(f"acc{k}", [128, CW], F32) for k in range(2)]
            th = [A.alloc(f"th{k}", [128, CW], F32) for k in range(2)]
            qc = A.alloc("qc", [128, S], BF16)
            kc = A.alloc("kc", [128, S], BF16)
            sq = A.alloc("sq", [128, 512], BF16)
            rsn = A.alloc("rsn", [128, 512], F32)
            blocks = [(i, lambda t0, cw: qc[:, t0:t0 + cw]), (4 + i, lambda t0, cw: kc[:, t0:t0 + cw]),
                      (8 + 2 * i, lambda t0, cw: vT[:, 0, t0:t0 + cw]), (9 + 2 * i, lambda t0, cw: vT[:, 1, t0:t0 + cw])]
            for (blk, dfn) in blocks:
                self.conv_block(self.qkvT, blk * 128, cp[:, 60 + blk * 4:64 + blk * 4], None, dfn, (raw, acc, th), S)
            for (src, dst, sc_) in ((qc, qn, 128.0 ** -0.5), (kc, kn, 1.0)):
                for q in range(self.NQ):
                    qs = slice(q * 512, (q + 1) * 512)
                    P.act(sq[:], src[:, qs], AF.Square)
                    P.matmul(bk[q % 2][:], C['onesb'][:], sq[:])
                    P.act(rsn[:], bk[q % 2][:], AF.Sqrt, bias=C['eps'][:, 0:1], scale=1.0)
                    P.recip(rsn[:], rsn[:])
                    P.stt(dst[:, qs], src[:, qs], sc_, rsn[:], ALU.mult, ALU.mult)
            P.barrier()
            if 'g1' in dbg:
                return
            A.off = mark1
            Sf = A.alloc("Sf", [128, 2, 128], F32)
            Sb = A.alloc("Sb", [128, 2, 128], BF16)
            P.memset(Sf[:], 0.0)
            P.memset(Sb[:], 0.0, eng='pool')
            gm = A.alloc("gm", [64, 2, 64], F32)
            etot = A.alloc("etot", [128, 16], F32)
            erem = A.alloc("erem", [64, 16], F32)
            egR = A.alloc("egR", [128, 2, 64], BF16)
            decT = A.alloc("decT", [64, 2, 64], F32)
            dm = A.alloc("dm", [64, 2, 64], F32)
            dm2 = A.alloc("dm2", [64, 2, 64], F32)
            PT0f = A.alloc("PT0f", [64, 2, 64], F32)
            PT = [A.alloc(f"PT{k}", [64, 2, 64], BF16) for k in range(2)]
            Pm = [A.alloc(f"Pm{k}", [64, 2, 64], BF16) for k in range(2)]
            TT = A.alloc("TT", [64, 2, 64], F32)
            TTb = A.alloc("TTb", [64, 2, 64], BF16)
            QKd = A.alloc("QKd", [64, 2, 64], BF16)
            kgT = A.alloc("kgT", [128, 2, 64], BF16)
            qgT = A.alloc("qgT", [128, 2, 64], BF16)
            kd = A.alloc("kd", [64, 2, 128], BF16)
            vtok = A.alloc("vtok", [64, 2, 128], F32)
            Xs = A.alloc("Xs", [64, 2, 128], BF16)
            vnew = A.alloc("vnew", [64, 2, 128], BF16)
            zt = [A.alloc(f"gzt{k}", [64, 256], F32) for k in range(2)]
            jk = A.alloc("gjk", [64, 128], BF16)
            ssq = A.alloc("gssq", [64, 2], F32)
            rstd = A.alloc("grstd", [64, 2], F32)
            on = A.alloc("on", [64, 2, 128], F32)
            otok = A.alloc("otok", [64, 2, 128], BF16)

            def loadz(n):
                P.dma(zt[n % 2][:], self.zg2[n * 64:(n + 1) * 64, (2 * i) * 128:(2 * i + 2) * 128])
            loadz(0)
            for n in range(NCH):
                if n + 1 < NCH:
                    loadz(n + 1)
                cs = slice(n * 64, (n + 1) * 64)
                g2 = g_all[:, n, 2 * i:2 * i + 2]
                b2 = beta_all[:, n, 2 * i:2 * i + 2]
                Ba = bk[0]
                P.matmul(Ba[0:64, 0:128], kn[:, cs], C['ident'][:])
                for e in range(2):
                    P.matmul(Ba[0:64, 128 + e * 128:256 + e * 128], vT[:, e, cs], C['ident'][:])
                P.tt(gm[:], mLE[0:64, 0:64].unsq(1).bc([64, 2, 64]), g2.unsq(2).bc([64, 2, 64]), ALU.mult, eng='pool')
                gmf = gm[:].re("p e l -> p (e l)")
                Bb = bk[1]
                gq = g16[:, n, :]
                P.matmul(Bb[:, 0:16], ones[0:64, :], gq)
                P.matmul(Bb[0:64, 16:32], mGT[0:64, 0:64], gq)
                P.matmul(Bb[:, 128:256], ones[0:64, :], gmf)
                P.matmul(Bb[0:64, 256:384], mGT[0:64, 0:64], gmf)
                P.act(etot[:], Bb[:, 0:16], AF.Exp)
                P.act(erem[:], Bb[0:64, 16:32], AF.Exp)
                P.act(egR[:].re("p e l -> p (e l)"), Bb[:, 128:256], AF.Exp)
                P.act(decT[:].re("p e l -> p (e l)"), Bb[0:64, 256:384], AF.Exp)
                if 'g2a' in dbg:
                    continue
                Bc = bk[2]
                P.matmul(Bc[0:64, 0:64], kn[:, cs], kn[:, cs])
                P.matmul(Bc[0:64, 64:128], kn[:, cs], qn[:, cs])
                P.tt(dm[:], decT[:], nmLT[0:64, 0:64].unsq(1).bc([64, 2, 64]), ALU.mult, eng='pool')
                P.tt(dm[:], dm[:], Bc[0:64, 0:64].unsq(1).bc([64, 2, 64]), ALU.mult)
                for e in range(2):
                    P.ts(PT0f[:, e, :], dm[:, e, :], b2[:, e:e + 1], None, ALU.mult)
                P.copy(PT[0][:], PT0f[:], eng='pool')
                P.tt(dm2[:], decT[:], mLE[0:64, 0:64].unsq(1).bc([64, 2, 64]), ALU.mult, eng='pool')
                P.tt(QKd[:], dm2[:], Bc[0:64, 64:128].unsq(1).bc([64, 2, 64]), ALU.mult)
                if 'g2b' in dbg:
                    continue
                Bd = bk[3]
                for e in range(2):
                    P.matmul(Bd[0:64, e * 64:(e + 1) * 64], PT[0][:, e, :], C['ident'][0:64, 0:64])
                P.copy(Pm[0][:].re("p e l -> p (e l)"), Bd[0:64, 0:128], eng='act')
                P.tt(TT[:], PT0f[:], identf[0:64, 0:64].unsq(1).bc([64, 2, 64]), ALU.add)
                P.copy(TTb[:], TT[:], eng='pool')
                cur = 0
                for it in range(5):
                    nxt = 1 - cur
                    for e in range(2):
                        P.matmul(Bd[0:64, e * 64:(e + 1) * 64], PT[cur][:, e, :], Pm[cur][:, e, :])
                        if it < 4:
                            P.matmul(Bd[0:64, 128 + e * 64:128 + (e + 1) * 64], Pm[cur][:, e, :], PT[cur][:, e, :])
                    P.copy(Pm[nxt][:].re("p e l -> p (e l)"), Bd[0:64, 0:128], eng='act')
                    if it < 4:
                        P.copy(PT[nxt][:].re("p e l -> p (e l)"), Bd[0:64, 128:256])
                    for e in range(2):
                        P.matmul(Bd[0:64, 256 + e * 64:256 + (e + 1) * 64], Pm[nxt][:, e, :], TTb[:, e, :])
                    P.tt(TT[:].re("p e l -> p (e l)"), TT[:].re("p e l -> p (e l)"), Bd[0:64, 256:384], ALU.add)
                    P.copy(TTb[:], TT[:], eng='pool')
                    cur = nxt
                if 'g2c' in dbg:
                    continue
                for e in range(2):
                    P.tt(kgT[:, e, :], kn[:, cs], egR[:, e, :], ALU.mult, eng='pool')
                    P.tt(qgT[:, e, :], qn[:, cs], egR[:, e, :], ALU.mult, eng='pool')
                    P.ts(kd[:, e, :], Ba[0:64, 0:128], erem[:, 2 * i + e:2 * i + e + 1], None, ALU.mult)
                P.copy(vtok[:].re("p e v -> p (e v)"), Ba[0:64, 128:384], eng='act')
                if 'g2' in dbg:
                    continue
                Be, Bg = bk[4], bk[6]
                Bf = (bk[5], bk[7])
                for e in range(2):
                    P.matmul(Be[0:64, e * 128:(e + 1) * 128], kgT[:, e, :], Sb[:, e, :])
                    P.matmul(Bf[e][0:64, 0:128], qgT[:, e, :], Sb[:, e, :], start=True, stop=False)
                P.tt(Xs[:].re("p e v -> p (e v)"), vtok[:].re("p e v -> p (e v)"), Be[0:64, 0:256], ALU.subtract)
                for e in range(2):
                    P.matmul(Be[0:64, 256 + e * 128:256 + (e + 1) * 128], TTb[:, e, :], Xs[:, e, :])
                for e in range(2):
                    P.ts(vnew[:, e, :], Be[0:64, 256 + e * 128:256 + (e + 1) * 128], b2[:, e:e + 1], None, ALU.mult)
                for e in range(2):
                    P.matmul(Bf[e][0:64, 0:128], QKd[:, e, :], vnew[:, e, :], start=False, stop=True)
                    P.matmul(Bg[:, e * 128:(e + 1) * 128], kd[:, e, :], vnew[:, e, :])
                for e in range(2):
                    P.ts(Sf[:, e, :], Sf[:, e, :], etot[:, 2 * i + e:2 * i + e + 1], None, ALU.mult)
                P.tt(Sf[:].re("p e v -> p (e v)"), Sf[:].re("p e v -> p (e v)"), Bg[:, 0:256], ALU.add)
                P.copy(Sb[:], Sf[:], eng='pool')
                if 'g3' in dbg:
                    continue
                for e in range(2):
                    P.act(jk[:], Bf[e][0:64, 0:128], AF.Square, accum_out=ssq[:, e:e + 1])
                self.rmsnorm_rstd(rstd[:], ssq[:], 128)
                for e in range(2):
                    P.stt(on[:, e, :], Bf[e][0:64, 0:128], rstd[:, e:e + 1], ngB[:], ALU.mult, ALU.mult)
                P.stt(otok[:].re("p e v -> p (e v)"), on[:].re("p e v -> p (e v)"), 0.5, zt[n % 2][:], ALU.mult, ALU.mult)
                Bh = bk[0]
                for e in range(2):
                    P.matmul(Bh[:, 384 + e * 64:384 + (e + 1) * 64], otok[:, e, :], C['ident'][0:64, 0:64])
                P.copy(ostg[:, :, cs], Bh[:, 384:512].re("p (e t) -> p e t", e=2), eng='act')
            for e in range(2):
                P.dma(self.yT[2, (2 * i + e) * 128:(2 * i + e + 1) * 128, :], ostg[:, e, :], q='act')
            P.barrier()

    def phase_merge(self, l, xsrc):
        P, A, S, C = self.P, self.A, self.S, self.C
        NQ = self.NQ
        A.reset()
        bk = self.banks
        ws = []
        for n in ("w_ssd_out", "w_mla_out", "w_gdn_out", "w_out"):
            w = A.alloc(n, [128, 8, D], BF16)
            self.load_w(w[:], V(None, self.W[n].t[l].rearrange("(j p) c -> p j c", p=128)))
            ws.append(w)
        gB = A.alloc("g2B", [128, D], F32)
        P.dma(gB[:], self.W['norm2_g'][l].pbc(128))
        yin = [[A.alloc(f"yin{b}_{i}", [128, 8, 512], BF16) for b in range(3)] for i in range(2)]
        gt = [A.alloc(f"gt{i}", [128, 3, 512], F32) for i in range(2)]
        mixT = A.alloc("mixT", [128, 8, 512], BF16)
        tm = A.alloc("tm", [128, 512], F32)
        tm2 = A.alloc("tm2", [128, 512], F32)
        xt = [A.alloc(f"xt{i}", [128, D], F32) for i in range(2)]
        ht = [A.alloc(f"ht{i}", [128, D], F32) for i in range(2)]
        hn = A.alloc("hn", [128, D], BF16)
        junk = A.alloc("junk", [128, D], BF16)
        ssq = A.alloc("ssq", [128, 1], F32)
        rstd = A.alloc("rstd", [128, 1], F32)
        hstg = [A.alloc(f"hstg{i}", [128, 8, 128], BF16) for i in range(2)]
        gv = self.gatesT.t.ap().rearrange("(b r) t -> r b t", b=3)
        hnTd = self.hnT.t.ap().rearrange("(j p) t -> p j t", p=128)
        cnt = {'g': 0, 'x': 0}

        def loady(q):
            qs = slice(q * 512, (q + 1) * 512)
            for b in range(3):
                P.dma(yin[q % 2][b][:], V(None, self.yT.t[b].rearrange("(j p) t -> p j t", p=128)[:, :, qs]))
        loady(0)
        for q in range(NQ):
            if q + 1 < NQ:
                loady(q + 1)
            qs = slice(q * 512, (q + 1) * 512)
            for cb in range(8):
                g = gt[cnt['g'] % 2]
                cnt['g'] += 1
                P.dma(g[:], V(None, gv[cb * 128:(cb + 1) * 128, :, qs]))
                for b in range(3):
                    ps = bk[b]
                    for k in range(8):
                        P.matmul(ps[:], ws[b][:, k, cb * 128:(cb + 1) * 128], yin[q % 2][b][:, k, :],
                                 start=(k == 0), stop=(k == 7))
                P.tt(tm[:], bk[0][:], g[:, 0, :], ALU.mult)
                P.tt(tm2[:], bk[1][:], g[:, 1, :], ALU.mult)
                P.tt(tm[:], tm[:], tm2[:], ALU.add, eng='pool')
                P.tt(tm2[:], bk[2][:], g[:, 2, :], ALU.mult)
                P.tt(mixT[:, cb, :], tm[:], tm2[:], ALU.add, eng='pool')
            for t4 in range(4):
                t = q * 4 + t4
                x_ = xt[cnt['x'] % 2]
                h_ = ht[cnt['x'] % 2]
                hs_ = hstg[cnt['x'] % 2]
                cnt['x'] += 1
                P.dma(x_[:], xsrc[t * 128:(t + 1) * 128, :])
                for half in range(2):
                    ps = bk[4 + half]
                    for k in range(8):
                        P.matmul(ps[:], mixT[:, k, t4 * 128:(t4 + 1) * 128], ws[3][:, k, half * 512:(half + 1) * 512],
                                 start=(k == 0), stop=(k == 7))
                    P.tt(h_[:, half * 512:(half + 1) * 512], ps[:], x_[:, half * 512:(half + 1) * 512], ALU.add)
                P.dma(self.hres[t * 128:(t + 1) * 128, :], h_[:], q='act')
                P.act(junk[:], h_[:], AF.Square, accum_out=ssq[:])
                self.rmsnorm_rstd(rstd[:], ssq[:], D)
                P.stt(hn[:], h_[:], rstd[:, 0:1], gB[:], ALU.mult, ALU.mult)
                pb = bk[6 + t4 % 2][:].bitcast(BF16)
                for j in range(8):
                    P.transpose(pb[:, j * 128:(j + 1) * 128], hn[:, j * 128:(j + 1) * 128], C['ident'][:])
                P.copy(hs_[:], pb.re("p (j t) -> p j t", j=8), eng='pool' if False else 'dve')
                P.dma(V(None, hnTd[:, :, t * 128:(t + 1) * 128]), hs_[:], q='act')

    def phase_ffn_up(self, l):
        P, A, S, C = self.P, self.A, self.S, self.C
        NQ = self.NQ
        A.reset()
        bk = self.banks
        wup = A.alloc("wup", [128, 8, 4 * D], BF16)
        wv = self.W['w_up'].t[l].rearrange("(j p) c -> p j c", p=128)
        for s in range(8):
            self.load_w(wup[:, :, s * 512:(s + 1) * 512], V(None, wv[:, :, s * 512:(s + 1) * 512]))
        hin = [A.alloc(f"hin{i}", [128, 8, 512], BF16) for i in range(2)]
        rl = [A.alloc(f"rl{i}", [128, 512], BF16) for i in range(2)]
        ust = [A.alloc(f"ust{i}", [128, 4, 512], BF16) for i in range(2)]
        hnTd = self.hnT.t.ap().rearrange("(j p) t -> p j t", p=128)
        uTd = self.uT.t.ap().rearrange("(f p) t -> p f t", p=128)
        cnt = 0

        def loadh(q):
            P.dma(hin[q % 2][:], V(None, hnTd[:, :, q * 512:(q + 1) * 512]))
        loadh(0)
        for q in range(NQ):
            if q + 1 < NQ:
                loadh(q + 1)
            qs = slice(q * 512, (q + 1) * 512)
            for f4 in range(8):
                us = ust[f4 % 2]
                for ff in range(4):
                    fb = f4 * 4 + ff
                    ps = bk[cnt % 4]
                    r = rl[cnt % 2]
                    cnt += 1
                    for k in range(8):
                        P.matmul(ps[:], wup[:, k, fb * 128:(fb + 1) * 128], hin[q % 2][:, k, :],
                                 start=(k == 0), stop=(k == 7))
                    P.act(r[:], ps[:], AF.Relu)
                    P.tt(us[:, ff, :], r[:], r[:], ALU.mult, eng='dve' if cnt % 2 else 'pool')
                P.dma(V(None, uTd[:, f4 * 4:(f4 + 1) * 4, qs]), us[:], q='act')

    def phase_ffn_down(self, l, last):
        P, A, S, C = self.P, self.A, self.S, self.C
        NQ = self.NQ
        A.reset()
        bk = self.banks
        wd = A.alloc("wd", [128, 32, D], BF16)
        wv = self.W['w_down'].t[l].rearrange("(f p) c -> p f c", p=128)
        for s in range(4):
            self.load_w(wd[:, s * 8:(s + 1) * 8, :], V(None, wv[:, s * 8:(s + 1) * 8, :]))
        uin = [A.alloc(f"uin{i}", [128, 32, 512], BF16) for i in range(2)]
        ht = [A.alloc(f"ht{i}", [128, D], F32) for i in range(2)]
        ot = [A.alloc(f"ot{i}", [128, D], F32) for i in range(2)]
        gF = A.alloc("gF", [128, D], F32)
        junk = A.alloc("junk", [128, D], BF16)
        ssq = A.alloc("ssq", [128, 1], F32)
        rstd = A.alloc("rstd", [128, 1], F32)
        if last:
            P.dma(gF[:], self.W['final_norm_g'][:].pbc(128))
        uTd = self.uT.t.ap().rearrange("(f p) t -> p f t", p=128)
        cnt = 0

        def loadu(q):
            for s in range(4):
                P.dma(uin[q % 2][:, s * 8:(s + 1) * 8, :], V(None, uTd[:, s * 8:(s + 1) * 8, q * 512:(q + 1) * 512]))
        loadu(0)
        for q in range(NQ):
            if q + 1 < NQ:
                loadu(q + 1)
            for t4 in range(4):
                t = q * 4 + t4
                h_ = ht[cnt % 2]
                o_ = ot[cnt % 2]
                cnt += 1
                P.dma(h_[:], self.hres[t * 128:(t + 1) * 128, :])
                for half in range(2):
                    ps = bk[(cnt % 2) * 2 + half]
                    for f in range(32):
                        P.matmul(ps[:], uin[q % 2][:, f, t4 * 128:(t4 + 1) * 128], wd[:, f, half * 512:(half + 1) * 512],
                                 start=(f == 0), stop=(f == 31))
                    P.tt(o_[:, half * 512:(half + 1) * 512], ps[:], h_[:, half * 512:(half + 1) * 512], ALU.add)
                if last:
                    P.act(junk[:], o_[:], AF.Square, accum_out=ssq[:])
                    self.rmsnorm_rstd(rstd[:], ssq[:], D)
                    P.stt(o_[:], o_[:], rstd[:, 0:1], gF[:], ALU.mult, ALU.mult)
                    P.dma(self.out[t * 128:(t + 1) * 128, :], o_[:], q='act')
                else:
                    P.dma(self.xres[t * 128:(t + 1) * 128, :], o_[:], q='act')


def segs_base(c0, segs):
    return c0


def make_colpack(inputs):
    L = inputs['ssd_conv_w'].shape[0]
    cp = np.zeros((L, 128, NCOL), np.float32)
    for l in range(L):
        w = np.asarray(inputs['ssd_conv_w'][l])
        cp[l, :, 0:48] = w.reshape(4, 12, 128).transpose(2, 1, 0).reshape(128, 48)
        cp[l, :, 48:60] = np.asarray(inputs['ssd_conv_b'][l]).reshape(12, 128).T
        w = np.asarray(inputs['gdn_conv_w'][l])
        cp[l, :, 60:124] = w.reshape(4, 16, 128).transpose(2, 1, 0).reshape(128, 64)
        cp[l, :, 124:128] = np.asarray(inputs['mla_q_norm_g'][l]).reshape(4, 128).T
        cp[l, :, 128:130] = np.asarray(inputs['mla_kv_norm_g'][l]).reshape(2, 128).T
    return cp


_CACHE = {}


def get_nc(S, depth=DEPTH, debug=()):
    key = (S, depth, tuple(sorted(debug)))
    if key not in _CACHE:
        k = K(S, depth, debug)
        k.build()
        _CACHE[key] = k
    return _CACHE[key]


def run(inputs, ncores, S, depth=DEPTH, debug=(), trace=False):
    k = get_nc(S, depth, debug)
    cp = make_colpack(inputs)
    invf = (10000.0 ** (-np.arange(0, 64, 2, dtype=np.float32) / 64)).astype(np.float32)
    invf = np.concatenate([invf, invf]).reshape(64, 1).astype(np.float32)
    shared = {n: np.ascontiguousarray(np.asarray(inputs[n], dtype=np.float32)) for n in k.W}
    shared['colpack'] = cp
    shared['invf'] = invf
    in_maps = []
    for c in range(ncores):
        m = dict(shared)
        m['x'] = np.ascontiguousarray(np.asarray(inputs['x'][c], dtype=np.float32))
        m['positions'] = np.ascontiguousarray(np.asarray(inputs['positions'][c], dtype=np.int32))
        in_maps.append(m)
    res = run_bass_kernel_spmd(k.nc, in_maps, core_ids=list(range(ncores)), trace=trace)
    return res


def kernel(**inputs):
    x = np.asarray(inputs['x'])
    B, S, _ = x.shape
    res = run(inputs, B, S)
    out = np.stack([np.asarray(r['out']) for r in res.results], axis=0).astype(np.float32)
    return out
```

```python
import math
from contextlib import ExitStack
import numpy as np
import concourse.bass as bass
import concourse.mybir as mybir
from concourse.bass_utils import run_bass_kernel_spmd

F32 = mybir.dt.float32
BF16 = mybir.dt.bfloat16
I32 = mybir.dt.int32
AF = mybir.ActivationFunctionType
ALU = mybir.AluOpType
AX = mybir.AxisListType

COMPUTE = ('pe', 'act', 'dve', 'pool')
ALLENG = ('pe', 'act', 'dve', 'pool', 'sp')
NSEM_ENG = 4
NSEM_DMA = 40
NSEM_SWDMA = 16

D = 1024
DEPTH = 2
N_IN = 9568
EPS = 1e-6
NCOL = 130


class Buf:
    def __init__(self, name, t):
        self.name = name
        self.t = t
        self.last_w = None
        self.readers = []
        self.excl = False

    def __getitem__(self, k):
        return V(self, self.t[k])


class DR:
    def __init__(self, t):
        self.t = t

    def __getitem__(self, k):
        return V(None, self.t[k])

    def re(self, s, **kw):
        return V(None, self.t[:].rearrange(s, **kw) if not hasattr(self.t, 'rearrange') else self.t.rearrange(s, **kw))


class V:
    def __init__(self, buf, ap):
        self.buf = buf
        self.ap = ap

    def __getitem__(self, k):
        return V(self.buf, self.ap[k])

    def bc(self, shape):
        return V(self.buf, self.ap.to_broadcast(list(shape)))

    def re(self, s, **kw):
        return V(self.buf, self.ap.rearrange(s, **kw))

    def unsq(self, axis):
        return V(self.buf, self.ap.unsqueeze(axis))

    def pbc(self, n):
        return V(self.buf, self.ap.partition_broadcast(n))

    def bitcast(self, dt):
        return V(self.buf, self.ap.bitcast(dt))

    @property
    def shape(self):
        return self.ap.shape


class Op:
    __slots__ = ('eng', 'fn', 'deps', 'idx', 'dma', 'signal', 'sem', 'val', 'waits', 'gid')

    def __init__(self, eng, fn, dma):
        self.eng = eng
        self.fn = fn
        self.dma = dma
        self.deps = []
        self.idx = -1
        self.signal = False
        self.sem = None
        self.val = 0
        self.waits = []
        self.gid = -1


class Arena:
    def __init__(self, ap, nwords):
        self.ap = ap
        self.n = nwords
        self.off = 0

    def reset(self):
        self.off = 0

    def alloc(self, name, shape, dtype=F32):
        p = shape[0]
        nfree = 1
        for s in shape[1:]:
            nfree *= s
        esz = 2 if dtype == BF16 else 4
        words = (nfree * esz + 3) // 4
        words = (words + 7) // 8 * 8
        assert self.off + words <= self.n, f"arena overflow allocating {name}: {self.off}+{words}>{self.n}"
        a = self.ap[0:p, self.off:self.off + words]
        self.off += words
        if dtype != F32:
            a = a.bitcast(dtype)
        a = a[:, 0:nfree]
        if len(shape) == 3:
            a = a.rearrange("p (a b) -> p a b", a=shape[1])
        elif len(shape) == 4:
            a = a.rearrange("p (a b c) -> p a b c", a=shape[1], b=shape[2])
        return Buf(name, a)


class Prog:
    def __init__(self, nc):
        self.nc = nc
        self.ops = []
        self.stack = None
        self.last_op = {}
        self.dmas_since = []
        self.pending = {}
        self.bar_t = None
        self.rr = 0

    def sbt(self, name, shape, dtype=F32):
        t = self.stack.enter_context(self.nc.sbuf_tensor(name, list(shape), dtype))
        return Buf(name, t)

    def pst(self, name, shape, dtype=F32):
        t = self.stack.enter_context(self.nc.psum_tensor(name, list(shape), dtype))
        return Buf(name, t)

    def dram(self, name, shape, dtype=F32, kind="Internal"):
        return DR(self.nc.dram_tensor(name, list(shape), dtype, kind=kind))

    def add(self, eng, fn, reads, writes, dma=False, extra_deps=()):
        op = Op(eng, fn, dma)
        op.gid = len(self.ops)
        deps = {}
        rb, wb = [], []
        for v in reads:
            b = v.buf if isinstance(v, V) else v
            if b is not None and b not in rb:
                rb.append(b)
        for v in writes:
            b = v.buf if isinstance(v, V) else v
            if b is not None and b not in wb:
                wb.append(b)
        for b in rb:
            if b.last_w is not None:
                deps[b.last_w.gid] = b.last_w
            if b.excl:
                for r in b.readers:
                    if r.eng != eng:
                        deps[r.gid] = r
        for b in wb:
            if b.last_w is not None:
                deps[b.last_w.gid] = b.last_w
            for r in b.readers:
                deps[r.gid] = r
        for d in extra_deps:
            deps[d.gid] = d
        pb = self.pending.pop(eng, None)
        if pb is not None:
            deps[pb.gid] = pb
        for b in rb:
            if b not in wb:
                b.readers.append(op)
        for b in wb:
            b.last_w = op
            b.readers = []
        op.deps = list(deps.values())
        self.ops.append(op)
        if dma:
            self.dmas_since.append(op)
        else:
            self.last_op[eng] = op
        return op

    def barrier(self):
        deps = [o for o in self.last_op.values()] + list(self.dmas_since)
        a = self._a
        bt = self.bar_t
        op = self.add('dve', lambda e: e.memset(a(bt[:]), 0.0), [], [bt[:]], extra_deps=deps)
        self.dmas_since = []
        self.pending = {e: op for e in ALLENG if e != 'dve'}
        return op

    @staticmethod
    def _a(x):
        return x.ap if isinstance(x, V) else x

    def matmul(self, out, lhsT, rhs, start=True, stop=True):
        a = self._a
        return self.add('pe', lambda e: e.matmul(a(out), a(lhsT), a(rhs), start=start, stop=stop),
                        [lhsT, rhs], [out])

    def transpose(self, out, in_, ident):
        a = self._a
        return self.add('pe', lambda e: e.transpose(a(out), a(in_), a(ident)), [in_, ident], [out])

    def act(self, out, in_, func, bias=None, scale=None, accum_out=None):
        a = self._a
        kw = {}
        reads = [in_]
        writes = [out]
        if bias is not None:
            kw['bias'] = a(bias)
            if isinstance(bias, V):
                reads.append(bias)
        if scale is not None:
            kw['scale'] = a(scale)
            if isinstance(scale, V):
                reads.append(scale)
        if accum_out is not None:
            kw['accum_out'] = a(accum_out)
            writes.append(accum_out)
        return self.add('act', lambda e: e.activation(a(out), a(in_), func, **kw), reads, writes)

    def tt(self, out, in0, in1, op, eng='dve'):
        a = self._a
        return self.add(eng, lambda e: e.tensor_tensor(a(out), a(in0), a(in1), op), [in0, in1], [out])

    def ts(self, out, in0, s1, s2, op0, op1=None, eng='dve'):
        a = self._a
        reads = [in0] + [s for s in (s1, s2) if isinstance(s, V)]
        kw = {}
        if op1 is not None:
            kw['op1'] = op1
        return self.add(eng, lambda e: e.tensor_scalar(a(out), a(in0), a(s1), a(s2) if s2 is not None else None, op0, **kw),
                        reads, [out])

    def stt(self, out, in0, scalar, in1, op0, op1, eng='dve'):
        a = self._a
        reads = [in0, in1] + ([scalar] if isinstance(scalar, V) else [])
        return self.add(eng, lambda e: e.scalar_tensor_tensor(a(out), a(in0), a(scalar), a(in1), op0, op1),
                        reads, [out])

    def copy(self, out, in_, eng='dve'):
        a = self._a
        if eng == 'act':
            return self.add('act', lambda e: e.copy(a(out), a(in_)), [in_], [out])
        return self.add(eng, lambda e: e.tensor_copy(a(out), a(in_)), [in_], [out])

    def acopy(self, out, in_):
        self.rr += 1
        return self.copy(out, in_, eng='act' if self.rr % 2 else 'dve')

    def memset(self, out, val, eng='dve'):
        a = self._a
        return self.add(eng, lambda e: e.memset(a(out), val), [], [out])

    def recip(self, out, in_):
        a = self._a
        return self.add('dve', lambda e: e.reciprocal(a(out), a(in_)), [in_], [out])

    def affine_select(self, out, in_, pattern, cmp, fill, base=0, cm=0):
        a = self._a
        return self.add('pool', lambda e: e.affine_select(a(out), a(in_), pattern, cmp, fill, base=base,
                                                          channel_multiplier=cm), [in_], [out])

    def dma(self, out, in_, q='sp'):
        a = self._a
        return self.add(q, lambda e: e.dma_start(a(out), a(in_)), [in_], [out], dma=True)

    def emit(self):
        nc = self.nc
        ops = self.ops
        eng_ops = {e: [] for e in ALLENG}
        for op in ops:
            op.idx = len(eng_ops[op.eng])
            eng_ops[op.eng].append(op)
        known = {e: {x: -1 for x in ALLENG} for e in ALLENG}
        known_dma = {e: set() for e in ALLENG}
        for op in ops:
            E = op.eng
            for d in op.deps:
                if d.dma:
                    if d.gid in known_dma[E]:
                        continue
                    known_dma[E].add(d.gid)
                    op.waits.append(d)
                else:
                    if d.eng == 'pe' and E == 'pe' and not op.dma:
                        continue
                    if known[E][d.eng] >= d.idx:
                        continue
                    known[E][d.eng] = d.idx
                    d.signal = True
                    op.waits.append(d)
        st = self.stack
        sems = {e: [st.enter_context(nc.semaphore(f"s_{e}{i}")) for i in range(NSEM_ENG)] for e in COMPUTE}
        dsems = [st.enter_context(nc.semaphore(f"s_dma{i}")) for i in range(NSEM_DMA)]
        swsems = [st.enter_context(nc.semaphore(f"s_swdma{i}")) for i in range(NSEM_SWDMA)]
        cnt = {e: 0 for e in COMPUTE}
        dcnt = 0
        swcnt = 0
        last_dma_val = {}
        for op in ops:
            if op.dma and op.eng == 'pool':
                op.signal = True
                k = swcnt
                swcnt += 1
                op.sem = swsems[k % NSEM_SWDMA]
                op.val = 16 * (k // NSEM_SWDMA + 1)
                last_dma_val[('sw', k % NSEM_SWDMA)] = (op.sem, op.val)
            elif op.dma:
                op.signal = True
                k = dcnt
                dcnt += 1
                op.sem = dsems[k % NSEM_DMA]
                op.val = 16 * (k // NSEM_DMA + 1)
                last_dma_val[('hw', k % NSEM_DMA)] = (op.sem, op.val)
            elif op.signal:
                k = cnt[op.eng]
                cnt[op.eng] += 1
                op.sem = sems[op.eng][k % NSEM_ENG]
                op.val = k // NSEM_ENG + 1
        block = st.enter_context(nc.Block())

        def body(ename):
            def f(e):
                for op in eng_ops[ename]:
                    for d in op.waits:
                        e.wait_ge(d.sem, d.val)
                    ins = op.fn(e)
                    if op.signal:
                        ins.then_inc(op.sem, 16 if op.dma else 1)
                if ename == 'sp':
                    for (sm, v) in last_dma_val.values():
                        e.wait_ge(sm, v)
            return f

        block.tensor(body('pe'))
        block.scalar(body('act'))
        block.vector(body('dve'))
        block.gpsimd(body('pool'))
        block.sync(body('sp'))
        stats = {e: len(eng_ops[e]) for e in ALLENG}
        stats['signals'] = dict(cnt)
        stats['dmas'] = dcnt
        return stats


class K:
    def __init__(self, S, depth=DEPTH, debug=()):
        self.S = S
        self.NT = S // 128
        self.NQ = S // 512
        self.NCH = S // 64
        self.depth = depth
        self.debug = set(debug)
        self.nc = bass.Bass("TRN2", target_bir_lowering=False)
        self.P = Prog(self.nc)

    def scratch(self, name, shape, dtype=F32):
        kind = "ExternalOutput" if name in self.debug else "Internal"
        return self.P.dram(name, shape, dtype, kind=kind)

    def build(self):
        P, nc, S = self.P, self.nc, self.S
        L = self.depth
        inp = lambda n, sh, dt=F32: P.dram(n, sh, dt, kind="ExternalInput")
        self.x_in = inp("x", [S, D])
        self.pos_in = inp("positions", [S], I32)
        self.colpack = inp("colpack", [DEPTH, 128, NCOL])
        self.invf = inp("invf", [64, 1])
        W = {}
        for n, sh in [("norm1_g", [DEPTH, D]), ("w_in", [DEPTH, D, N_IN]), ("ssd_dt_bias", [DEPTH, 16]),
                      ("ssd_a_log", [DEPTH, 16]), ("ssd_d", [DEPTH, 16]), ("ssd_norm_g", [DEPTH, D]),
                      ("mla_w_uq", [DEPTH, 512, 1536]), ("mla_w_ukv", [DEPTH, 256, 2048]),
                      ("gdn_dt_bias", [DEPTH, 8]), ("gdn_a_log", [DEPTH, 8]), ("gdn_norm_g", [DEPTH, 128]),
                      ("w_ssd_out", [DEPTH, D, D]), ("w_mla_out", [DEPTH, D, D]), ("w_gdn_out", [DEPTH, D, D]),
                      ("w_out", [DEPTH, D, D]), ("norm2_g", [DEPTH, D]), ("w_up", [DEPTH, D, 4 * D]),
                      ("w_down", [DEPTH, 4 * D, D]), ("final_norm_g", [D])]:
            W[n] = inp(n, sh)
        self.W = W
        self.out = P.dram("out", [S, D], F32, kind="ExternalOutput")
        sc = self.scratch
        self.xres = sc("xres", [S, D])
        self.hres = sc("hres", [S, D])
        self.zs2 = sc("zs2", [S, D])
        self.zg2 = sc("zg2", [S, D])
        self.xbcT = sc("xbcT", [1536, S])
        self.qkvT = sc("qkvT", [2048, S])
        self.dtk = sc("dtk", [S, 16])
        self.gbk = sc("gbk", [S, 16])
        self.cqT = sc("cqT", [512, S])
        self.ckvT = sc("ckvT", [256, S])
        self.krT = sc("krT", [128, S])
        self.gatesT = sc("gatesT", [3072, S])
        self.yT = sc("yT", [3, D, S], BF16)
        self.uT = sc("uT", [4 * D, S], BF16)
        self.hnT = sc("hnT", [D, S], BF16)
        self.cosT = sc("cosT", [64, S])
        self.sinT = sc("sinT", [64, S])

        with ExitStack() as st:
            P.stack = st
            C = {}
            C['maskLE'] = P.sbt("maskLE", [128, 128], F32)
            C['maskGT'] = P.sbt("maskGT", [128, 128], F32)
            C['maskLT'] = P.sbt("maskLT", [128, 128], F32)
            C['ones'] = P.sbt("onesf", [128, 128], F32)
            C['identf'] = P.sbt("identf", [128, 128], F32)
            C['ident'] = P.sbt("identb", [128, 128], BF16)
            C['onesb'] = P.sbt("onesb", [128, 128], BF16)
            C['maskLEb'] = P.sbt("maskLEb", [128, 128], BF16)
            C['nmaskLT'] = P.sbt("nmaskLT", [128, 128], F32)
            C['eps'] = P.sbt("epsc", [128, 1], F32)
            C['one'] = P.sbt("onec", [128, 1], F32)
            P.bar_t = P.sbt("bar_t", [128, 1], F32)
            self.C = C
            ARW = 46000
            arena_t = st.enter_context(nc.sbuf_tensor("arena", [128, ARW], F32))
            self.A = Arena(arena_t, ARW)
            self.banks = [P.pst(f"bank{i}", [128, 512], F32) for i in range(8)]
            for b_ in self.banks:
                b_.excl = True

            P.memset(C['ones'][:], 1.0)
            P.memset(C['onesb'][:], 1.0)
            P.memset(C['eps'][:], EPS)
            P.memset(C['one'][:], 1.0)
            P.memset(C['identf'][:], 0.0)
            P.affine_select(C['identf'][:], C['identf'][:], [[-1, 128]], ALU.not_equal, 1.0, base=0, cm=1)
            P.copy(C['ident'][:], C['identf'][:])
            P.affine_select(C['maskLE'][:], C['ones'][:], [[1, 128]], ALU.is_ge, 0.0, base=0, cm=-1)
            P.affine_select(C['maskLT'][:], C['ones'][:], [[1, 128]], ALU.is_ge, 0.0, base=-1, cm=-1)
            P.affine_select(C['maskGT'][:], C['ones'][:], [[-1, 128]], ALU.is_ge, 0.0, base=-1, cm=1)
            P.copy(C['maskLEb'][:], C['maskLE'][:])
            P.ts(C['nmaskLT'][:], C['maskLT'][:], -1.0, None, ALU.mult)

            self.rope_tables()
            P.barrier()
            for l in range(L):
                xsrc = self.x_in if l == 0 else self.xres
                self.phase1(l, xsrc)
                P.barrier()
                if 'stop1' in self.debug:
                    break
                if 'skipssd' not in self.debug:
                    self.phase_ssd(l)
                    P.barrier()
                if 'stop2' in self.debug:
                    break
                if 'skipmla' not in self.debug:
                    self.phase_mla(l)
                    P.barrier()
                if 'stop3' in self.debug:
                    break
                self.phase_gdn(l)
                P.barrier()
                if 'stop4' in self.debug:
                    break
                self.phase_merge(l, xsrc)
                P.barrier()
                self.phase_ffn_up(l)
                P.barrier()
                self.phase_ffn_down(l, last=(l == L - 1))
                P.barrier()
            self.stats = P.emit()
        return nc

    def rmsnorm_rstd(self, rstd, ssq, n):
        P = self.P
        p = ssq.shape[0]
        P.act(rstd, ssq, AF.Sqrt, bias=self.C['eps'][0:p, 0:1], scale=1.0 / n)
        P.recip(rstd, rstd)

    def load_w(self, dst, src):
        self.P.dma(dst, src, q='pool')

    def rope_tables(self):
        P, A, S = self.P, self.A, self.S
        A.reset()
        posi = A.alloc("posi", [64, S], I32)
        posf = A.alloc("posf", [64, S], F32)
        ang = A.alloc("ang", [64, S], F32)
        kk = A.alloc("kk", [64, S], F32)
        res = A.alloc("res", [64, S], F32)
        invc = A.alloc("invc", [64, 1], F32)
        P.dma(posi[:], self.pos_in[:].pbc(64))
        P.dma(invc[:], self.invf[:])
        P.copy(posf[:], posi[:])
        P.ts(ang[:], posf[:], invc[:, 0:1], None, ALU.mult)
        MAG = 12582912.0
        for name, shift, dst in (("sin", 0.0, self.sinT), ("cos", math.pi / 2, self.cosT)):
            a2 = ang
            if shift != 0.0:
                P.ts(posf[:], ang[:], shift, None, ALU.add)
                a2 = posf
            P.ts(kk[:], a2[:], 1.0 / (2 * math.pi), MAG, ALU.mult, ALU.add)
            P.ts(kk[:], kk[:], -MAG, None, ALU.add)
            P.stt(res[:], kk[:], -2 * math.pi, a2[:], ALU.mult, ALU.add)
            P.ts(res[:], res[:], math.pi, -math.pi, ALU.min, ALU.max)
            P.act(res[:], res[:], AF.Sin)
            P.dma(dst[:], res[:])

    def phase1(self, l, xsrc):
        P, A, S, C = self.P, self.A, self.S, self.C
        NT, NQ = self.NT, self.NQ
        A.reset()
        xnT = A.alloc("xnT", [128, 8, S], BF16)
        gB = A.alloc("gB", [128, D], F32)
        xt = [A.alloc(f"xt{i}", [128, D], F32) for i in range(2)]
        xn = [A.alloc(f"xn{i}", [128, D], BF16) for i in range(2)]
        junk = A.alloc("junk", [128, D], BF16)
        ssq = [A.alloc(f"ssq{i}", [128, 1], F32) for i in range(2)]
        rstd = [A.alloc(f"rstd{i}", [128, 1], F32) for i in range(2)]
        P.dma(gB[:], self.W['norm1_g'][l].pbc(128))
        bk = self.banks

        def load(t):
            P.dma(xt[t % 2][:], xsrc[t * 128:(t + 1) * 128, :])
        load(0)
        for t in range(NT):
            if t + 1 < NT:
                load(t + 1)
            b = t % 2
            P.act(junk[:], xt[b][:], AF.Square, accum_out=ssq[b][:])
            self.rmsnorm_rstd(rstd[b][:], ssq[b][:], D)
            P.stt(xn[b][:], xt[b][:], rstd[b][:, 0:1], gB[:], ALU.mult, ALU.mult)
            pb = bk[t % 2][:].bitcast(BF16)
            for j in range(8):
                P.transpose(pb[:, j * 128:(j + 1) * 128], xn[b][:, j * 128:(j + 1) * 128], C['ident'][:])
            P.acopy(xnT[:, :, t * 128:(t + 1) * 128], pb.re("p (j t) -> p j t", j=8))

        wsl = [A.alloc(f"wsl{i}", [128, 8, 512], BF16) for i in range(2)]
        stg = [A.alloc(f"stg{i}", [128, S], F32) for i in range(2)]
        stk = [A.alloc(f"stk{i}", [128, 512], F32) for i in range(3)]
        tmp = [A.alloc(f"tmp{i}", [128, 512], F32) for i in range(2)]
        win = self.W['w_in']
        wv = win.t[l].rearrange("(j p) c -> p j c", p=128)
        segs = [
            (0, 1024, 'silu2', 'tok', self.zs2, 0),
            (1024, 1536, 'copy', 'feat', self.xbcT, 0),
            (2560, 16, 'copy', 'tok', self.dtk, 0),
            (2576, 512, 'copy', 'feat', self.cqT, 0),
            (3088, 256, 'copy', 'feat', self.ckvT, 0),
            (3344, 64, 'rope', 'feat', self.krT, 0),
            (3408, 2048, 'copy', 'feat', self.qkvT, 0),
            (5456, 1024, 'silu2', 'tok', self.zg2, 0),
            (6480, 16, 'copy', 'tok', self.gbk, 0),
            (6496, 3072, 'sigmoid', 'feat', self.gatesT, 0),
        ]
        cnt = {'slab': 0, 'bank': 0, 'stg': 0, 'stk': 0, 'tmp': 0}

        def evac(dst, ps, kind):
            if kind == 'copy' or kind == 'rope':
                P.acopy(dst, ps)
            elif kind == 'silu2':
                tm = tmp[cnt['tmp'] % 2]
                cnt['tmp'] += 1
                w = ps.shape[1]
                P.act(tm[:, 0:w], ps, AF.Tanh, scale=0.5)
                P.stt(dst, tm[:, 0:w], 1.0, ps, ALU.add, ALU.mult)
            elif kind == 'sigmoid':
                P.act(dst, ps, AF.Tanh, scale=0.5)
                P.ts(dst, dst, 0.5, 0.5, ALU.mult, ALU.add, eng='pool')

        for (c0, ncols, kind, layout, dst, _) in segs:
            for s0 in range(0, ncols, 512):
                w = min(512, ncols - s0)
                sl = wsl[cnt['slab'] % 2]
                cnt['slab'] += 1
                self.load_w(sl[:, :, 0:w], V(None, wv[:, :, c0 + s0:c0 + s0 + w]))
                wuse = w
                if kind == 'rope':
                    P.ts(sl[:, :, 64:96], sl[:, :, 32:64], -1.0, None, ALU.mult)
                    P.copy(sl[:, :, 96:128], sl[:, :, 0:32])
                    wuse = 128
                if layout == 'feat':
                    for b0 in range(0, wuse, 128):
                        nb = min(128, wuse - b0)
                        sg = stg[cnt['stg'] % 2]
                        cnt['stg'] += 1
                        for q in range(NQ):
                            ps = bk[cnt['bank'] % 4]
                            cnt['bank'] += 1
                            for k in range(8):
                                P.matmul(ps[0:nb, :], sl[:, k, b0:b0 + nb], xnT[:, k, q * 512:(q + 1) * 512],
                                         start=(k == 0), stop=(k == 7))
                            evac(sg[0:nb, q * 512:(q + 1) * 512], ps[0:nb, :], kind)
                        r0 = c0 - segs_base(c0, segs) + s0 + b0
                        P.dma(dst[r0:r0 + nb, :], sg[0:nb, :], q='act')
                else:
                    for t in range(NT):
                        ps = bk[cnt['bank'] % 4]
                        cnt['bank'] += 1
                        for k in range(8):
                            P.matmul(ps[:, 0:w], xnT[:, k, t * 128:(t + 1) * 128], sl[:, k, 0:w],
                                     start=(k == 0), stop=(k == 7))
                        sk = stk[cnt['stk'] % 3]
                        cnt['stk'] += 1
                        evac(sk[:, 0:w], ps[:, 0:w], kind)
                        P.dma(dst[t * 128:(t + 1) * 128, s0:s0 + w], sk[:, 0:w], q='act')

    def conv_block(self, srcT, row0, wcols, bcol, dst_fn, bufs, S):
        P = self.P
        CW = min(S, 1024)
        raw, acc, th = bufs
        nseg = S // CW
        for sgi in range(nseg):
            t0 = sgi * CW
            r = raw[sgi % 2]
            if t0 == 0:
                P.memset(r[:, 0:3], 0.0, eng='pool')
                P.dma(r[:, 3:3 + CW], srcT[row0:row0 + 128, t0:t0 + CW])
            else:
                P.dma(r[:, 0:3 + CW], srcT[row0:row0 + 128, t0 - 3:t0 + CW])
            a = acc[sgi % 2]
            if bcol is not None:
                P.ts(a[:], r[:, 3:3 + CW], wcols[:, 3:4], bcol, ALU.mult, ALU.add)
            else:
                P.ts(a[:], r[:, 3:3 + CW], wcols[:, 3:4], None, ALU.mult)
            P.stt(a[:], r[:, 2:2 + CW], wcols[:, 2:3], a[:], ALU.mult, ALU.add)
            P.stt(a[:], r[:, 1:1 + CW], wcols[:, 1:2], a[:], ALU.mult, ALU.add)
            P.stt(a[:], r[:, 0:CW], wcols[:, 0:1], a[:], ALU.mult, ALU.add)
            t = th[sgi % 2]
            P.act(t[:], a[:], AF.Tanh, scale=0.5)
            P.stt(a[:], t[:], 1.0, a[:], ALU.add, ALU.mult)
            P.add('act', (lambda o, i: (lambda e: e.mul(o, i, 0.5)))(dst_fn(t0, CW).ap, a[:].ap), [a[:]], [dst_fn(t0, CW)])

    def softplus(self, out, x, tmp1, tmp2):
        P = self.P
        p = x.shape[0]
        P.act(tmp1, x, AF.Abs)
        P.act(tmp1, tmp1, AF.Exp, scale=-1.0)
        P.act(tmp1, tmp1, AF.Ln, bias=self.C['one'][0:p, 0:1], scale=1.0)
        P.stt(out, x, 0.0, tmp1, ALU.max, ALU.add)

    def phase_ssd(self, l):
        P, A, S, C = self.P, self.A, self.S, self.C
        NT = self.NT
        A.reset()
        bk = self.banks
        xcT = A.alloc("xcT", [128, 12, S], BF16)
        cp = A.alloc("cp", [128, NCOL], F32)
        P.dma(cp[:], self.colpack[l])
        CW = min(S, 1024)
        mark = A.off
        raw = [A.alloc(f"raw{i}", [128, CW + 3], F32) for i in range(2)]
        acc = [A.alloc(f"acc{i}", [128, CW], F32) for i in range(2)]
        th = [A.alloc(f"th{i}", [128, CW], F32) for i in range(2)]
        for j in range(12):
            self.conv_block(self.xbcT, j * 128, cp[:, j * 4:(j + 1) * 4], cp[:, 48 + j:49 + j],
                            (lambda jj: (lambda t0, cw: xcT[:, jj, t0:t0 + cw]))(j), (raw, acc, th), S)
        P.barrier()
        A.off = mark
        dtb = A.alloc("dtb", [128, 16], F32)
        alog = A.alloc("alog", [128, 16], F32)
        aB = A.alloc("aB", [128, 16], F32)
        dB = A.alloc("dB", [128, 16], F32)
        ngB = A.alloc("ngB", [128, D], F32)
        P.dma(dtb[:], self.W['ssd_dt_bias'][l].pbc(128))
        P.dma(alog[:], self.W['ssd_a_log'][l].pbc(128))
        P.dma(dB[:], self.W['ssd_d'][l].pbc(128))
        P.dma(ngB[:], self.W['ssd_norm_g'][l].pbc(128))
        P.act(aB[:], alog[:], AF.Exp)
        P.ts(aB[:], aB[:], -1.0, None, ALU.mult)
        dt_all = A.alloc("dt_all", [128, NT, 16], F32)
        dA_all = A.alloc("dA_all", [128, NT, 16], F32)
        t1 = A.alloc("t1", [128, NT, 16], F32)
        P.dma(dt_all[:], self.dtk.t[:].rearrange("(n p) h -> p n h", p=128) if False else V(None, self.dtk.t.ap().rearrange("(n p) h -> p n h", p=128)))
        P.tt(dt_all[:], dt_all[:], dtb[:].unsq(1).bc([128, NT, 16]), ALU.add)
        self.softplus(dt_all[:], dt_all[:], t1[:], None)
        P.tt(dA_all[:], dt_all[:], aB[:].unsq(1).bc([128, NT, 16]), ALU.mult)

        hs = A.alloc("hs", [128, D], F32)
        hsb = A.alloc("hsb", [128, D], BF16)
        P.memset(hs[:], 0.0)
        P.memset(hsb[:], 0.0, eng='pool')
        Xsb = A.alloc("Xsb", [128, D], BF16)
        Btok = A.alloc("Btok", [128, 256], BF16)
        ex = A.alloc("ex", [128, 48], F32)
        CBm = A.alloc("CBm", [128, 2, 128], BF16)
        rhsD = A.alloc("rhsD", [128, 16, 128], F32)
        LT = A.alloc("LT", [128, 8, 128], BF16)
        MT = A.alloc("MT", [128, 16, 128], BF16)
        Xdt = A.alloc("Xdt", [128, D], BF16)
        Xds = A.alloc("Xds", [128, D], BF16)
        y1 = A.alloc("y1", [128, D], F32)
        t2 = A.alloc("t2", [128, D], F32)
        yz = A.alloc("yz", [128, D], F32)
        jk = A.alloc("jk", [128, 512], BF16)
        ssq = A.alloc("ssq", [128, 2], F32)
        rstd = A.alloc("rstd", [128, 2], F32)
        yn = A.alloc("yn", [128, D], BF16)
        zt = [A.alloc(f"zt{i}", [128, D], F32) for i in range(2)]
        ystg = [A.alloc(f"ystg{i}", [128, 8, 128], BF16) for i in range(2)]
        mLE, mGT, ones = C['maskLE'], C['maskGT'], C['ones']
        yTd = self.yT.t[0].rearrange("(j p) t -> p j t", p=128)

        def loadz(c):
            P.dma(zt[c % 2][:], self.zs2[c * 128:(c + 1) * 128, :])
        loadz(0)
        for c in range(NT):
            if c + 1 < NT:
                loadz(c + 1)
            cs = slice(c * 128, (c + 1) * 128)
            dA = dA_all[:, c, :]
            dt = dt_all[:, c, :]
            pb0 = bk[0][:].bitcast(BF16)
            for j in range(8):
                P.transpose(pb0[:, j * 128:(j + 1) * 128], xcT[:, j, cs], C['ident'][:])
            P.copy(Xsb[:], pb0, eng='act')
            pb1 = bk[1][:].bitcast(BF16)
            for j in range(2):
                P.transpose(pb1[:, j * 128:(j + 1) * 128], xcT[:, 8 + j, cs], C['ident'][:])
            P.copy(Btok[:], pb1[:, 0:256])
            P.matmul(bk[2][:, 0:16], mLE[:], dA)
            P.matmul(bk[2][:, 16:32], mGT[:], dA)
            P.matmul(bk[2][:, 32:48], ones[:], dA)
            P.act(ex[:], bk[2][:, 0:48], AF.Exp)
            eA, ds, cd = ex[:, 0:16], ex[:, 16:32], ex[:, 32:48]
            for g in range(2):
                P.matmul(bk[2][:, 64 + g * 128:64 + (g + 1) * 128], xcT[:, 8 + g, cs], xcT[:, 10 + g, cs])
            P.tt(CBm[:], bk[2][:, 64:320].re("p (g l) -> p g l", g=2), mLE[:].unsq(1).bc([128, 2, 128]), ALU.mult)
            P.tt(rhsD[:], mLE[:].unsq(1).bc([128, 16, 128]), dA.unsq(2).bc([128, 16, 128]), ALU.mult, eng='pool')
            for g in range(2):
                for i in range(2):
                    P.matmul(bk[3 + i][:], mGT[:], rhsD[:, g * 8 + i * 4:g * 8 + i * 4 + 4, :].re("p h l -> p (h l)"))
                    P.act(LT[:, i * 4:(i + 1) * 4, :].re("p h l -> p (h l)"), bk[3 + i][:], AF.Exp)
                P.tt(MT[:, g * 8:(g + 1) * 8, :], LT[:], CBm[:, g:g + 1, :].bc([128, 8, 128]), ALU.mult)
            P.tt(Xdt[:].re("p (h q) -> p h q", h=16), Xsb[:].re("p (h q) -> p h q", h=16),
                 dt.unsq(2).bc([128, 16, 64]), ALU.mult)
            P.tt(Xds[:].re("p (h q) -> p h q", h=16), Xdt[:].re("p (h q) -> p h q", h=16),
                 ds.unsq(2).bc([128, 16, 64]), ALU.mult, eng='pool')
            for h in range(16):
                P.matmul(bk[5 + h // 8][:, (h % 8) * 64:(h % 8 + 1) * 64], MT[:, h, :], Xdt[:, h * 64:(h + 1) * 64])
            for g in range(2):
                P.matmul(bk[3 + g][:], xcT[:, 10 + g, cs], hsb[:, g * 512:(g + 1) * 512])
            sbank = (bk[7], bk[1])
            for g in range(2):
                P.matmul(sbank[g][:], Btok[:, g * 128:(g + 1) * 128], Xds[:, g * 512:(g + 1) * 512])
            for g in range(2):
                gs = slice(g * 512, (g + 1) * 512)
                P.tt(y1[:, gs].re("p (h q) -> p h q", h=8), bk[3 + g][:].re("p (h q) -> p h q", h=8),
                     eA[:, g * 8:(g + 1) * 8].unsq(2).bc([128, 8, 64]), ALU.mult)
                P.tt(y1[:, gs], y1[:, gs], bk[5 + g][:], ALU.add)
            P.tt(t2[:].re("p (h q) -> p h q", h=16), Xsb[:].re("p (h q) -> p h q", h=16),
                 dB[:].unsq(2).bc([128, 16, 64]), ALU.mult, eng='pool')
            P.tt(y1[:], y1[:], t2[:], ALU.add)
            P.stt(yz[:], y1[:], 0.5, zt[c % 2][:], ALU.mult, ALU.mult)
            for g in range(2):
                gs = slice(g * 512, (g + 1) * 512)
                P.act(jk[:], yz[:, gs], AF.Square, accum_out=ssq[:, g:g + 1])
            self.rmsnorm_rstd(rstd[:], ssq[:], 512)
            for g in range(2):
                gs = slice(g * 512, (g + 1) * 512)
                P.stt(yn[:, gs], yz[:, gs], rstd[:, g:g + 1], ngB[:, gs], ALU.mult, ALU.mult)
            pb0 = bk[0][:].bitcast(BF16)
            for j in range(8):
                P.transpose(pb0[:, j * 128:(j + 1) * 128], yn[:, j * 128:(j + 1) * 128], C['ident'][:])
            ys = ystg[c % 2]
            P.copy(ys[:], pb0.re("p (j t) -> p j t", j=8), eng='act')
            P.dma(V(None, yTd[:, :, cs]), ys[:], q='act')
            for g in range(2):
                gs = slice(g * 512, (g + 1) * 512)
                P.tt(hs[:, gs].re("p (h q) -> p h q", h=8), hs[:, gs].re("p (h q) -> p h q", h=8),
                     cd[:, g * 8:(g + 1) * 8].unsq(2).bc([128, 8, 64]), ALU.mult)
                P.tt(hs[:, gs], hs[:, gs], sbank[g][:], ALU.add)
            P.copy(hsb[:], hs[:], eng='pool')

    def phase_mla(self, l):
        P, A, S, C = self.P, self.A, self.S, self.C
        NT, NQ = self.NT, self.NQ
        A.reset()
        bk = self.banks
        cp = A.alloc("cp", [128, NCOL], F32)
        P.dma(cp[:], self.colpack[l])
        cqn = A.alloc("cqn", [128, 4, S], BF16)
        ckvn = A.alloc("ckvn", [128, 2, S], BF16)
        kpe = A.alloc("kpe", [64, S], BF16)
        wuq = A.alloc("wuq", [128, 4, 1536], BF16)
        wukv = A.alloc("wukv", [128, 2, 2048], BF16)
        wrot = A.alloc("wrot", [128, 4, 512], BF16)
        self.load_w(wuq[:], V(None, self.W['mla_w_uq'].t[l].rearrange("(j p) c -> p j c", p=128)))
        self.load_w(wukv[:], V(None, self.W['mla_w_ukv'].t[l].rearrange("(j p) c -> p j c", p=128)))
        for h in range(8):
            P.ts(wrot[:, :, h * 64:h * 64 + 32], wuq[:, :, h * 192 + 160:h * 192 + 192], -1.0, None, ALU.mult)
            P.copy(wrot[:, :, h * 64 + 32:h * 64 + 64], wuq[:, :, h * 192 + 128:h * 192 + 160], eng='pool')
        mark = A.off
        cin = [A.alloc(f"cin{i}", [128, 4, 512], F32) for i in range(2)]
        kin = [A.alloc(f"kin{i}", [128, 2, 512], F32) for i in range(2)]
        rin = [A.alloc(f"rin{i}", [64, 2, 512], F32) for i in range(2)]
        cs_ = [A.alloc(f"cs{i}", [64, 2, 512], F32) for i in range(2)]
        sq = A.alloc("sq", [128, 4, 512], BF16)
        rs = A.alloc("rs", [128, 2, 512], F32)
        tr = A.alloc("tr", [64, 2, 512], F32)
        cqv = self.cqT.t.ap().rearrange("(j p) t -> p j t", p=128)
        ckv = self.ckvT.t.ap().rearrange("(j p) t -> p j t", p=128)
        krv = self.krT.t.ap().rearrange("(j p) t -> p j t", p=64)

        def load3a(q):
            qs = slice(q * 512, (q + 1) * 512)
            P.dma(cin[q % 2][:], V(None, cqv[:, :, qs]))
            P.dma(kin[q % 2][:], V(None, ckv[:, :, qs]))
            P.dma(rin[q % 2][:], V(None, krv[:, :, qs]))
            P.dma(cs_[q % 2][:, 0, :], self.cosT[:, qs])
            P.dma(cs_[q % 2][:, 1, :], self.sinT[:, qs])
        load3a(0)
        for q in range(NQ):
            if q + 1 < NQ:
                load3a(q + 1)
            qs = slice(q * 512, (q + 1) * 512)
            ci, ki, ri, cs2 = cin[q % 2], kin[q % 2], rin[q % 2], cs_[q % 2]
            P.act(sq[:], ci[:], AF.Square)
            for j in range(4):
                P.matmul(bk[0][:], C['onesb'][:], sq[:, j, :], start=(j == 0), stop=(j == 3))
            P.act(sq[:, 0:2, :], ki[:], AF.Square)
            for j in range(2):
                P.matmul(bk[1][:], C['onesb'][:], sq[:, j, :], start=(j == 0), stop=(j == 1))
            P.act(rs[:, 0, :], bk[0][:], AF.Sqrt, bias=C['eps'][:, 0:1], scale=1.0 / 512)
            P.act(rs[:, 1, :], bk[1][:], AF.Sqrt, bias=C['eps'][:, 0:1], scale=1.0 / 256)
            P.recip(rs[:], rs[:])
            for j in range(4):
                P.stt(cqn[:, j, qs], ci[:, j, :], cp[:, 124 + j:125 + j], rs[:, 0, :], ALU.mult, ALU.mult)
            for j in range(2):
                P.stt(ckvn[:, j, qs], ki[:, j, :], cp[:, 128 + j:129 + j], rs[:, 1, :], ALU.mult, ALU.mult)
            P.tt(tr[:], ri[:], cs2[:], ALU.mult)
            P.tt(kpe[:, qs], tr[:, 0, :], tr[:, 1, :], ALU.add)
        P.barrier()
        A.off = mark
        KT = [A.alloc(f"KT{i}", [128, S], BF16) for i in range(2)]
        Vh = [A.alloc(f"Vh{i}", [128, NT, 128], BF16) for i in range(2)]
        QT = [A.alloc(f"QT{i}", [128, S], BF16) for i in range(2)]
        qpe = [A.alloc(f"qpe{i}", [64, S], BF16) for i in range(2)]
        csq = [A.alloc(f"csq{i}", [64, 2, 512], F32) for i in range(2)]
        tq = A.alloc("tq", [64, 2, 512], F32)
        pT = [A.alloc(f"pT{i}", [128, 512], BF16) for i in range(3)]
        rden = A.alloc("rden", [128, 512], F32)
        yst = [A.alloc(f"yst{i}", [128, 512], BF16) for i in range(2)]
        scale = 192.0 ** -0.5
        cnt = {'p': 0, 'y': 0, 'cs': 0}
        for h in range(8):
            hb = h % 2
            for q in range(NQ):
                qs = slice(q * 512, (q + 1) * 512)
                ps = bk[6 + q % 2]
                for j in range(2):
                    P.matmul(ps[:], wukv[:, j, h * 256:h * 256 + 128], ckvn[:, j, qs], start=(j == 0), stop=(j == 1))
                P.acopy(KT[hb][:, qs], ps[:])
                ps = bk[6 + (q + 1) % 2]
                for j in range(4):
                    P.matmul(ps[:], wuq[:, j, h * 192:h * 192 + 128], cqn[:, j, qs], start=(j == 0), stop=(j == 3))
                P.acopy(QT[hb][:, qs], ps[:])
                cq2 = csq[cnt['cs'] % 2]
                cnt['cs'] += 1
                P.dma(cq2[:, 0, :], self.cosT[:, qs])
                P.dma(cq2[:, 1, :], self.sinT[:, qs])
                ps = bk[6 + q % 2]
                for j in range(4):
                    P.matmul(ps[0:64, 0:512], wuq[:, j, h * 192 + 128:h * 192 + 192], cqn[:, j, qs],
                             start=(j == 0), stop=(j == 3))
                P.tt(tq[:, 0, :], ps[0:64, :], cq2[:, 0, :], ALU.mult)
                ps = bk[6 + (q + 1) % 2]
                for j in range(4):
                    P.matmul(ps[0:64, 0:512], wrot[:, j, h * 64:(h + 1) * 64], cqn[:, j, qs],
                             start=(j == 0), stop=(j == 3))
                P.tt(tq[:, 1, :], ps[0:64, :], cq2[:, 1, :], ALU.mult)
                P.tt(qpe[hb][:, qs], tq[:, 0, :], tq[:, 1, :], ALU.add)
            for t4 in range(0, NT, 4):
                ps = bk[6 + (t4 // 4) % 2]
                for tt_ in range(4):
                    t = t4 + tt_
                    for j in range(2):
                        P.matmul(ps[:, tt_ * 128:(tt_ + 1) * 128], ckvn[:, j, t * 128:(t + 1) * 128],
                                 wukv[:, j, h * 256 + 128:h * 256 + 256], start=(j == 0), stop=(j == 1))
                P.acopy(Vh[hb][:, t4:t4 + 4, :], ps[:].re("p (a v) -> p a v", a=4))
            for c in range(NQ):
                qs = slice(c * 512, (c + 1) * 512)
                ob = bk[2 + c % 2]
                db = bk[4 + c % 2]
                nkb = 4 * c + 4
                for kb in range(nkb):
                    j = kb - 4 * c
                    lo = 0 if j <= 0 else 128 * j
                    ks = slice(kb * 128, (kb + 1) * 128)
                    sb_ = bk[kb % 2]
                    P.matmul(sb_[:, lo:512], KT[hb][:, ks], QT[hb][:, c * 512 + lo:(c + 1) * 512], start=True, stop=False)
                    P.matmul(sb_[:, lo:512], kpe[:, ks], qpe[hb][:, c * 512 + lo:(c + 1) * 512], start=False, stop=True)
                    pt = pT[cnt['p'] % 3]
                    cnt['p'] += 1
                    P.act(pt[:, lo:512], sb_[:, lo:512], AF.Exp, scale=scale)
                    if j >= 0:
                        P.tt(pt[:, lo:lo + 128], pt[:, lo:lo + 128], C['maskLEb'][:], ALU.mult, eng='pool')
                    P.matmul(ob[:, lo:512], Vh[hb][:, kb, :], pt[:, lo:512], start=(kb == 0), stop=(kb == nkb - 1))
                    P.matmul(db[:, lo:512], C['onesb'][:], pt[:, lo:512], start=(kb == 0), stop=(kb == nkb - 1))
                P.recip(rden[:], db[:])
                ys = yst[cnt['y'] % 2]
                cnt['y'] += 1
                P.tt(ys[:], ob[:], rden[:], ALU.mult)
                P.dma(self.yT[1, h * 128:(h + 1) * 128, qs], ys[:], q='act')

    def phase_gdn(self, l):
        P, A, S, C = self.P, self.A, self.S, self.C
        NCH = self.NCH
        A.reset()
        bk = self.banks
        cp = A.alloc("cp", [128, NCOL], F32)
        P.dma(cp[:], self.colpack[l])
        gb = A.alloc("gb", [64, NCH, 16], F32)
        P.dma(gb[:], V(None, self.gbk.t.ap().rearrange("(n p) c -> p n c", p=64)))
        beta_all = A.alloc("beta_all", [64, NCH, 8], F32)
        g_all = A.alloc("g_all", [64, NCH, 8], F32)
        tg = A.alloc("tg", [64, NCH, 8], F32)
        dtb = A.alloc("gdtb", [64, 8], F32)
        alog = A.alloc("galog", [64, 8], F32)
        aB = A.alloc("gaB", [64, 8], F32)
        ngB = A.alloc("gngB", [64, 128], F32)
        P.dma(dtb[:], self.W['gdn_dt_bias'][l].pbc(64))
        P.dma(alog[:], self.W['gdn_a_log'][l].pbc(64))
        P.dma(ngB[:], self.W['gdn_norm_g'][l].pbc(64))
        P.act(aB[:], alog[:], AF.Exp)
        P.ts(aB[:], aB[:], -1.0, None, ALU.mult)
        P.act(beta_all[:], gb[:, :, 0:8], AF.Tanh, scale=0.5)
        P.ts(beta_all[:], beta_all[:], 0.5, 0.5, ALU.mult, ALU.add)
        P.tt(g_all[:], gb[:, :, 8:16], dtb[:].unsq(1).bc([64, NCH, 8]), ALU.add)
        self.softplus(g_all[:], g_all[:], tg[:], None)
        P.tt(g_all[:], g_all[:], aB[:].unsq(1).bc([64, NCH, 8]), ALU.mult)
        g16 = A.alloc("g16", [64, NCH, 16], F32)
        P.copy(g16[:, :, 0:8], g_all[:])
        P.copy(g16[:, :, 8:16], g_all[:])
        mark0 = A.off
        mLE, mGT, nmLT, ones, identf = C['maskLE'], C['maskGT'], C['nmaskLT'], C['ones'], C['identf']
        CW = min(S, 1024)
        dbg = self.debug
        if 'g0' in dbg:
            return
        NQ = self.NQ

        def alloc_work():
            w = {}
            w['Sf'] = A.alloc("Sf", [128, 2, 128], F32)
            w['Sb'] = A.alloc("Sb", [128, 2, 128], BF16)
            w['gm'] = A.alloc("gm", [64, 2, 64], F32)
            w['etot'] = A.alloc("etot", [128, 16], F32)
            w['erem'] = A.alloc("erem", [64, 16], F32)
            w['egR'] = A.alloc("egR", [128, 2, 64], BF16)
            w['decT'] = A.alloc("decT", [64, 2, 64], F32)
            w['dm'] = A.alloc("dm", [64, 2, 64], F32)
            w['dm2'] = A.alloc("dm2", [64, 2, 64], F32)
            w['PT0f'] = A.alloc("PT0f", [64, 2, 64], F32)
            w['PT'] = [A.alloc(f"PT{k}", [64, 2, 64], BF16) for k in range(2)]
            w['Pm'] = [A.alloc(f"Pm{k}", [64, 2, 64], BF16) for k in range(2)]
            w['TT'] = A.alloc("TT", [64, 2, 64], F32)
            w['TTb'] = A.alloc("TTb", [64, 2, 64], BF16)
            w['QKd'] = A.alloc("QKd", [64, 2, 64], BF16)
            w['kgT'] = A.alloc("kgT", [128, 2, 64], BF16)
            w['qgT'] = A.alloc("qgT", [128, 2, 64], BF16)
            w['kd'] = A.alloc("kd", [64, 2, 128], BF16)
            w['vtok'] = A.alloc("vtok", [64, 2, 128], F32)
            w['Xs'] = A.alloc("Xs", [64, 2, 128], BF16)
            w['vnew'] = A.alloc("vnew", [64, 2, 128], BF16)
            w['zt'] = [A.alloc(f"gzt{k}", [64, 256], F32) for k in range(2)]
            w['jk'] = A.alloc("gjk", [64, 128], BF16)
            w['ssq'] = A.alloc("gssq", [64, 2], F32)
            w['rstd'] = A.alloc("grstd", [64, 2], F32)
            w['on'] = A.alloc("on", [64, 2, 128], F32)
            w['otok'] = A.alloc("otok", [64, 2, 128], BF16)
            return w

        def chunks_gen(i, pers, w, rot):
            qn, kn, vT, ostg = pers
            bkr = [bk[(j + rot) % 8] for j in range(8)]
            Sf, Sb, gm, etot, erem, egR, decT, dm, dm2 = (w[k_] for k_ in ('Sf', 'Sb', 'gm', 'etot', 'erem', 'egR', 'decT', 'dm', 'dm2'))
            PT0f, PT, Pm, TT, TTb, QKd, kgT, qgT, kd = (w[k_] for k_ in ('PT0f', 'PT', 'Pm', 'TT', 'TTb', 'QKd', 'kgT', 'qgT', 'kd'))
            vtok, Xs, vnew, zt, jk, ssq, rstd, on, otok = (w[k_] for k_ in ('vtok', 'Xs', 'vnew', 'zt', 'jk', 'ssq', 'rstd', 'on', 'otok'))
            P.memset(Sf[:], 0.0)
            P.memset(Sb[:], 0.0, eng='pool')

            def loadz(n):
                P.dma(zt[n % 2][:], self.zg2[n * 64:(n + 1) * 64, (2 * i) * 128:(2 * i + 2) * 128])
            loadz(0)
            for n in range(NCH):
                if n + 1 < NCH:
                    loadz(n + 1)
                cs = slice(n * 64, (n + 1) * 64)
                g2 = g_all[:, n, 2 * i:2 * i + 2]
                b2 = beta_all[:, n, 2 * i:2 * i + 2]
                Ba = bkr[0]
                P.matmul(Ba[0:64, 0:128], kn[:, cs], C['ident'][:])
                for e in range(2):
                    P.matmul(Ba[0:64, 128 + e * 128:256 + e * 128], vT[:, e, cs], C['ident'][:])
                P.tt(gm[:], mLE[0:64, 0:64].unsq(1).bc([64, 2, 64]), g2.unsq(2).bc([64, 2, 64]), ALU.mult, eng='pool')
                yield
                gmf = gm[:].re("p e l -> p (e l)")
                Bb = bkr[1]
                gq = g16[:, n, :]
                P.matmul(Bb[:, 0:16], ones[0:64, :], gq)
                P.matmul(Bb[0:64, 16:32], mGT[0:64, 0:64], gq)
                P.matmul(Bb[:, 128:256], ones[0:64, :], gmf)
                P.matmul(Bb[0:64, 256:384], mGT[0:64, 0:64], gmf)
                Bc = bkr[2]
                P.matmul(Bc[0:64, 0:64], kn[:, cs], kn[:, cs])
                P.matmul(Bc[0:64, 64:128], kn[:, cs], qn[:, cs])
                yield
                P.act(etot[:], Bb[:, 0:16], AF.Exp)
                P.act(erem[:], Bb[0:64, 16:32], AF.Exp)
                P.act(egR[:].re("p e l -> p (e l)"), Bb[:, 128:256], AF.Exp)
                P.act(decT[:].re("p e l -> p (e l)"), Bb[0:64, 256:384], AF.Exp)
                yield
                P.tt(dm[:], decT[:], nmLT[0:64, 0:64].unsq(1).bc([64, 2, 64]), ALU.mult, eng='pool')
                P.tt(dm2[:], decT[:], mLE[0:64, 0:64].unsq(1).bc([64, 2, 64]), ALU.mult, eng='pool')
                yield
                P.tt(dm[:], dm[:], Bc[0:64, 0:64].unsq(1).bc([64, 2, 64]), ALU.mult)
                for e in range(2):
                    P.ts(PT0f[:, e, :], dm[:, e, :], b2[:, e:e + 1], None, ALU.mult)
                P.tt(QKd[:], dm2[:], Bc[0:64, 64:128].unsq(1).bc([64, 2, 64]), ALU.mult)
                P.tt(TT[:], PT0f[:], identf[0:64, 0:64].unsq(1).bc([64, 2, 64]), ALU.add)
                yield
                P.copy(PT[0][:], PT0f[:], eng='pool')
                P.copy(TTb[:], TT[:], eng='pool')
                yield
                Bd = bkr[3]
                for e in range(2):
                    P.matmul(Bd[0:64, e * 64:(e + 1) * 64], PT[0][:, e, :], C['ident'][0:64, 0:64])
                yield
                P.copy(Pm[0][:].re("p e l -> p (e l)"), Bd[0:64, 0:128], eng='act')
                yield
                cur = 0
                for it in range(5):
                    nxt = 1 - cur
                    for e in range(2):
                        P.matmul(Bd[0:64, e * 64:(e + 1) * 64], PT[cur][:, e, :], Pm[cur][:, e, :])
                        if it < 4:
                            P.matmul(Bd[0:64, 128 + e * 64:128 + (e + 1) * 64], Pm[cur][:, e, :], PT[cur][:, e, :])
                    yield
                    P.copy(Pm[nxt][:].re("p e l -> p (e l)"), Bd[0:64, 0:128], eng='act')
                    if it < 4:
                        P.copy(PT[nxt][:].re("p e l -> p (e l)"), Bd[0:64, 128:256], eng='act')
                    yield
                    for e in range(2):
                        P.matmul(Bd[0:64, 256 + e * 64:256 + (e + 1) * 64], Pm[nxt][:, e, :], TTb[:, e, :])
                    yield
                    P.tt(TT[:].re("p e l -> p (e l)"), TT[:].re("p e l -> p (e l)"), Bd[0:64, 256:384], ALU.add)
                    yield
                    P.copy(TTb[:], TT[:], eng='pool')
                    yield
                    cur = nxt
                for e in range(2):
                    P.tt(kgT[:, e, :], kn[:, cs], egR[:, e, :], ALU.mult, eng='pool')
                    P.tt(qgT[:, e, :], qn[:, cs], egR[:, e, :], ALU.mult, eng='pool')
                    P.ts(kd[:, e, :], Ba[0:64, 0:128], erem[:, 2 * i + e:2 * i + e + 1], None, ALU.mult)
                P.copy(vtok[:].re("p e v -> p (e v)"), Ba[0:64, 128:384])
                yield
                Be, Bg = bkr[4], bkr[6]
                Bf = (bkr[5], bkr[7])
                for e in range(2):
                    P.matmul(Be[0:64, e * 128:(e + 1) * 128], kgT[:, e, :], Sb[:, e, :])
                    P.matmul(Bf[e][0:64, 0:128], qgT[:, e, :], Sb[:, e, :], start=True, stop=False)
                yield
                P.tt(Xs[:].re("p e v -> p (e v)"), vtok[:].re("p e v -> p (e v)"), Be[0:64, 0:256], ALU.subtract)
                yield
                for e in range(2):
                    P.matmul(Be[0:64, 256 + e * 128:256 + (e + 1) * 128], TTb[:, e, :], Xs[:, e, :])
                yield
                for e in range(2):
                    P.ts(vnew[:, e, :], Be[0:64, 256 + e * 128:256 + (e + 1) * 128], b2[:, e:e + 1], None, ALU.mult)
                yield
                for e in range(2):
                    P.matmul(Bf[e][0:64, 0:128], QKd[:, e, :], vnew[:, e, :], start=False, stop=True)
                    P.matmul(Bg[:, e * 128:(e + 1) * 128], kd[:, e, :], vnew[:, e, :])
                yield
                for e in range(2):
                    P.ts(Sf[:, e, :], Sf[:, e, :], etot[:, 2 * i + e:2 * i + e + 1], None, ALU.mult)
                P.tt(Sf[:].re("p e v -> p (e v)"), Sf[:].re("p e v -> p (e v)"), Bg[:, 0:256], ALU.add)
                yield
                P.copy(Sb[:], Sf[:], eng='pool')
                for e in range(2):
                    P.act(jk[:], Bf[e][0:64, 0:128], AF.Square, accum_out=ssq[:, e:e + 1])
                self.rmsnorm_rstd(rstd[:], ssq[:], 128)
                yield
                for e in range(2):
                    P.stt(on[:, e, :], Bf[e][0:64, 0:128], rstd[:, e:e + 1], ngB[:], ALU.mult, ALU.mult)
                P.stt(otok[:].re("p e v -> p (e v)"), on[:].re("p e v -> p (e v)"), 0.5, zt[n % 2][:], ALU.mult, ALU.mult)
                yield
                Bh = bkr[6]
                for e in range(2):
                    P.matmul(Bh[:, 384 + e * 64:384 + (e + 1) * 64], otok[:, e, :], C['ident'][0:64, 0:64])
                yield
                P.copy(ostg[:, :, cs], Bh[:, 384:512].re("p (e t) -> p e t", e=2), eng='act')
                yield

        for grp in ((0, 1), (2, 3)):
            A.off = mark0
            pers = {}
            for i in grp:
                qn = A.alloc(f"qn{i}", [128, S], BF16)
                kn = A.alloc(f"kn{i}", [128, S], BF16)
                vT = A.alloc(f"vT{i}", [128, 2, S], BF16)
                ostg = A.alloc(f"ostg{i}", [128, 2, S], BF16)
                pers[i] = (qn, kn, vT, ostg)
            mark1 = A.off
            raw = [A.alloc(f"raw{k}", [128, CW + 3], F32) for k in range(2)]
            acc = [A.alloc(f"acc{k}", [128, CW], F32) for k in range(2)]
            th = [A.alloc(f"th{k}", [128, CW], F32) for k in range(2)]
            qc = A.alloc("qc", [128, S], BF16)
            kc = A.alloc("kc", [128, S], BF16)
            sq = A.alloc("sq", [128, 512], BF16)
            rsn = A.alloc("rsn", [128, 512], F32)
            for i in grp:
                qn, kn, vT, ostg = pers[i]
                blocks = [(i, (lambda t0, cw: qc[:, t0:t0 + cw])), (4 + i, (lambda t0, cw: kc[:, t0:t0 + cw])),
                          (8 + 2 * i, (lambda vv: (lambda t0, cw: vv[:, 0, t0:t0 + cw]))(vT)),
                          (9 + 2 * i, (lambda vv: (lambda t0, cw: vv[:, 1, t0:t0 + cw]))(vT))]
                for (blk, dfn) in blocks:
                    self.conv_block(self.qkvT, blk * 128, cp[:, 60 + blk * 4:64 + blk * 4], None, dfn, (raw, acc, th), S)
                for (src, dst, sc_) in ((qc, qn, 128.0 ** -0.5), (kc, kn, 1.0)):
                    for q in range(NQ):
                        qs = slice(q * 512, (q + 1) * 512)
                        P.act(sq[:], src[:, qs], AF.Square)
                        P.matmul(bk[q % 2][:], C['onesb'][:], sq[:])
                        P.act(rsn[:], bk[q % 2][:], AF.Sqrt, bias=C['eps'][:, 0:1], scale=1.0)
                        P.recip(rsn[:], rsn[:])
                        P.stt(dst[:, qs], src[:, qs], sc_, rsn[:], ALU.mult, ALU.mult)
            P.barrier()
            A.off = mark1
            gens = []
            for gi, i in enumerate(grp):
                w = alloc_work()
                gens.append(chunks_gen(i, pers[i], w, 4 * gi))
            live = list(gens)
            while live:
                for g_ in list(live):
                    try:
                        next(g_)
                    except StopIteration:
                        live.remove(g_)
            for i in grp:
                ostg = pers[i][3]
                for e in range(2):
                    P.dma(self.yT[2, (2 * i + e) * 128:(2 * i + e + 1) * 128, :], ostg[:, e, :], q='act')
            P.barrier()

    def phase_merge(self, l, xsrc):
        P, A, S, C = self.P, self.A, self.S, self.C
        NQ = self.NQ
        A.reset()
        bk = self.banks
        ws = []
        for n in ("w_ssd_out", "w_mla_out", "w_gdn_out", "w_out"):
            w = A.alloc(n, [128, 8, D], BF16)
            self.load_w(w[:], V(None, self.W[n].t[l].rearrange("(j p) c -> p j c", p=128)))
            ws.append(w)
        gB = A.alloc("g2B", [128, D], F32)
        P.dma(gB[:], self.W['norm2_g'][l].pbc(128))
        yin = [[A.alloc(f"yin{b}_{i}", [128, 8, 512], BF16) for b in range(3)] for i in range(2)]
        gt = [A.alloc(f"gt{i}", [128, 3, 512], F32) for i in range(2)]
        mixT = A.alloc("mixT", [128, 8, 512], BF16)
        tm = A.alloc("tm", [128, 512], F32)
        tm2 = A.alloc("tm2", [128, 512], F32)
        xt = [A.alloc(f"xt{i}", [128, D], F32) for i in range(2)]
        ht = [A.alloc(f"ht{i}", [128, D], F32) for i in range(2)]
        hn = A.alloc("hn", [128, D], BF16)
        junk = A.alloc("junk", [128, D], BF16)
        ssq = A.alloc("ssq", [128, 1], F32)
        rstd = A.alloc("rstd", [128, 1], F32)
        hstg = [A.alloc(f"hstg{i}", [128, 8, 128], BF16) for i in range(2)]
        gv = self.gatesT.t.ap().rearrange("(b r) t -> r b t", b=3)
        hnTd = self.hnT.t.ap().rearrange("(j p) t -> p j t", p=128)
        cnt = {'g': 0, 'x': 0}

        def loady(q):
            qs = slice(q * 512, (q + 1) * 512)
            for b in range(3):
                P.dma(yin[q % 2][b][:], V(None, self.yT.t[b].rearrange("(j p) t -> p j t", p=128)[:, :, qs]))
        loady(0)
        for q in range(NQ):
            if q + 1 < NQ:
                loady(q + 1)
            qs = slice(q * 512, (q + 1) * 512)
            for cb in range(8):
                g = gt[cnt['g'] % 2]
                cnt['g'] += 1
                P.dma(g[:], V(None, gv[cb * 128:(cb + 1) * 128, :, qs]))
                for b in range(3):
                    ps = bk[b]
                    for k in range(8):
                        P.matmul(ps[:], ws[b][:, k, cb * 128:(cb + 1) * 128], yin[q % 2][b][:, k, :],
                                 start=(k == 0), stop=(k == 7))
                P.tt(tm[:], bk[0][:], g[:, 0, :], ALU.mult)
                P.tt(tm2[:], bk[1][:], g[:, 1, :], ALU.mult)
                P.tt(tm[:], tm[:], tm2[:], ALU.add, eng='pool')
                P.tt(tm2[:], bk[2][:], g[:, 2, :], ALU.mult)
                P.tt(mixT[:, cb, :], tm[:], tm2[:], ALU.add, eng='pool')
            for t4 in range(4):
                t = q * 4 + t4
                x_ = xt[cnt['x'] % 2]
                h_ = ht[cnt['x'] % 2]
                hs_ = hstg[cnt['x'] % 2]
                cnt['x'] += 1
                P.dma(x_[:], xsrc[t * 128:(t + 1) * 128, :])
                for half in range(2):
                    ps = bk[4 + half]
                    for k in range(8):
                        P.matmul(ps[:], mixT[:, k, t4 * 128:(t4 + 1) * 128], ws[3][:, k, half * 512:(half + 1) * 512],
                                 start=(k == 0), stop=(k == 7))
                    P.tt(h_[:, half * 512:(half + 1) * 512], ps[:], x_[:, half * 512:(half + 1) * 512], ALU.add)
                P.dma(self.hres[t * 128:(t + 1) * 128, :], h_[:], q='act')
                P.act(junk[:], h_[:], AF.Square, accum_out=ssq[:])
                self.rmsnorm_rstd(rstd[:], ssq[:], D)
                P.stt(hn[:], h_[:], rstd[:, 0:1], gB[:], ALU.mult, ALU.mult)
                pb = bk[6 + t4 % 2][:].bitcast(BF16)
                for j in range(8):
                    P.transpose(pb[:, j * 128:(j + 1) * 128], hn[:, j * 128:(j + 1) * 128], C['ident'][:])
                P.copy(hs_[:], pb.re("p (j t) -> p j t", j=8), eng='pool' if False else 'dve')
                P.dma(V(None, hnTd[:, :, t * 128:(t + 1) * 128]), hs_[:], q='act')

    def phase_ffn_up(self, l):
        P, A, S, C = self.P, self.A, self.S, self.C
        NQ = self.NQ
        A.reset()
        bk = self.banks
        wup = A.alloc("wup", [128, 8, 4 * D], BF16)
        wv = self.W['w_up'].t[l].rearrange("(j p) c -> p j c", p=128)
        for s in range(8):
            self.load_w(wup[:, :, s * 512:(s + 1) * 512], V(None, wv[:, :, s * 512:(s + 1) * 512]))
        hin = [A.alloc(f"hin{i}", [128, 8, 512], BF16) for i in range(2)]
        rl = [A.alloc(f"rl{i}", [128, 512], BF16) for i in range(2)]
        ust = [A.alloc(f"ust{i}", [128, 4, 512], BF16) for i in range(2)]
        hnTd = self.hnT.t.ap().rearrange("(j p) t -> p j t", p=128)
        uTd = self.uT.t.ap().rearrange("(f p) t -> p f t", p=128)
        cnt = 0

        def loadh(q):
            P.dma(hin[q % 2][:], V(None, hnTd[:, :, q * 512:(q + 1) * 512]))
        loadh(0)
        for q in range(NQ):
            if q + 1 < NQ:
                loadh(q + 1)
            qs = slice(q * 512, (q + 1) * 512)
            for f4 in range(8):
                us = ust[f4 % 2]
                for ff in range(4):
                    fb = f4 * 4 + ff
                    ps = bk[cnt % 4]
                    r = rl[cnt % 2]
                    cnt += 1
                    for k in range(8):
                        P.matmul(ps[:], wup[:, k, fb * 128:(fb + 1) * 128], hin[q % 2][:, k, :],
                                 start=(k == 0), stop=(k == 7))
                    P.act(r[:], ps[:], AF.Relu)
                    P.tt(us[:, ff, :], r[:], r[:], ALU.mult, eng='dve' if cnt % 2 else 'pool')
                P.dma(V(None, uTd[:, f4 * 4:(f4 + 1) * 4, qs]), us[:], q='act')

    def phase_ffn_down(self, l, last):
        P, A, S, C = self.P, self.A, self.S, self.C
        NQ = self.NQ
        A.reset()
        bk = self.banks
        wd = A.alloc("wd", [128, 32, D], BF16)
        wv = self.W['w_down'].t[l].rearrange("(f p) c -> p f c", p=128)
        for s in range(4):
            self.load_w(wd[:, s * 8:(s + 1) * 8, :], V(None, wv[:, s * 8:(s + 1) * 8, :]))
        uin = [A.alloc(f"uin{i}", [128, 32, 512], BF16) for i in range(2)]
        ht = [A.alloc(f"ht{i}", [128, D], F32) for i in range(2)]
        ot = [A.alloc(f"ot{i}", [128, D], F32) for i in range(2)]
        gF = A.alloc("gF", [128, D], F32)
        junk = A.alloc("junk", [128, D], BF16)
        ssq = A.alloc("ssq", [128, 1], F32)
        rstd = A.alloc("rstd", [128, 1], F32)
        if last:
            P.dma(gF[:], self.W['final_norm_g'][:].pbc(128))
        uTd = self.uT.t.ap().rearrange("(f p) t -> p f t", p=128)
        cnt = 0

        def loadu(q):
            for s in range(4):
                P.dma(uin[q % 2][:, s * 8:(s + 1) * 8, :], V(None, uTd[:, s * 8:(s + 1) * 8, q * 512:(q + 1) * 512]))
        loadu(0)
        for q in range(NQ):
            if q + 1 < NQ:
                loadu(q + 1)
            for t4 in range(4):
                t = q * 4 + t4
                h_ = ht[cnt % 2]
                o_ = ot[cnt % 2]
                cnt += 1
                P.dma(h_[:], self.hres[t * 128:(t + 1) * 128, :])
                for half in range(2):
                    ps = bk[(cnt % 2) * 2 + half]
                    for f in range(32):
                        P.matmul(ps[:], uin[q % 2][:, f, t4 * 128:(t4 + 1) * 128], wd[:, f, half * 512:(half + 1) * 512],
                                 start=(f == 0), stop=(f == 31))
                    P.tt(o_[:, half * 512:(half + 1) * 512], ps[:], h_[:, half * 512:(half + 1) * 512], ALU.add)
                if last:
                    P.act(junk[:], o_[:], AF.Square, accum_out=ssq[:])
                    self.rmsnorm_rstd(rstd[:], ssq[:], D)
                    P.stt(o_[:], o_[:], rstd[:, 0:1], gF[:], ALU.mult, ALU.mult)
                    P.dma(self.out[t * 128:(t + 1) * 128, :], o_[:], q='act')
                else:
                    P.dma(self.xres[t * 128:(t + 1) * 128, :], o_[:], q='act')


def segs_base(c0, segs):
    return c0


def make_colpack(inputs):
    L = inputs['ssd_conv_w'].shape[0]
    cp = np.zeros((L, 128, NCOL), np.float32)
    for l in range(L):
        w = np.asarray(inputs['ssd_conv_w'][l])
        cp[l, :, 0:48] = w.reshape(4, 12, 128).transpose(2, 1, 0).reshape(128, 48)
        cp[l, :, 48:60] = np.asarray(inputs['ssd_conv_b'][l]).reshape(12, 128).T
        w = np.asarray(inputs['gdn_conv_w'][l])
        cp[l, :, 60:124] = w.reshape(4, 16, 128).transpose(2, 1, 0).reshape(128, 64)
        cp[l, :, 124:128] = np.asarray(inputs['mla_q_norm_g'][l]).reshape(4, 128).T
        cp[l, :, 128:130] = np.asarray(inputs['mla_kv_norm_g'][l]).reshape(2, 128).T
    return cp


_CACHE = {}


def get_nc(S, depth=DEPTH, debug=()):
    key = (S, depth, tuple(sorted(debug)))
    if key not in _CACHE:
        k = K(S, depth, debug)
        k.build()
        _CACHE[key] = k
    return _CACHE[key]


def run(inputs, ncores, S, depth=DEPTH, debug=(), trace=False):
    k = get_nc(S, depth, debug)
    cp = make_colpack(inputs)
    invf = (10000.0 ** (-np.arange(0, 64, 2, dtype=np.float32) / 64)).astype(np.float32)
    invf = np.concatenate([invf, invf]).reshape(64, 1).astype(np.float32)
    shared = {n: np.ascontiguousarray(np.asarray(inputs[n], dtype=np.float32)) for n in k.W}
    shared['colpack'] = cp
    shared['invf'] = invf
    in_maps = []
    for c in range(ncores):
        m = dict(shared)
        m['x'] = np.ascontiguousarray(np.asarray(inputs['x'][c], dtype=np.float32))
        m['positions'] = np.ascontiguousarray(np.asarray(inputs['positions'][c], dtype=np.int32))
        in_maps.append(m)
    res = run_bass_kernel_spmd(k.nc, in_maps, core_ids=list(range(ncores)), trace=trace)
    return res


def kernel(**inputs):
    x = np.asarray(inputs['x'])
    B, S, _ = x.shape
    res = run(inputs, B, S)
    out = np.stack([np.asarray(r['out']) for r in res.results], axis=0).astype(np.float32)
    return out
```

```python
import math
from contextlib import ExitStack
import numpy as np
import concourse.bass as bass
import concourse.mybir as mybir
from concourse.bass_utils import run_bass_kernel_spmd

F32 = mybir.dt.float32
BF16 = mybir.dt.bfloat16
I32 = mybir.dt.int32
AF = mybir.ActivationFunctionType
ALU = mybir.AluOpType
AX = mybir.AxisListType

COMPUTE = ('pe', 'act', 'dve', 'pool')
ALLENG = ('pe', 'act', 'dve', 'pool', 'sp')
NSEM_ENG = 4
NSEM_DMA = 40
NSEM_SWDMA = 16

D = 1024
DEPTH = 2
N_IN = 9568
EPS = 1e-6
NCOL = 130


class Buf:
    def __init__(self, name, t):
        self.name = name
        self.t = t
        self.last_w = None
        self.readers = []
        self.excl = False

    def __getitem__(self, k):
        return V(self, self.t[k])


class DR:
    def __init__(self, t):
        self.t = t

    def __getitem__(self, k):
        return V(None, self.t[k])

    def re(self, s, **kw):
        return V(None, self.t[:].rearrange(s, **kw) if not hasattr(self.t, 'rearrange') else self.t.rearrange(s, **kw))


class V:
    def __init__(self, buf, ap):
        self.buf = buf
        self.ap = ap

    def __getitem__(self, k):
        return V(self.buf, self.ap[k])

    def bc(self, shape):
        return V(self.buf, self.ap.to_broadcast(list(shape)))

    def re(self, s, **kw):
        return V(self.buf, self.ap.rearrange(s, **kw))

    def unsq(self, axis):
        return V(self.buf, self.ap.unsqueeze(axis))

    def pbc(self, n):
        return V(self.buf, self.ap.partition_broadcast(n))

    def bitcast(self, dt):
        return V(self.buf, self.ap.bitcast(dt))

    @property
    def shape(self):
        return self.ap.shape


class Op:
    __slots__ = ('eng', 'fn', 'deps', 'idx', 'dma', 'signal', 'sem', 'val', 'waits', 'gid')

    def __init__(self, eng, fn, dma):
        self.eng = eng
        self.fn = fn
        self.dma = dma
        self.deps = []
        self.idx = -1
        self.signal = False
        self.sem = None
        self.val = 0
        self.waits = []
        self.gid = -1


class Arena:
    def __init__(self, ap, nwords):
        self.ap = ap
        self.n = nwords
        self.off = 0

    def reset(self):
        self.off = 0

    def alloc(self, name, shape, dtype=F32):
        p = shape[0]
        nfree = 1
        for s in shape[1:]:
            nfree *= s
        esz = 2 if dtype == BF16 else 4
        words = (nfree * esz + 3) // 4
        words = (words + 7) // 8 * 8
        assert self.off + words <= self.n, f"arena overflow allocating {name}: {self.off}+{words}>{self.n}"
        a = self.ap[0:p, self.off:self.off + words]
        self.off += words
        if dtype != F32:
            a = a.bitcast(dtype)
        a = a[:, 0:nfree]
        if len(shape) == 3:
            a = a.rearrange("p (a b) -> p a b", a=shape[1])
        elif len(shape) == 4:
            a = a.rearrange("p (a b c) -> p a b c", a=shape[1], b=shape[2])
        return Buf(name, a)


class Prog:
    def __init__(self, nc):
        self.nc = nc
        self.ops = []
        self.stack = None
        self.last_op = {}
        self.dmas_since = []
        self.pending = {}
        self.bar_t = None
        self.rr = 0

    def sbt(self, name, shape, dtype=F32):
        t = self.stack.enter_context(self.nc.sbuf_tensor(name, list(shape), dtype))
        return Buf(name, t)

    def pst(self, name, shape, dtype=F32):
        t = self.stack.enter_context(self.nc.psum_tensor(name, list(shape), dtype))
        return Buf(name, t)

    def dram(self, name, shape, dtype=F32, kind="Internal"):
        return DR(self.nc.dram_tensor(name, list(shape), dtype, kind=kind))

    def add(self, eng, fn, reads, writes, dma=False, extra_deps=()):
        op = Op(eng, fn, dma)
        op.gid = len(self.ops)
        deps = {}
        rb, wb = [], []
        for v in reads:
            b = v.buf if isinstance(v, V) else v
            if b is not None and b not in rb:
                rb.append(b)
        for v in writes:
            b = v.buf if isinstance(v, V) else v
            if b is not None and b not in wb:
                wb.append(b)
        for b in rb:
            if b.last_w is not None:
                deps[b.last_w.gid] = b.last_w
            if b.excl:
                for r in b.readers:
                    if r.eng != eng:
                        deps[r.gid] = r
        for b in wb:
            if b.last_w is not None:
                deps[b.last_w.gid] = b.last_w
            for r in b.readers:
                deps[r.gid] = r
        for d in extra_deps:
            deps[d.gid] = d
        pb = self.pending.pop(eng, None)
        if pb is not None:
            deps[pb.gid] = pb
        for b in rb:
            if b not in wb:
                b.readers.append(op)
        for b in wb:
            b.last_w = op
            b.readers = []
        op.deps = list(deps.values())
        self.ops.append(op)
        if dma:
            self.dmas_since.append(op)
        else:
            self.last_op[eng] = op
        return op

    def barrier(self):
        deps = [o for o in self.last_op.values()] + list(self.dmas_since)
        a = self._a
        bt = self.bar_t
        op = self.add('dve', lambda e: e.memset(a(bt[:]), 0.0), [], [bt[:]], extra_deps=deps)
        self.dmas_since = []
        self.pending = {e: op for e in ALLENG if e != 'dve'}
        return op

    @staticmethod
    def _a(x):
        return x.ap if isinstance(x, V) else x

    def matmul(self, out, lhsT, rhs, start=True, stop=True):
        a = self._a
        return self.add('pe', lambda e: e.matmul(a(out), a(lhsT), a(rhs), start=start, stop=stop),
                        [lhsT, rhs], [out])

    def transpose(self, out, in_, ident):
        a = self._a
        return self.add('pe', lambda e: e.transpose(a(out), a(in_), a(ident)), [in_, ident], [out])

    def act(self, out, in_, func, bias=None, scale=None, accum_out=None):
        a = self._a
        kw = {}
        reads = [in_]
        writes = [out]
        if bias is not None:
            kw['bias'] = a(bias)
            if isinstance(bias, V):
                reads.append(bias)
        if scale is not None:
            kw['scale'] = a(scale)
            if isinstance(scale, V):
                reads.append(scale)
        if accum_out is not None:
            kw['accum_out'] = a(accum_out)
            writes.append(accum_out)
        return self.add('act', lambda e: e.activation(a(out), a(in_), func, **kw), reads, writes)

    def tt(self, out, in0, in1, op, eng='dve'):
        a = self._a
        return self.add(eng, lambda e: e.tensor_tensor(a(out), a(in0), a(in1), op), [in0, in1], [out])

    def ts(self, out, in0, s1, s2, op0, op1=None, eng='dve'):
        a = self._a
        reads = [in0] + [s for s in (s1, s2) if isinstance(s, V)]
        kw = {}
        if op1 is not None:
            kw['op1'] = op1
        return self.add(eng, lambda e: e.tensor_scalar(a(out), a(in0), a(s1), a(s2) if s2 is not None else None, op0, **kw),
                        reads, [out])

    def stt(self, out, in0, scalar, in1, op0, op1, eng='dve'):
        a = self._a
        reads = [in0, in1] + ([scalar] if isinstance(scalar, V) else [])
        return self.add(eng, lambda e: e.scalar_tensor_tensor(a(out), a(in0), a(scalar), a(in1), op0, op1),
                        reads, [out])

    def copy(self, out, in_, eng='dve'):
        a = self._a
        if eng == 'act':
            return self.add('act', lambda e: e.copy(a(out), a(in_)), [in_], [out])
        return self.add(eng, lambda e: e.tensor_copy(a(out), a(in_)), [in_], [out])

    def acopy(self, out, in_):
        self.rr += 1
        return self.copy(out, in_, eng='act' if self.rr % 2 else 'dve')

    def memset(self, out, val, eng='dve'):
        a = self._a
        return self.add(eng, lambda e: e.memset(a(out), val), [], [out])

    def recip(self, out, in_):
        a = self._a
        return self.add('dve', lambda e: e.reciprocal(a(out), a(in_)), [in_], [out])

    def affine_select(self, out, in_, pattern, cmp, fill, base=0, cm=0):
        a = self._a
        return self.add('pool', lambda e: e.affine_select(a(out), a(in_), pattern, cmp, fill, base=base,
                                                          channel_multiplier=cm), [in_], [out])

    def dma(self, out, in_, q='sp'):
        a = self._a
        return self.add(q, lambda e: e.dma_start(a(out), a(in_)), [in_], [out], dma=True)

    def emit(self):
        nc = self.nc
        ops = self.ops
        eng_ops = {e: [] for e in ALLENG}
        for op in ops:
            op.idx = len(eng_ops[op.eng])
            eng_ops[op.eng].append(op)
        known = {e: {x: -1 for x in ALLENG} for e in ALLENG}
        known_dma = {e: set() for e in ALLENG}
        for op in ops:
            E = op.eng
            for d in op.deps:
                if d.dma:
                    if d.gid in known_dma[E]:
                        continue
                    known_dma[E].add(d.gid)
                    op.waits.append(d)
                else:
                    if d.eng == 'pe' and E == 'pe' and not op.dma:
                        continue
                    if known[E][d.eng] >= d.idx:
                        continue
                    known[E][d.eng] = d.idx
                    d.signal = True
                    op.waits.append(d)
        st = self.stack
        sems = {e: [st.enter_context(nc.semaphore(f"s_{e}{i}")) for i in range(NSEM_ENG)] for e in COMPUTE}
        dsems = [st.enter_context(nc.semaphore(f"s_dma{i}")) for i in range(NSEM_DMA)]
        swsems = [st.enter_context(nc.semaphore(f"s_swdma{i}")) for i in range(NSEM_SWDMA)]
        cnt = {e: 0 for e in COMPUTE}
        dcnt = 0
        swcnt = 0
        last_dma_val = {}
        for op in ops:
            if op.dma and op.eng == 'pool':
                op.signal = True
                k = swcnt
                swcnt += 1
                op.sem = swsems[k % NSEM_SWDMA]
                op.val = 16 * (k // NSEM_SWDMA + 1)
                last_dma_val[('sw', k % NSEM_SWDMA)] = (op.sem, op.val)
            elif op.dma:
                op.signal = True
                k = dcnt
                dcnt += 1
                op.sem = dsems[k % NSEM_DMA]
                op.val = 16 * (k // NSEM_DMA + 1)
                last_dma_val[('hw', k % NSEM_DMA)] = (op.sem, op.val)
            elif op.signal:
                k = cnt[op.eng]
                cnt[op.eng] += 1
                op.sem = sems[op.eng][k % NSEM_ENG]
                op.val = k // NSEM_ENG + 1
        block = st.enter_context(nc.Block())

        def body(ename):
            def f(e):
                for op in eng_ops[ename]:
                    for d in op.waits:
                        e.wait_ge(d.sem, d.val)
                    ins = op.fn(e)
                    if op.signal:
                        ins.then_inc(op.sem, 16 if op.dma else 1)
                if ename == 'sp':
                    for (sm, v) in last_dma_val.values():
                        e.wait_ge(sm, v)
            return f

        block.tensor(body('pe'))
        block.scalar(body('act'))
        block.vector(body('dve'))
        block.gpsimd(body('pool'))
        block.sync(body('sp'))
        stats = {e: len(eng_ops[e]) for e in ALLENG}
        stats['signals'] = dict(cnt)
        stats['dmas'] = dcnt
        return stats


class K:
    def __init__(self, S, depth=DEPTH, debug=()):
        self.S = S
        self.NT = S // 128
        self.NQ = S // 512
        self.NCH = S // 64
        self.depth = depth
        self.debug = set(debug)
        self.nc = bass.Bass("TRN2", target_bir_lowering=False)
        self.P = Prog(self.nc)

    def scratch(self, name, shape, dtype=F32):
        kind = "ExternalOutput" if name in self.debug else "Internal"
        return self.P.dram(name, shape, dtype, kind=kind)

    def build(self):
        P, nc, S = self.P, self.nc, self.S
        L = self.depth
        inp = lambda n, sh, dt=F32: P.dram(n, sh, dt, kind="ExternalInput")
        self.x_in = inp("x", [S, D])
        self.pos_in = inp("positions", [S], I32)
        self.colpack = inp("colpack", [DEPTH, 128, NCOL])
        self.invf = inp("invf", [64, 1])
        W = {}
        for n, sh in [("norm1_g", [DEPTH, D]), ("w_in", [DEPTH, D, N_IN]), ("ssd_dt_bias", [DEPTH, 16]),
                      ("ssd_a_log", [DEPTH, 16]), ("ssd_d", [DEPTH, 16]), ("ssd_norm_g", [DEPTH, D]),
                      ("mla_w_uq", [DEPTH, 512, 1536]), ("mla_w_ukv", [DEPTH, 256, 2048]),
                      ("gdn_dt_bias", [DEPTH, 8]), ("gdn_a_log", [DEPTH, 8]), ("gdn_norm_g", [DEPTH, 128]),
                      ("w_ssd_out", [DEPTH, D, D]), ("w_mla_out", [DEPTH, D, D]), ("w_gdn_out", [DEPTH, D, D]),
                      ("w_out", [DEPTH, D, D]), ("norm2_g", [DEPTH, D]), ("w_up", [DEPTH, D, 4 * D]),
                      ("w_down", [DEPTH, 4 * D, D]), ("final_norm_g", [D])]:
            W[n] = inp(n, sh)
        self.W = W
        self.out = P.dram("out", [S, D], F32, kind="ExternalOutput")
        sc = self.scratch
        self.xres = sc("xres", [S, D])
        self.hres = sc("hres", [S, D])
        self.zs2 = sc("zs2", [S, D])
        self.zg2 = sc("zg2", [S, D])
        self.xbcT = sc("xbcT", [1536, S])
        self.qkvT = sc("qkvT", [2048, S])
        self.dtk = sc("dtk", [S, 16])
        self.gbk = sc("gbk", [S, 16])
        self.cqT = sc("cqT", [512, S])
        self.ckvT = sc("ckvT", [256, S])
        self.krT = sc("krT", [128, S])
        self.gatesT = sc("gatesT", [3072, S])
        self.yT = sc("yT", [3, D, S], BF16)
        self.uT = sc("uT", [4 * D, S], BF16)
        self.hnT = sc("hnT", [D, S], BF16)
        self.cosT = sc("cosT", [64, S])
        self.sinT = sc("sinT", [64, S])

        with ExitStack() as st:
            P.stack = st
            C = {}
            C['maskLE'] = P.sbt("maskLE", [128, 128], F32)
            C['maskGT'] = P.sbt("maskGT", [128, 128], F32)
            C['maskLT'] = P.sbt("maskLT", [128, 128], F32)
            C['ones'] = P.sbt("onesf", [128, 128], F32)
            C['identf'] = P.sbt("identf", [128, 128], F32)
            C['ident'] = P.sbt("identb", [128, 128], BF16)
            C['onesb'] = P.sbt("onesb", [128, 128], BF16)
            C['maskLEb'] = P.sbt("maskLEb", [128, 128], BF16)
            C['nmaskLT'] = P.sbt("nmaskLT", [128, 128], F32)
            C['eps'] = P.sbt("epsc", [128, 1], F32)
            C['one'] = P.sbt("onec", [128, 1], F32)
            P.bar_t = P.sbt("bar_t", [128, 1], F32)
            self.C = C
            ARW = 48000
            arena_t = st.enter_context(nc.sbuf_tensor("arena", [128, ARW], F32))
            self.A = Arena(arena_t, ARW)
            self.banks = [P.pst(f"bank{i}", [128, 512], F32) for i in range(8)]
            for b_ in self.banks:
                b_.excl = True

            P.memset(C['ones'][:], 1.0)
            P.memset(C['onesb'][:], 1.0)
            P.memset(C['eps'][:], EPS)
            P.memset(C['one'][:], 1.0)
            P.memset(C['identf'][:], 0.0)
            P.affine_select(C['identf'][:], C['identf'][:], [[-1, 128]], ALU.not_equal, 1.0, base=0, cm=1)
            P.copy(C['ident'][:], C['identf'][:])
            P.affine_select(C['maskLE'][:], C['ones'][:], [[1, 128]], ALU.is_ge, 0.0, base=0, cm=-1)
            P.affine_select(C['maskLT'][:], C['ones'][:], [[1, 128]], ALU.is_ge, 0.0, base=-1, cm=-1)
            P.affine_select(C['maskGT'][:], C['ones'][:], [[-1, 128]], ALU.is_ge, 0.0, base=-1, cm=1)
            P.copy(C['maskLEb'][:], C['maskLE'][:])
            P.ts(C['nmaskLT'][:], C['maskLT'][:], -1.0, None, ALU.mult)

            self.rope_tables()
            P.barrier()
            for l in range(L):
                xsrc = self.x_in if l == 0 else self.xres
                self.phase1(l, xsrc)
                P.barrier()
                if 'stop1' in self.debug:
                    break
                if 'skipssd' not in self.debug:
                    self.phase_ssd(l)
                    P.barrier()
                if 'stop2' in self.debug:
                    break
                if 'skipmla' not in self.debug:
                    self.phase_mla(l)
                    P.barrier()
                if 'stop3' in self.debug:
                    break
                self.phase_gdn(l)
                P.barrier()
                if 'stop4' in self.debug:
                    break
                self.phase_merge(l, xsrc)
                P.barrier()
                self.phase_ffn_up(l)
                P.barrier()
                self.phase_ffn_down(l, last=(l == L - 1))
                P.barrier()
            self.stats = P.emit()
        return nc

    def rmsnorm_rstd(self, rstd, ssq, n):
        P = self.P
        p = ssq.shape[0]
        P.act(rstd, ssq, AF.Sqrt, bias=self.C['eps'][0:p, 0:1], scale=1.0 / n)
        P.recip(rstd, rstd)

    def load_w(self, dst, src):
        self.P.dma(dst, src, q='pool')

    def rope_tables(self):
        P, A, S = self.P, self.A, self.S
        A.reset()
        posi = A.alloc("posi", [64, S], I32)
        posf = A.alloc("posf", [64, S], F32)
        ang = A.alloc("ang", [64, S], F32)
        kk = A.alloc("kk", [64, S], F32)
        res = A.alloc("res", [64, S], F32)
        invc = A.alloc("invc", [64, 1], F32)
        P.dma(posi[:], self.pos_in[:].pbc(64))
        P.dma(invc[:], self.invf[:])
        P.copy(posf[:], posi[:])
        P.ts(ang[:], posf[:], invc[:, 0:1], None, ALU.mult)
        MAG = 12582912.0
        for name, shift, dst in (("sin", 0.0, self.sinT), ("cos", math.pi / 2, self.cosT)):
            a2 = ang
            if shift != 0.0:
                P.ts(posf[:], ang[:], shift, None, ALU.add)
                a2 = posf
            P.ts(kk[:], a2[:], 1.0 / (2 * math.pi), MAG, ALU.mult, ALU.add)
            P.ts(kk[:], kk[:], -MAG, None, ALU.add)
            P.stt(res[:], kk[:], -2 * math.pi, a2[:], ALU.mult, ALU.add)
            P.ts(res[:], res[:], math.pi, -math.pi, ALU.min, ALU.max)
            P.act(res[:], res[:], AF.Sin)
            P.dma(dst[:], res[:])

    def phase1(self, l, xsrc):
        P, A, S, C = self.P, self.A, self.S, self.C
        NT, NQ = self.NT, self.NQ
        A.reset()
        xnT = A.alloc("xnT", [128, 8, S], BF16)
        gB = A.alloc("gB", [128, D], F32)
        xt = [A.alloc(f"xt{i}", [128, D], F32) for i in range(2)]
        xn = [A.alloc(f"xn{i}", [128, D], BF16) for i in range(2)]
        junk = A.alloc("junk", [128, D], BF16)
        ssq = [A.alloc(f"ssq{i}", [128, 1], F32) for i in range(2)]
        rstd = [A.alloc(f"rstd{i}", [128, 1], F32) for i in range(2)]
        P.dma(gB[:], self.W['norm1_g'][l].pbc(128))
        bk = self.banks

        def load(t):
            P.dma(xt[t % 2][:], xsrc[t * 128:(t + 1) * 128, :])
        load(0)
        for t in range(NT):
            if t + 1 < NT:
                load(t + 1)
            b = t % 2
            P.act(junk[:], xt[b][:], AF.Square, accum_out=ssq[b][:])
            self.rmsnorm_rstd(rstd[b][:], ssq[b][:], D)
            P.stt(xn[b][:], xt[b][:], rstd[b][:, 0:1], gB[:], ALU.mult, ALU.mult)
            pb = bk[t % 2][:].bitcast(BF16)
            for j in range(8):
                P.transpose(pb[:, j * 128:(j + 1) * 128], xn[b][:, j * 128:(j + 1) * 128], C['ident'][:])
            P.acopy(xnT[:, :, t * 128:(t + 1) * 128], pb.re("p (j t) -> p j t", j=8))

        wsl = [A.alloc(f"wsl{i}", [128, 8, 512], BF16) for i in range(2)]
        stg = [A.alloc(f"stg{i}", [128, S], F32) for i in range(2)]
        stk = [A.alloc(f"stk{i}", [128, 512], F32) for i in range(3)]
        tmp = [A.alloc(f"tmp{i}", [128, 512], F32) for i in range(2)]
        win = self.W['w_in']
        wv = win.t[l].rearrange("(j p) c -> p j c", p=128)
        segs = [
            (0, 1024, 'silu2', 'tok', self.zs2, 0),
            (1024, 1536, 'copy', 'feat', self.xbcT, 0),
            (2560, 16, 'copy', 'tok', self.dtk, 0),
            (2576, 512, 'copy', 'feat', self.cqT, 0),
            (3088, 256, 'copy', 'feat', self.ckvT, 0),
            (3344, 64, 'rope', 'feat', self.krT, 0),
            (3408, 2048, 'copy', 'feat', self.qkvT, 0),
            (5456, 1024, 'silu2', 'tok', self.zg2, 0),
            (6480, 16, 'copy', 'tok', self.gbk, 0),
            (6496, 3072, 'sigmoid', 'feat', self.gatesT, 0),
        ]
        cnt = {'slab': 0, 'bank': 0, 'stg': 0, 'stk': 0, 'tmp': 0}

        def evac(dst, ps, kind):
            if kind == 'copy' or kind == 'rope':
                P.acopy(dst, ps)
            elif kind == 'silu2':
                tm = tmp[cnt['tmp'] % 2]
                cnt['tmp'] += 1
                w = ps.shape[1]
                P.act(tm[:, 0:w], ps, AF.Tanh, scale=0.5)
                P.stt(dst, tm[:, 0:w], 1.0, ps, ALU.add, ALU.mult)
            elif kind == 'sigmoid':
                P.act(dst, ps, AF.Tanh, scale=0.5)
                P.ts(dst, dst, 0.5, 0.5, ALU.mult, ALU.add, eng='pool')

        for (c0, ncols, kind, layout, dst, _) in segs:
            for s0 in range(0, ncols, 512):
                w = min(512, ncols - s0)
                sl = wsl[cnt['slab'] % 2]
                cnt['slab'] += 1
                self.load_w(sl[:, :, 0:w], V(None, wv[:, :, c0 + s0:c0 + s0 + w]))
                wuse = w
                if kind == 'rope':
                    P.ts(sl[:, :, 64:96], sl[:, :, 32:64], -1.0, None, ALU.mult)
                    P.copy(sl[:, :, 96:128], sl[:, :, 0:32])
                    wuse = 128
                if layout == 'feat':
                    for b0 in range(0, wuse, 128):
                        nb = min(128, wuse - b0)
                        sg = stg[cnt['stg'] % 2]
                        cnt['stg'] += 1
                        for q in range(NQ):
                            ps = bk[cnt['bank'] % 4]
                            cnt['bank'] += 1
                            for k in range(8):
                                P.matmul(ps[0:nb, :], sl[:, k, b0:b0 + nb], xnT[:, k, q * 512:(q + 1) * 512],
                                         start=(k == 0), stop=(k == 7))
                            evac(sg[0:nb, q * 512:(q + 1) * 512], ps[0:nb, :], kind)
                        r0 = c0 - segs_base(c0, segs) + s0 + b0
                        P.dma(dst[r0:r0 + nb, :], sg[0:nb, :], q='act')
                else:
                    for t in range(NT):
                        ps = bk[cnt['bank'] % 4]
                        cnt['bank'] += 1
                        for k in range(8):
                            P.matmul(ps[:, 0:w], xnT[:, k, t * 128:(t + 1) * 128], sl[:, k, 0:w],
                                     start=(k == 0), stop=(k == 7))
                        sk = stk[cnt['stk'] % 3]
                        cnt['stk'] += 1
                        evac(sk[:, 0:w], ps[:, 0:w], kind)
                        P.dma(dst[t * 128:(t + 1) * 128, s0:s0 + w], sk[:, 0:w], q='act')

    def conv_block(self, srcT, row0, wcols, bcol, dst_fn, bufs, S):
        P = self.P
        CW = min(S, 1024)
        raw, acc, th = bufs
        nseg = S // CW
        for sgi in range(nseg):
            t0 = sgi * CW
            r = raw[sgi % 2]
            if t0 == 0:
                P.memset(r[:, 0:3], 0.0, eng='pool')
                P.dma(r[:, 3:3 + CW], srcT[row0:row0 + 128, t0:t0 + CW])
            else:
                P.dma(r[:, 0:3 + CW], srcT[row0:row0 + 128, t0 - 3:t0 + CW])
            a = acc[sgi % 2]
            if bcol is not None:
                P.ts(a[:], r[:, 3:3 + CW], wcols[:, 3:4], bcol, ALU.mult, ALU.add)
            else:
                P.ts(a[:], r[:, 3:3 + CW], wcols[:, 3:4], None, ALU.mult)
            P.stt(a[:], r[:, 2:2 + CW], wcols[:, 2:3], a[:], ALU.mult, ALU.add)
            P.stt(a[:], r[:, 1:1 + CW], wcols[:, 1:2], a[:], ALU.mult, ALU.add)
            P.stt(a[:], r[:, 0:CW], wcols[:, 0:1], a[:], ALU.mult, ALU.add)
            t = th[sgi % 2]
            P.act(t[:], a[:], AF.Tanh, scale=0.5)
            P.stt(a[:], t[:], 1.0, a[:], ALU.add, ALU.mult)
            P.add('act', (lambda o, i: (lambda e: e.mul(o, i, 0.5)))(dst_fn(t0, CW).ap, a[:].ap), [a[:]], [dst_fn(t0, CW)])

    def softplus(self, out, x, tmp1, tmp2):
        P = self.P
        p = x.shape[0]
        P.act(tmp1, x, AF.Abs)
        P.act(tmp1, tmp1, AF.Exp, scale=-1.0)
        P.act(tmp1, tmp1, AF.Ln, bias=self.C['one'][0:p, 0:1], scale=1.0)
        P.stt(out, x, 0.0, tmp1, ALU.max, ALU.add)

    def phase_ssd(self, l):
        P, A, S, C = self.P, self.A, self.S, self.C
        NT = self.NT
        A.reset()
        bk = self.banks
        xcT = A.alloc("xcT", [128, 12, S], BF16)
        cp = A.alloc("cp", [128, NCOL], F32)
        P.dma(cp[:], self.colpack[l])
        CW = min(S, 1024)
        mark = A.off
        raw = [A.alloc(f"raw{i}", [128, CW + 3], F32) for i in range(2)]
        acc = [A.alloc(f"acc{i}", [128, CW], F32) for i in range(2)]
        th = [A.alloc(f"th{i}", [128, CW], F32) for i in range(2)]
        for j in range(12):
            self.conv_block(self.xbcT, j * 128, cp[:, j * 4:(j + 1) * 4], cp[:, 48 + j:49 + j],
                            (lambda jj: (lambda t0, cw: xcT[:, jj, t0:t0 + cw]))(j), (raw, acc, th), S)
        P.barrier()
        A.off = mark
        dtb = A.alloc("dtb", [128, 16], F32)
        alog = A.alloc("alog", [128, 16], F32)
        aB = A.alloc("aB", [128, 16], F32)
        dB = A.alloc("dB", [128, 16], F32)
        ngB = A.alloc("ngB", [128, D], F32)
        P.dma(dtb[:], self.W['ssd_dt_bias'][l].pbc(128))
        P.dma(alog[:], self.W['ssd_a_log'][l].pbc(128))
        P.dma(dB[:], self.W['ssd_d'][l].pbc(128))
        P.dma(ngB[:], self.W['ssd_norm_g'][l].pbc(128))
        P.act(aB[:], alog[:], AF.Exp)
        P.ts(aB[:], aB[:], -1.0, None, ALU.mult)
        dt_all = A.alloc("dt_all", [128, NT, 16], F32)
        dA_all = A.alloc("dA_all", [128, NT, 16], F32)
        t1 = A.alloc("t1", [128, NT, 16], F32)
        P.dma(dt_all[:], self.dtk.t[:].rearrange("(n p) h -> p n h", p=128) if False else V(None, self.dtk.t.ap().rearrange("(n p) h -> p n h", p=128)))
        P.tt(dt_all[:], dt_all[:], dtb[:].unsq(1).bc([128, NT, 16]), ALU.add)
        self.softplus(dt_all[:], dt_all[:], t1[:], None)
        P.tt(dA_all[:], dt_all[:], aB[:].unsq(1).bc([128, NT, 16]), ALU.mult)

        hs = A.alloc("hs", [128, D], F32)
        hsb = A.alloc("hsb", [128, D], BF16)
        P.memset(hs[:], 0.0)
        P.memset(hsb[:], 0.0, eng='pool')
        Xsb = A.alloc("Xsb", [128, D], BF16)
        Btok = A.alloc("Btok", [128, 256], BF16)
        ex = A.alloc("ex", [128, 48], F32)
        CBm = A.alloc("CBm", [128, 2, 128], BF16)
        rhsD = A.alloc("rhsD", [128, 16, 128], F32)
        LT = A.alloc("LT", [128, 8, 128], BF16)
        MT = A.alloc("MT", [128, 16, 128], BF16)
        Xdt = A.alloc("Xdt", [128, D], BF16)
        Xds = A.alloc("Xds", [128, D], BF16)
        y1 = A.alloc("y1", [128, D], F32)
        t2 = A.alloc("t2", [128, D], F32)
        yz = A.alloc("yz", [128, D], F32)
        jk = A.alloc("jk", [128, 512], BF16)
        ssq = A.alloc("ssq", [128, 2], F32)
        rstd = A.alloc("rstd", [128, 2], F32)
        yn = A.alloc("yn", [128, D], BF16)
        zt = [A.alloc(f"zt{i}", [128, D], F32) for i in range(2)]
        ystg = [A.alloc(f"ystg{i}", [128, 8, 128], BF16) for i in range(2)]
        mLE, mGT, ones = C['maskLE'], C['maskGT'], C['ones']
        yTd = self.yT.t[0].rearrange("(j p) t -> p j t", p=128)

        def loadz(c):
            P.dma(zt[c % 2][:], self.zs2[c * 128:(c + 1) * 128, :])
        loadz(0)
        for c in range(NT):
            if c + 1 < NT:
                loadz(c + 1)
            cs = slice(c * 128, (c + 1) * 128)
            dA = dA_all[:, c, :]
            dt = dt_all[:, c, :]
            pb0 = bk[0][:].bitcast(BF16)
            for j in range(8):
                P.transpose(pb0[:, j * 128:(j + 1) * 128], xcT[:, j, cs], C['ident'][:])
            P.copy(Xsb[:], pb0, eng='act')
            pb1 = bk[1][:].bitcast(BF16)
            for j in range(2):
                P.transpose(pb1[:, j * 128:(j + 1) * 128], xcT[:, 8 + j, cs], C['ident'][:])
            P.copy(Btok[:], pb1[:, 0:256])
            P.matmul(bk[2][:, 0:16], mLE[:], dA)
            P.matmul(bk[2][:, 16:32], mGT[:], dA)
            P.matmul(bk[2][:, 32:48], ones[:], dA)
            P.act(ex[:], bk[2][:, 0:48], AF.Exp)
            eA, ds, cd = ex[:, 0:16], ex[:, 16:32], ex[:, 32:48]
            for g in range(2):
                P.matmul(bk[2][:, 64 + g * 128:64 + (g + 1) * 128], xcT[:, 8 + g, cs], xcT[:, 10 + g, cs])
            P.tt(CBm[:], bk[2][:, 64:320].re("p (g l) -> p g l", g=2), mLE[:].unsq(1).bc([128, 2, 128]), ALU.mult)
            P.tt(rhsD[:], mLE[:].unsq(1).bc([128, 16, 128]), dA.unsq(2).bc([128, 16, 128]), ALU.mult, eng='pool')
            for g in range(2):
                for i in range(2):
                    P.matmul(bk[3 + i][:], mGT[:], rhsD[:, g * 8 + i * 4:g * 8 + i * 4 + 4, :].re("p h l -> p (h l)"))
                    P.act(LT[:, i * 4:(i + 1) * 4, :].re("p h l -> p (h l)"), bk[3 + i][:], AF.Exp)
                P.tt(MT[:, g * 8:(g + 1) * 8, :], LT[:], CBm[:, g:g + 1, :].bc([128, 8, 128]), ALU.mult)
            P.tt(Xdt[:].re("p (h q) -> p h q", h=16), Xsb[:].re("p (h q) -> p h q", h=16),
                 dt.unsq(2).bc([128, 16, 64]), ALU.mult)
            P.tt(Xds[:].re("p (h q) -> p h q", h=16), Xdt[:].re("p (h q) -> p h q", h=16),
                 ds.unsq(2).bc([128, 16, 64]), ALU.mult, eng='pool')
            for h in range(16):
                P.matmul(bk[5 + h // 8][:, (h % 8) * 64:(h % 8 + 1) * 64], MT[:, h, :], Xdt[:, h * 64:(h + 1) * 64])
            for g in range(2):
                P.matmul(bk[3 + g][:], xcT[:, 10 + g, cs], hsb[:, g * 512:(g + 1) * 512])
            sbank = (bk[7], bk[1])
            for g in range(2):
                P.matmul(sbank[g][:], Btok[:, g * 128:(g + 1) * 128], Xds[:, g * 512:(g + 1) * 512])
            for g in range(2):
                gs = slice(g * 512, (g + 1) * 512)
                P.tt(y1[:, gs].re("p (h q) -> p h q", h=8), bk[3 + g][:].re("p (h q) -> p h q", h=8),
                     eA[:, g * 8:(g + 1) * 8].unsq(2).bc([128, 8, 64]), ALU.mult)
                P.tt(y1[:, gs], y1[:, gs], bk[5 + g][:], ALU.add)
            P.tt(t2[:].re("p (h q) -> p h q", h=16), Xsb[:].re("p (h q) -> p h q", h=16),
                 dB[:].unsq(2).bc([128, 16, 64]), ALU.mult, eng='pool')
            P.tt(y1[:], y1[:], t2[:], ALU.add)
            P.stt(yz[:], y1[:], 0.5, zt[c % 2][:], ALU.mult, ALU.mult)
            for g in range(2):
                gs = slice(g * 512, (g + 1) * 512)
                P.act(jk[:], yz[:, gs], AF.Square, accum_out=ssq[:, g:g + 1])
            self.rmsnorm_rstd(rstd[:], ssq[:], 512)
            for g in range(2):
                gs = slice(g * 512, (g + 1) * 512)
                P.stt(yn[:, gs], yz[:, gs], rstd[:, g:g + 1], ngB[:, gs], ALU.mult, ALU.mult)
            pb0 = bk[0][:].bitcast(BF16)
            for j in range(8):
                P.transpose(pb0[:, j * 128:(j + 1) * 128], yn[:, j * 128:(j + 1) * 128], C['ident'][:])
            ys = ystg[c % 2]
            P.copy(ys[:], pb0.re("p (j t) -> p j t", j=8), eng='act')
            P.dma(V(None, yTd[:, :, cs]), ys[:], q='act')
            for g in range(2):
                gs = slice(g * 512, (g + 1) * 512)
                P.tt(hs[:, gs].re("p (h q) -> p h q", h=8), hs[:, gs].re("p (h q) -> p h q", h=8),
                     cd[:, g * 8:(g + 1) * 8].unsq(2).bc([128, 8, 64]), ALU.mult)
                P.tt(hs[:, gs], hs[:, gs], sbank[g][:], ALU.add)
            P.copy(hsb[:], hs[:], eng='pool')

    def phase_mla(self, l):
        P, A, S, C = self.P, self.A, self.S, self.C
        NT, NQ = self.NT, self.NQ
        A.reset()
        bk = self.banks
        cp = A.alloc("cp", [128, NCOL], F32)
        P.dma(cp[:], self.colpack[l])
        cqn = A.alloc("cqn", [128, 4, S], BF16)
        ckvn = A.alloc("ckvn", [128, 2, S], BF16)
        kpe = A.alloc("kpe", [64, S], BF16)
        wuq = A.alloc("wuq", [128, 4, 1536], BF16)
        wukv = A.alloc("wukv", [128, 2, 2048], BF16)
        wrot = A.alloc("wrot", [128, 4, 512], BF16)
        self.load_w(wuq[:], V(None, self.W['mla_w_uq'].t[l].rearrange("(j p) c -> p j c", p=128)))
        self.load_w(wukv[:], V(None, self.W['mla_w_ukv'].t[l].rearrange("(j p) c -> p j c", p=128)))
        for h in range(8):
            P.ts(wrot[:, :, h * 64:h * 64 + 32], wuq[:, :, h * 192 + 160:h * 192 + 192], -1.0, None, ALU.mult)
            P.copy(wrot[:, :, h * 64 + 32:h * 64 + 64], wuq[:, :, h * 192 + 128:h * 192 + 160], eng='pool')
        mark = A.off
        cin = [A.alloc(f"cin{i}", [128, 4, 512], F32) for i in range(2)]
        kin = [A.alloc(f"kin{i}", [128, 2, 512], F32) for i in range(2)]
        rin = [A.alloc(f"rin{i}", [64, 2, 512], F32) for i in range(2)]
        cs_ = [A.alloc(f"cs{i}", [64, 2, 512], F32) for i in range(2)]
        sq = A.alloc("sq", [128, 4, 512], BF16)
        rs = A.alloc("rs", [128, 2, 512], F32)
        tr = A.alloc("tr", [64, 2, 512], F32)
        cqv = self.cqT.t.ap().rearrange("(j p) t -> p j t", p=128)
        ckv = self.ckvT.t.ap().rearrange("(j p) t -> p j t", p=128)
        krv = self.krT.t.ap().rearrange("(j p) t -> p j t", p=64)

        def load3a(q):
            qs = slice(q * 512, (q + 1) * 512)
            P.dma(cin[q % 2][:], V(None, cqv[:, :, qs]))
            P.dma(kin[q % 2][:], V(None, ckv[:, :, qs]))
            P.dma(rin[q % 2][:], V(None, krv[:, :, qs]))
            P.dma(cs_[q % 2][:, 0, :], self.cosT[:, qs])
            P.dma(cs_[q % 2][:, 1, :], self.sinT[:, qs])
        load3a(0)
        for q in range(NQ):
            if q + 1 < NQ:
                load3a(q + 1)
            qs = slice(q * 512, (q + 1) * 512)
            ci, ki, ri, cs2 = cin[q % 2], kin[q % 2], rin[q % 2], cs_[q % 2]
            P.act(sq[:], ci[:], AF.Square)
            for j in range(4):
                P.matmul(bk[0][:], C['onesb'][:], sq[:, j, :], start=(j == 0), stop=(j == 3))
            P.act(sq[:, 0:2, :], ki[:], AF.Square)
            for j in range(2):
                P.matmul(bk[1][:], C['onesb'][:], sq[:, j, :], start=(j == 0), stop=(j == 1))
            P.act(rs[:, 0, :], bk[0][:], AF.Sqrt, bias=C['eps'][:, 0:1], scale=1.0 / 512)
            P.act(rs[:, 1, :], bk[1][:], AF.Sqrt, bias=C['eps'][:, 0:1], scale=1.0 / 256)
            P.recip(rs[:], rs[:])
            for j in range(4):
                P.stt(cqn[:, j, qs], ci[:, j, :], cp[:, 124 + j:125 + j], rs[:, 0, :], ALU.mult, ALU.mult)
            for j in range(2):
                P.stt(ckvn[:, j, qs], ki[:, j, :], cp[:, 128 + j:129 + j], rs[:, 1, :], ALU.mult, ALU.mult)
            P.tt(tr[:], ri[:], cs2[:], ALU.mult)
            P.tt(kpe[:, qs], tr[:, 0, :], tr[:, 1, :], ALU.add)
        P.barrier()
        A.off = mark
        KT = [A.alloc(f"KT{i}", [128, S], BF16) for i in range(2)]
        Vh = [A.alloc(f"Vh{i}", [128, NT, 128], BF16) for i in range(2)]
        QT = [A.alloc(f"QT{i}", [128, S], BF16) for i in range(2)]
        qpe = [A.alloc(f"qpe{i}", [64, S], BF16) for i in range(2)]
        csq = [A.alloc(f"csq{i}", [64, 2, 512], F32) for i in range(2)]
        tq = A.alloc("tq", [64, 2, 512], F32)
        pT = [A.alloc(f"pT{i}", [128, 512], BF16) for i in range(3)]
        rden = A.alloc("rden", [128, 512], F32)
        yst = [A.alloc(f"yst{i}", [128, 512], BF16) for i in range(2)]
        scale = 192.0 ** -0.5
        cnt = {'p': 0, 'y': 0, 'cs': 0}
        for h in range(8):
            hb = h % 2
            for q in range(NQ):
                qs = slice(q * 512, (q + 1) * 512)
                ps = bk[6 + q % 2]
                for j in range(2):
                    P.matmul(ps[:], wukv[:, j, h * 256:h * 256 + 128], ckvn[:, j, qs], start=(j == 0), stop=(j == 1))
                P.acopy(KT[hb][:, qs], ps[:])
                ps = bk[6 + (q + 1) % 2]
                for j in range(4):
                    P.matmul(ps[:], wuq[:, j, h * 192:h * 192 + 128], cqn[:, j, qs], start=(j == 0), stop=(j == 3))
                P.acopy(QT[hb][:, qs], ps[:])
                cq2 = csq[cnt['cs'] % 2]
                cnt['cs'] += 1
                P.dma(cq2[:, 0, :], self.cosT[:, qs])
                P.dma(cq2[:, 1, :], self.sinT[:, qs])
                ps = bk[6 + q % 2]
                for j in range(4):
                    P.matmul(ps[0:64, 0:512], wuq[:, j, h * 192 + 128:h * 192 + 192], cqn[:, j, qs],
                             start=(j == 0), stop=(j == 3))
                P.tt(tq[:, 0, :], ps[0:64, :], cq2[:, 0, :], ALU.mult)
                ps = bk[6 + (q + 1) % 2]
                for j in range(4):
                    P.matmul(ps[0:64, 0:512], wrot[:, j, h * 64:(h + 1) * 64], cqn[:, j, qs],
                             start=(j == 0), stop=(j == 3))
                P.tt(tq[:, 1, :], ps[0:64, :], cq2[:, 1, :], ALU.mult)
                P.tt(qpe[hb][:, qs], tq[:, 0, :], tq[:, 1, :], ALU.add)
            for t4 in range(0, NT, 4):
                ps = bk[6 + (t4 // 4) % 2]
                for tt_ in range(4):
                    t = t4 + tt_
                    for j in range(2):
                        P.matmul(ps[:, tt_ * 128:(tt_ + 1) * 128], ckvn[:, j, t * 128:(t + 1) * 128],
                                 wukv[:, j, h * 256 + 128:h * 256 + 256], start=(j == 0), stop=(j == 1))
                P.acopy(Vh[hb][:, t4:t4 + 4, :], ps[:].re("p (a v) -> p a v", a=4))
            for c in range(NQ):
                qs = slice(c * 512, (c + 1) * 512)
                ob = bk[2 + c % 2]
                db = bk[4 + c % 2]
                nkb = 4 * c + 4
                for kb in range(nkb):
                    j = kb - 4 * c
                    lo = 0 if j <= 0 else 128 * j
                    ks = slice(kb * 128, (kb + 1) * 128)
                    sb_ = bk[kb % 2]
                    P.matmul(sb_[:, lo:512], KT[hb][:, ks], QT[hb][:, c * 512 + lo:(c + 1) * 512], start=True, stop=False)
                    P.matmul(sb_[:, lo:512], kpe[:, ks], qpe[hb][:, c * 512 + lo:(c + 1) * 512], start=False, stop=True)
                    pt = pT[cnt['p'] % 3]
                    cnt['p'] += 1
                    P.act(pt[:, lo:512], sb_[:, lo:512], AF.Exp, scale=scale)
                    if j >= 0:
                        P.tt(pt[:, lo:lo + 128], pt[:, lo:lo + 128], C['maskLEb'][:], ALU.mult, eng='pool')
                    P.matmul(ob[:, lo:512], Vh[hb][:, kb, :], pt[:, lo:512], start=(kb == 0), stop=(kb == nkb - 1))
                    P.matmul(db[:, lo:512], C['onesb'][:], pt[:, lo:512], start=(kb == 0), stop=(kb == nkb - 1))
                P.recip(rden[:], db[:])
                ys = yst[cnt['y'] % 2]
                cnt['y'] += 1
                P.tt(ys[:], ob[:], rden[:], ALU.mult)
                P.dma(self.yT[1, h * 128:(h + 1) * 128, qs], ys[:], q='act')

    def phase_gdn(self, l):
        P, A, S, C = self.P, self.A, self.S, self.C
        NCH = self.NCH
        A.reset()
        bk = self.banks
        cp = A.alloc("cp", [128, NCOL], F32)
        P.dma(cp[:], self.colpack[l])
        gb = A.alloc("gb", [64, NCH, 16], F32)
        P.dma(gb[:], V(None, self.gbk.t.ap().rearrange("(n p) c -> p n c", p=64)))
        beta_all = A.alloc("beta_all", [64, NCH, 8], F32)
        g_all = A.alloc("g_all", [64, NCH, 8], F32)
        tg = A.alloc("tg", [64, NCH, 8], F32)
        dtb = A.alloc("gdtb", [64, 8], F32)
        alog = A.alloc("galog", [64, 8], F32)
        aB = A.alloc("gaB", [64, 8], F32)
        ngB = A.alloc("gngB", [64, 128], F32)
        P.dma(dtb[:], self.W['gdn_dt_bias'][l].pbc(64))
        P.dma(alog[:], self.W['gdn_a_log'][l].pbc(64))
        P.dma(ngB[:], self.W['gdn_norm_g'][l].pbc(64))
        P.act(aB[:], alog[:], AF.Exp)
        P.ts(aB[:], aB[:], -1.0, None, ALU.mult)
        P.act(beta_all[:], gb[:, :, 0:8], AF.Tanh, scale=0.5)
        P.ts(beta_all[:], beta_all[:], 0.5, 0.5, ALU.mult, ALU.add)
        P.tt(g_all[:], gb[:, :, 8:16], dtb[:].unsq(1).bc([64, NCH, 8]), ALU.add)
        self.softplus(g_all[:], g_all[:], tg[:], None)
        P.tt(g_all[:], g_all[:], aB[:].unsq(1).bc([64, NCH, 8]), ALU.mult)
        g16 = A.alloc("g16", [64, NCH, 16], F32)
        P.copy(g16[:, :, 0:8], g_all[:])
        P.copy(g16[:, :, 8:16], g_all[:])
        mark0 = A.off
        mLE, mGT, nmLT, ones, identf = C['maskLE'], C['maskGT'], C['nmaskLT'], C['ones'], C['identf']
        CW = min(S, 1024)
        dbg = self.debug
        if 'g0' in dbg:
            return
        NQ = self.NQ

        def alloc_work():
            w = {}
            w['Sf'] = A.alloc("Sf", [128, 2, 128], F32)
            w['Sb'] = A.alloc("Sb", [128, 2, 128], BF16)
            w['gm'] = A.alloc("gm", [64, 2, 64], F32)
            w['etot'] = A.alloc("etot", [128, 16], F32)
            w['erem'] = A.alloc("erem", [64, 16], F32)
            w['egR'] = A.alloc("egR", [128, 2, 64], BF16)
            w['decT'] = A.alloc("decT", [64, 2, 64], F32)
            w['dm'] = A.alloc("dm", [64, 2, 64], F32)
            w['dm2'] = A.alloc("dm2", [64, 2, 64], F32)
            w['PT0f'] = A.alloc("PT0f", [64, 2, 64], F32)
            w['PT'] = [A.alloc(f"PT{k}", [64, 2, 64], BF16) for k in range(2)]
            w['Pm'] = [A.alloc(f"Pm{k}", [64, 2, 64], BF16) for k in range(2)]
            w['TT'] = A.alloc("TT", [64, 2, 64], F32)
            w['TTb'] = A.alloc("TTb", [64, 2, 64], BF16)
            w['QKd'] = A.alloc("QKd", [64, 2, 64], BF16)
            w['kgT'] = A.alloc("kgT", [128, 2, 64], BF16)
            w['qgT'] = A.alloc("qgT", [128, 2, 64], BF16)
            w['kd'] = A.alloc("kd", [64, 2, 128], BF16)
            w['vtok'] = A.alloc("vtok", [64, 2, 128], F32)
            w['Xs'] = A.alloc("Xs", [64, 2, 128], BF16)
            w['vnew'] = A.alloc("vnew", [64, 2, 128], BF16)
            w['zt'] = [A.alloc("gzt0", [64, 8, 256], F32)]
            w['oraw'] = A.alloc("oraw", [64, 8, 2, 128], F32)
            w['ssq16'] = A.alloc("gssq16", [64, 16], F32)
            w['rstd16'] = A.alloc("grstd16", [64, 16], F32)
            w['otok8'] = A.alloc("otok8", [64, 8, 2, 128], BF16)
            w['jk'] = A.alloc("gjk", [64, 128], BF16)
            w['ssq'] = A.alloc("gssq", [64, 2], F32)
            w['rstd'] = A.alloc("grstd", [64, 2], F32)
            w['on'] = A.alloc("on", [64, 2, 128], F32)
            w['otok'] = A.alloc("otok", [64, 2, 128], BF16)
            return w

        def chunks_gen(i, pers, w, rot):
            qn, kn, vT, ostg = pers
            bkr = [bk[(j + rot) % 8] for j in range(8)]
            Sf, Sb, gm, etot, erem, egR, decT, dm, dm2 = (w[k_] for k_ in ('Sf', 'Sb', 'gm', 'etot', 'erem', 'egR', 'decT', 'dm', 'dm2'))
            PT0f, PT, Pm, TT, TTb, QKd, kgT, qgT, kd = (w[k_] for k_ in ('PT0f', 'PT', 'Pm', 'TT', 'TTb', 'QKd', 'kgT', 'qgT', 'kd'))
            vtok, Xs, vnew, zt, jk, ssq, rstd, on, otok = (w[k_] for k_ in ('vtok', 'Xs', 'vnew', 'zt', 'jk', 'ssq', 'rstd', 'on', 'otok'))
            oraw, ssq16, rstd16, otok8 = w['oraw'], w['ssq16'], w['rstd16'], w['otok8']
            zview = self.zg2.t.ap().rearrange("(g p) c -> p g c", p=64)
            P.memset(Sf[:], 0.0)
            P.memset(Sb[:], 0.0, eng='pool')

            def loadz(gq_):
                P.dma(zt[0][:], V(None, zview[:, gq_ * 8:(gq_ + 1) * 8, (2 * i) * 128:(2 * i + 2) * 128]))
            for n in range(NCH):
                if n % 8 == 0:
                    loadz(n // 8)
                cs = slice(n * 64, (n + 1) * 64)
                g2 = g_all[:, n, 2 * i:2 * i + 2]
                b2 = beta_all[:, n, 2 * i:2 * i + 2]
                Ba = bkr[0]
                P.matmul(Ba[0:64, 0:128], kn[:, cs], C['ident'][:])
                for e in range(2):
                    P.matmul(Ba[0:64, 128 + e * 128:256 + e * 128], vT[:, e, cs], C['ident'][:])
                P.tt(gm[:], mLE[0:64, 0:64].unsq(1).bc([64, 2, 64]), g2.unsq(2).bc([64, 2, 64]), ALU.mult, eng='pool')
                yield
                gmf = gm[:].re("p e l -> p (e l)")
                Bb = bkr[1]
                gq = g16[:, n, :]
                P.matmul(Bb[:, 0:16], ones[0:64, :], gq)
                P.matmul(Bb[0:64, 16:32], mGT[0:64, 0:64], gq)
                P.matmul(Bb[:, 128:256], ones[0:64, :], gmf)
                P.matmul(Bb[0:64, 256:384], mGT[0:64, 0:64], gmf)
                Bc = bkr[2]
                P.matmul(Bc[0:64, 0:64], kn[:, cs], kn[:, cs])
                P.matmul(Bc[0:64, 64:128], kn[:, cs], qn[:, cs])
                yield
                P.act(etot[:], Bb[:, 0:16], AF.Exp)
                P.act(erem[:], Bb[0:64, 16:32], AF.Exp)
                P.act(egR[:].re("p e l -> p (e l)"), Bb[:, 128:256], AF.Exp)
                P.act(decT[:].re("p e l -> p (e l)"), Bb[0:64, 256:384], AF.Exp)
                yield
                P.tt(dm[:], decT[:], nmLT[0:64, 0:64].unsq(1).bc([64, 2, 64]), ALU.mult, eng='pool')
                P.tt(dm2[:], decT[:], mLE[0:64, 0:64].unsq(1).bc([64, 2, 64]), ALU.mult, eng='pool')
                yield
                P.tt(dm[:], dm[:], Bc[0:64, 0:64].unsq(1).bc([64, 2, 64]), ALU.mult)
                for e in range(2):
                    P.ts(PT0f[:, e, :], dm[:, e, :], b2[:, e:e + 1], None, ALU.mult)
                P.tt(QKd[:], dm2[:], Bc[0:64, 64:128].unsq(1).bc([64, 2, 64]), ALU.mult)
                P.tt(TT[:], PT0f[:], identf[0:64, 0:64].unsq(1).bc([64, 2, 64]), ALU.add)
                yield
                P.copy(PT[0][:], PT0f[:], eng='pool')
                P.copy(TTb[:], TT[:], eng='pool')
                yield
                Bd = bkr[3]
                for e in range(2):
                    P.matmul(Bd[0:64, e * 64:(e + 1) * 64], PT[0][:, e, :], C['ident'][0:64, 0:64])
                yield
                P.copy(Pm[0][:].re("p e l -> p (e l)"), Bd[0:64, 0:128], eng='act')
                yield
                cur = 0
                for it in range(5):
                    nxt = 1 - cur
                    for e in range(2):
                        P.matmul(Bd[0:64, e * 64:(e + 1) * 64], PT[cur][:, e, :], Pm[cur][:, e, :])
                        if it < 4:
                            P.matmul(Bd[0:64, 128 + e * 64:128 + (e + 1) * 64], Pm[cur][:, e, :], PT[cur][:, e, :])
                    yield
                    P.copy(Pm[nxt][:].re("p e l -> p (e l)"), Bd[0:64, 0:128], eng='act')
                    if it < 4:
                        P.copy(PT[nxt][:].re("p e l -> p (e l)"), Bd[0:64, 128:256], eng='act')
                    yield
                    for e in range(2):
                        P.matmul(Bd[0:64, 256 + e * 64:256 + (e + 1) * 64], Pm[nxt][:, e, :], TTb[:, e, :])
                    yield
                    P.tt(TT[:].re("p e l -> p (e l)"), TT[:].re("p e l -> p (e l)"), Bd[0:64, 256:384], ALU.add)
                    yield
                    P.copy(TTb[:], TT[:], eng='pool')
                    yield
                    cur = nxt
                for e in range(2):
                    P.tt(kgT[:, e, :], kn[:, cs], egR[:, e, :], ALU.mult, eng='pool')
                    P.tt(qgT[:, e, :], qn[:, cs], egR[:, e, :], ALU.mult, eng='pool')
                    P.ts(kd[:, e, :], Ba[0:64, 0:128], erem[:, 2 * i + e:2 * i + e + 1], None, ALU.mult)
                P.copy(vtok[:].re("p e v -> p (e v)"), Ba[0:64, 128:384])
                yield
                Be, Bg = bkr[4], bkr[6]
                Bf = (bkr[5], bkr[7])
                for e in range(2):
                    P.matmul(Be[0:64, e * 128:(e + 1) * 128], kgT[:, e, :], Sb[:, e, :])
                    P.matmul(Bf[e][0:64, 0:128], qgT[:, e, :], Sb[:, e, :], start=True, stop=False)
                yield
                P.tt(Xs[:].re("p e v -> p (e v)"), vtok[:].re("p e v -> p (e v)"), Be[0:64, 0:256], ALU.subtract)
                yield
                for e in range(2):
                    P.matmul(Be[0:64, 256 + e * 128:256 + (e + 1) * 128], TTb[:, e, :], Xs[:, e, :])
                yield
                for e in range(2):
                    P.ts(vnew[:, e, :], Be[0:64, 256 + e * 128:256 + (e + 1) * 128], b2[:, e:e + 1], None, ALU.mult)
                yield
                for e in range(2):
                    P.matmul(Bf[e][0:64, 0:128], QKd[:, e, :], vnew[:, e, :], start=False, stop=True)
                    P.matmul(Bg[:, e * 128:(e + 1) * 128], kd[:, e, :], vnew[:, e, :])
                yield
                for e in range(2):
                    P.ts(Sf[:, e, :], Sf[:, e, :], etot[:, 2 * i + e:2 * i + e + 1], None, ALU.mult)
                P.tt(Sf[:].re("p e v -> p (e v)"), Sf[:].re("p e v -> p (e v)"), Bg[:, 0:256], ALU.add)
                yield
                P.copy(Sb[:], Sf[:], eng='pool')
                gi_ = n % 8
                for e in range(2):
                    P.copy(oraw[:, gi_, e, :], Bf[e][0:64, 0:128], eng='act')
                yield
                if gi_ == 7:
                    n0 = n - 7
                    for g_ in range(8):
                        for e in range(2):
                            P.act(jk[:], oraw[:, g_, e, :], AF.Square, accum_out=ssq16[:, g_ * 2 + e:g_ * 2 + e + 1])
                    yield
                    self.rmsnorm_rstd(rstd16[:], ssq16[:], 128)
                    yield
                    P.tt(oraw[:].re("p g e v -> p (g e) v"), oraw[:].re("p g e v -> p (g e) v"),
                         rstd16[:].unsq(2).bc([64, 16, 128]), ALU.mult)
                    P.tt(oraw[:].re("p g e v -> p (g e) v"), oraw[:].re("p g e v -> p (g e) v"),
                         ngB[:].unsq(1).bc([64, 16, 128]), ALU.mult)
                    yield
                    P.stt(otok8[:].re("p g e v -> p g (e v)"), oraw[:].re("p g e v -> p g (e v)"), 0.5,
                          zt[0][:], ALU.mult, ALU.mult)
                    yield
                    for e in range(2):
                        Bh = bkr[6] if e == 0 else bkr[4]
                        for g_ in range(8):
                            P.matmul(Bh[:, g_ * 64:(g_ + 1) * 64], otok8[:, g_, e, :], C['ident'][0:64, 0:64])
                        yield
                        P.copy(ostg[:, e, n0 * 64:(n0 + 8) * 64], Bh[:, 0:512], eng='act')
                        yield

        for grp in ((0, 1), (2, 3)):
            A.off = mark0
            pers = {}
            for i in grp:
                qn = A.alloc(f"qn{i}", [128, S], BF16)
                kn = A.alloc(f"kn{i}", [128, S], BF16)
                vT = A.alloc(f"vT{i}", [128, 2, S], BF16)
                ostg = A.alloc(f"ostg{i}", [128, 2, S], BF16)
                pers[i] = (qn, kn, vT, ostg)
            mark1 = A.off
            raw = [A.alloc(f"raw{k}", [128, CW + 3], F32) for k in range(2)]
            acc = [A.alloc(f"acc{k}", [128, CW], F32) for k in range(2)]
            th = [A.alloc(f"th{k}", [128, CW], F32) for k in range(2)]
            qc = A.alloc("qc", [128, S], BF16)
            kc = A.alloc("kc", [128, S], BF16)
            sq = A.alloc("sq", [128, 512], BF16)
            rsn = A.alloc("rsn", [128, 512], F32)
            for i in grp:
                qn, kn, vT, ostg = pers[i]
                blocks = [(i, (lambda t0, cw: qc[:, t0:t0 + cw])), (4 + i, (lambda t0, cw: kc[:, t0:t0 + cw])),
                          (8 + 2 * i, (lambda vv: (lambda t0, cw: vv[:, 0, t0:t0 + cw]))(vT)),
                          (9 + 2 * i, (lambda vv: (lambda t0, cw: vv[:, 1, t0:t0 + cw]))(vT))]
                for (blk, dfn) in blocks:
                    self.conv_block(self.qkvT, blk * 128, cp[:, 60 + blk * 4:64 + blk * 4], None, dfn, (raw, acc, th), S)
                for (src, dst, sc_) in ((qc, qn, 128.0 ** -0.5), (kc, kn, 1.0)):
                    for q in range(NQ):
                        qs = slice(q * 512, (q + 1) * 512)
                        P.act(sq[:], src[:, qs], AF.Square)
                        P.matmul(bk[q % 2][:], C['onesb'][:], sq[:])
                        P.act(rsn[:], bk[q % 2][:], AF.Sqrt, bias=C['eps'][:, 0:1], scale=1.0)
                        P.recip(rsn[:], rsn[:])
                        P.stt(dst[:, qs], src[:, qs], sc_, rsn[:], ALU.mult, ALU.mult)
            P.barrier()
            A.off = mark1
            gens = []
            for gi, i in enumerate(grp):
                w = alloc_work()
                gens.append(chunks_gen(i, pers[i], w, 4 * gi))
            live = list(gens)
            while live:
                for g_ in list(live):
                    try:
                        next(g_)
                    except StopIteration:
                        live.remove(g_)
            for i in grp:
                ostg = pers[i][3]
                for e in range(2):
                    P.dma(self.yT[2, (2 * i + e) * 128:(2 * i + e + 1) * 128, :], ostg[:, e, :], q='act')
            P.barrier()

    def phase_merge(self, l, xsrc):
        P, A, S, C = self.P, self.A, self.S, self.C
        NQ = self.NQ
        A.reset()
        bk = self.banks
        ws = []
        for n in ("w_ssd_out", "w_mla_out", "w_gdn_out", "w_out"):
            w = A.alloc(n, [128, 8, D], BF16)
            self.load_w(w[:], V(None, self.W[n].t[l].rearrange("(j p) c -> p j c", p=128)))
            ws.append(w)
        gB = A.alloc("g2B", [128, D], F32)
        P.dma(gB[:], self.W['norm2_g'][l].pbc(128))
        yin = [[A.alloc(f"yin{b}_{i}", [128, 8, 512], BF16) for b in range(3)] for i in range(2)]
        gt = [A.alloc(f"gt{i}", [128, 3, 512], F32) for i in range(2)]
        mixT = A.alloc("mixT", [128, 8, 512], BF16)
        tm = A.alloc("tm", [128, 512], F32)
        tm2 = A.alloc("tm2", [128, 512], F32)
        xt = [A.alloc(f"xt{i}", [128, D], F32) for i in range(2)]
        ht = [A.alloc(f"ht{i}", [128, D], F32) for i in range(2)]
        hn = A.alloc("hn", [128, D], BF16)
        junk = A.alloc("junk", [128, D], BF16)
        ssq = A.alloc("ssq", [128, 1], F32)
        rstd = A.alloc("rstd", [128, 1], F32)
        hstg = [A.alloc(f"hstg{i}", [128, 8, 128], BF16) for i in range(2)]
        gv = self.gatesT.t.ap().rearrange("(b r) t -> r b t", b=3)
        hnTd = self.hnT.t.ap().rearrange("(j p) t -> p j t", p=128)
        cnt = {'g': 0, 'x': 0}

        def loady(q):
            qs = slice(q * 512, (q + 1) * 512)
            for b in range(3):
                P.dma(yin[q % 2][b][:], V(None, self.yT.t[b].rearrange("(j p) t -> p j t", p=128)[:, :, qs]))
        loady(0)
        for q in range(NQ):
            if q + 1 < NQ:
                loady(q + 1)
            qs = slice(q * 512, (q + 1) * 512)
            for cb in range(8):
                g = gt[cnt['g'] % 2]
                cnt['g'] += 1
                P.dma(g[:], V(None, gv[cb * 128:(cb + 1) * 128, :, qs]))
                for b in range(3):
                    ps = bk[b]
                    for k in range(8):
                        P.matmul(ps[:], ws[b][:, k, cb * 128:(cb + 1) * 128], yin[q % 2][b][:, k, :],
                                 start=(k == 0), stop=(k == 7))
                P.tt(tm[:], bk[0][:], g[:, 0, :], ALU.mult)
                P.tt(tm2[:], bk[1][:], g[:, 1, :], ALU.mult)
                P.tt(tm[:], tm[:], tm2[:], ALU.add, eng='pool')
                P.tt(tm2[:], bk[2][:], g[:, 2, :], ALU.mult)
                P.tt(mixT[:, cb, :], tm[:], tm2[:], ALU.add, eng='pool')
            for t4 in range(4):
                t = q * 4 + t4
                x_ = xt[cnt['x'] % 2]
                h_ = ht[cnt['x'] % 2]
                hs_ = hstg[cnt['x'] % 2]
                cnt['x'] += 1
                P.dma(x_[:], xsrc[t * 128:(t + 1) * 128, :])
                for half in range(2):
                    ps = bk[4 + half]
                    for k in range(8):
                        P.matmul(ps[:], mixT[:, k, t4 * 128:(t4 + 1) * 128], ws[3][:, k, half * 512:(half + 1) * 512],
                                 start=(k == 0), stop=(k == 7))
                    P.tt(h_[:, half * 512:(half + 1) * 512], ps[:], x_[:, half * 512:(half + 1) * 512], ALU.add)
                P.dma(self.hres[t * 128:(t + 1) * 128, :], h_[:], q='act')
                P.act(junk[:], h_[:], AF.Square, accum_out=ssq[:])
                self.rmsnorm_rstd(rstd[:], ssq[:], D)
                P.stt(hn[:], h_[:], rstd[:, 0:1], gB[:], ALU.mult, ALU.mult)
                pb = bk[6 + t4 % 2][:].bitcast(BF16)
                for j in range(8):
                    P.transpose(pb[:, j * 128:(j + 1) * 128], hn[:, j * 128:(j + 1) * 128], C['ident'][:])
                P.copy(hs_[:], pb.re("p (j t) -> p j t", j=8), eng='pool' if False else 'dve')
                P.dma(V(None, hnTd[:, :, t * 128:(t + 1) * 128]), hs_[:], q='act')

    def phase_ffn_up(self, l):
        P, A, S, C = self.P, self.A, self.S, self.C
        NQ = self.NQ
        A.reset()
        bk = self.banks
        wup = A.alloc("wup", [128, 8, 4 * D], BF16)
        wv = self.W['w_up'].t[l].rearrange("(j p) c -> p j c", p=128)
        for s in range(8):
            self.load_w(wup[:, :, s * 512:(s + 1) * 512], V(None, wv[:, :, s * 512:(s + 1) * 512]))
        hin = [A.alloc(f"hin{i}", [128, 8, 512], BF16) for i in range(2)]
        rl = [A.alloc(f"rl{i}", [128, 512], BF16) for i in range(2)]
        ust = [A.alloc(f"ust{i}", [128, 4, 512], BF16) for i in range(2)]
        hnTd = self.hnT.t.ap().rearrange("(j p) t -> p j t", p=128)
        uTd = self.uT.t.ap().rearrange("(f p) t -> p f t", p=128)
        cnt = 0

        def loadh(q):
            P.dma(hin[q % 2][:], V(None, hnTd[:, :, q * 512:(q + 1) * 512]))
        loadh(0)
        for q in range(NQ):
            if q + 1 < NQ:
                loadh(q + 1)
            qs = slice(q * 512, (q + 1) * 512)
            for f4 in range(8):
                us = ust[f4 % 2]
                for ff in range(4):
                    fb = f4 * 4 + ff
                    ps = bk[cnt % 4]
                    r = rl[cnt % 2]
                    cnt += 1
                    for k in range(8):
                        P.matmul(ps[:], wup[:, k, fb * 128:(fb + 1) * 128], hin[q % 2][:, k, :],
                                 start=(k == 0), stop=(k == 7))
                    P.act(r[:], ps[:], AF.Relu)
                    P.tt(us[:, ff, :], r[:], r[:], ALU.mult, eng='dve' if cnt % 2 else 'pool')
                P.dma(V(None, uTd[:, f4 * 4:(f4 + 1) * 4, qs]), us[:], q='act')

    def phase_ffn_down(self, l, last):
        P, A, S, C = self.P, self.A, self.S, self.C
        NQ = self.NQ
        A.reset()
        bk = self.banks
        wd = A.alloc("wd", [128, 32, D], BF16)
        wv = self.W['w_down'].t[l].rearrange("(f p) c -> p f c", p=128)
        for s in range(4):
            self.load_w(wd[:, s * 8:(s + 1) * 8, :], V(None, wv[:, s * 8:(s + 1) * 8, :]))
        uin = [A.alloc(f"uin{i}", [128, 32, 512], BF16) for i in range(2)]
        ht = [A.alloc(f"ht{i}", [128, D], F32) for i in range(2)]
        ot = [A.alloc(f"ot{i}", [128, D], F32) for i in range(2)]
        gF = A.alloc("gF", [128, D], F32)
        junk = A.alloc("junk", [128, D], BF16)
        ssq = A.alloc("ssq", [128, 1], F32)
        rstd = A.alloc("rstd", [128, 1], F32)
        if last:
            P.dma(gF[:], self.W['final_norm_g'][:].pbc(128))
        uTd = self.uT.t.ap().rearrange("(f p) t -> p f t", p=128)
        cnt = 0

        def loadu(q):
            for s in range(4):
                P.dma(uin[q % 2][:, s * 8:(s + 1) * 8, :], V(None, uTd[:, s * 8:(s + 1) * 8, q * 512:(q + 1) * 512]))
        loadu(0)
        for q in range(NQ):
            if q + 1 < NQ:
                loadu(q + 1)
            for t4 in range(4):
                t = q * 4 + t4
                h_ = ht[cnt % 2]
                o_ = ot[cnt % 2]
                cnt += 1
                P.dma(h_[:], self.hres[t * 128:(t + 1) * 128, :])
                for half in range(2):
                    ps = bk[(cnt % 2) * 2 + half]
                    for f in range(32):
                        P.matmul(ps[:], uin[q % 2][:, f, t4 * 128:(t4 + 1) * 128], wd[:, f, half * 512:(half + 1) * 512],
                                 start=(f == 0), stop=(f == 31))
                    P.tt(o_[:, half * 512:(half + 1) * 512], ps[:], h_[:, half * 512:(half + 1) * 512], ALU.add)
                if last:
                    P.act(junk[:], o_[:], AF.Square, accum_out=ssq[:])
                    self.rmsnorm_rstd(rstd[:], ssq[:], D)
                    P.stt(o_[:], o_[:], rstd[:, 0:1], gF[:], ALU.mult, ALU.mult)
                    P.dma(self.out[t * 128:(t + 1) * 128, :], o_[:], q='act')
                else:
                    P.dma(self.xres[t * 128:(t + 1) * 128, :], o_[:], q='act')


def segs_base(c0, segs):
    return c0


def make_colpack(inputs):
    L = inputs['ssd_conv_w'].shape[0]
    cp = np.zeros((L, 128, NCOL), np.float32)
    for l in range(L):
        w = np.asarray(inputs['ssd_conv_w'][l])
        cp[l, :, 0:48] = w.reshape(4, 12, 128).transpose(2, 1, 0).reshape(128, 48)
        cp[l, :, 48:60] = np.asarray(inputs['ssd_conv_b'][l]).reshape(12, 128).T
        w = np.asarray(inputs['gdn_conv_w'][l])
        cp[l, :, 60:124] = w.reshape(4, 16, 128).transpose(2, 1, 0).reshape(128, 64)
        cp[l, :, 124:128] = np.asarray(inputs['mla_q_norm_g'][l]).reshape(4, 128).T
        cp[l, :, 128:130] = np.asarray(inputs['mla_kv_norm_g'][l]).reshape(2, 128).T
    return cp


_CACHE = {}


def get_nc(S, depth=DEPTH, debug=()):
    key = (S, depth, tuple(sorted(debug)))
    if key not in _CACHE:
        k = K(S, depth, debug)
        k.build()
        _CACHE[key] = k
    return _CACHE[key]


def run(inputs, ncores, S, depth=DEPTH, debug=(), trace=False):
    k = get_nc(S, depth, debug)
    cp = make_colpack(inputs)
    invf = (10000.0 ** (-np.arange(0, 64, 2, dtype=np.float32) / 64)).astype(np.float32)
    invf = np.concatenate([invf, invf]).reshape(64, 1).astype(np.float32)
    shared = {n: np.ascontiguousarray(np.asarray(inputs[n], dtype=np.float32)) for n in k.W}
    shared['colpack'] = cp
    shared['invf'] = invf
    in_maps = []
    for c in range(ncores):
        m = dict(shared)
        m['x'] = np.ascontiguousarray(np.asarray(inputs['x'][c], dtype=np.float32))
        m['positions'] = np.ascontiguousarray(np.asarray(inputs['positions'][c], dtype=np.int32))
        in_maps.append(m)
    res = run_bass_kernel_spmd(k.nc, in_maps, core_ids=list(range(ncores)), trace=trace)
    return res


def kernel(**inputs):
    x = np.asarray(inputs['x'])
    B, S, _ = x.shape
    res = run(inputs, B, S)
    out = np.stack([np.asarray(r['out']) for r in res.results], axis=0).astype(np.float32)
    return out
```
